# Optimizing a Trainium2 kernel written in Bass

```python
import numpy as np
import jax
import jax.numpy as jnp
from jax import lax

D_MODEL = 1024
BATCH = 2
SEQ = 8192
DEPTH = 4

N_A_LAYERS = DEPTH // 2
N_B_LAYERS = DEPTH - N_A_LAYERS
ALPHA = (2.0 * DEPTH) ** 0.25
BETA = (8.0 * DEPTH) ** -0.25
LN_EPS = 1e-5
NEG_INF = -1e30

RET_HEADS = 4
RET_QK_DIM = D_MODEL // RET_HEADS
RET_V_DIM = 2 * D_MODEL // RET_HEADS
RET_CHUNK = 128
RET_IN = 2 * RET_HEADS * RET_QK_DIM + 2 * RET_HEADS * RET_V_DIM

NSA_HEADS = 16
NSA_GROUPS = 4
NSA_REP = NSA_HEADS // NSA_GROUPS
NSA_HEAD_DIM = D_MODEL // NSA_HEADS
N_BRANCH = 3
CMP_STRIDE = 16
CMP_LEN = 2 * CMP_STRIDE
CMP_HIDDEN = 256
SLC_BLOCK = 64
SLC_TOPK = 16
WINDOW = 512
Q_BLOCK = 128
FORCE_BONUS = 100.0
NSA_IN = NSA_HEADS * NSA_HEAD_DIM + NSA_HEADS * N_BRANCH
NSA_KV = N_BRANCH * 2 * NSA_GROUPS * NSA_HEAD_DIM

PEER_HEADS = 8
PEER_NKEYS = 128
PEER_EXPERTS = PEER_NKEYS * PEER_NKEYS
PEER_TOPK = 16
PEER_QDIM = 256
PEER_TOKENS = 128

kernel_name = "yoco_retention_nsa_peer_deepnorm_adaln"


def layer_norm(x, g, b):
    xf = x.astype(jnp.float32)
    mu = jnp.mean(xf, axis=-1, keepdims=True)
    var = jnp.mean(jnp.square(xf - mu), axis=-1, keepdims=True)
    return ((xf - mu) * lax.rsqrt(var + LN_EPS) * g + b).astype(x.dtype)


def masked_softmax(s, mask):
    s = jnp.where(mask, s, NEG_INF)
    m = jnp.max(s, axis=-1, keepdims=True)
    e = jnp.where(mask, jnp.exp(s - m), 0.0)
    return e / jnp.maximum(jnp.sum(e, axis=-1, keepdims=True), 1e-30)


def rotate(x, cos, sin):
    x1, x2 = jnp.split(x, 2, axis=-1)
    return jnp.concatenate([x1 * cos - x2 * sin, x1 * sin + x2 * cos], axis=-1)


def retention(h, w_in, w_o):
    B, S, _ = h.shape
    H, dk, dv, C = RET_HEADS, RET_QK_DIM, RET_V_DIM, RET_CHUNK
    f32 = jnp.float32
    proj = h @ w_in
    q, k, v, g = jnp.split(proj, [H * dk, 2 * H * dk, 2 * H * dk + H * dv], axis=-1)
    to_heads = lambda t, d: t.reshape(B, S, H, d).transpose(0, 2, 1, 3).astype(f32)
    q, k, v = to_heads(q, dk), to_heads(k, dk), to_heads(v, dv)
    pos = jnp.arange(S, dtype=f32)
    theta = 1.0 / (10000.0 ** jnp.linspace(0.0, 1.0, dk // 2, dtype=f32))
    ang = pos[:, None] * theta[None, :]
    cos, sin = jnp.cos(ang), jnp.sin(ang)
    q = rotate(q, cos, sin)
    k = rotate(k, cos, sin) * (dk ** -0.5)
    log_g = jnp.log1p(-jnp.exp2(-5.0 - jnp.arange(H, dtype=f32)))
    idx = jnp.arange(C, dtype=f32)
    diff = idx[:, None] - idx[None, :]
    decay = jnp.where(diff >= 0, jnp.exp(jnp.maximum(diff, 0.0)[None] * log_g[:, None, None]), 0.0)
    q_dec = jnp.exp((idx + 1.0)[None] * log_g[:, None])
    k_dec = jnp.exp((C - 1.0 - idx)[None] * log_g[:, None])
    c_dec = jnp.exp(C * log_g)
    n_chunks = S // C
    chunks = lambda t: t.reshape(B, H, n_chunks, C, t.shape[-1]).transpose(2, 0, 1, 3, 4)

    def step(state, qkv):
        qc, kc, vc = qkv
        inner = jnp.einsum('bhqk,bhke->bhqe', jnp.einsum('bhqd,bhkd->bhqk', qc, kc) * decay, vc)
        cross = jnp.einsum('bhqd,bhde->bhqe', qc, state) * q_dec[:, :, None]
        state = state * c_dec[:, None, None] + jnp.einsum('bhkd,bhke->bhde', kc * k_dec[:, :, None], vc)
        return state, inner + cross

    state0 = jnp.zeros((B, H, dk, dv), f32)
    _, o = lax.scan(step, state0, (chunks(q), chunks(k), chunks(v)))
    o = o.transpose(1, 0, 3, 2, 4).reshape(B, S, H, dv)
    mu = jnp.mean(o, axis=-1, keepdims=True)
    var = jnp.mean(jnp.square(o - mu), axis=-1, keepdims=True)
    o = ((o - mu) * lax.rsqrt(var + LN_EPS)).reshape(B, S, H * dv)
    return (jax.nn.silu(g.astype(f32)) * o).astype(h.dtype) @ w_o


def nsa_shared_kv(xs, w_kv, cmp_pe, cmp_w1, cmp_b1, cmp_w2):
    B, S, _ = xs.shape
    G, hd = NSA_GROUPS, NSA_HEAD_DIM
    kv = (xs @ w_kv).reshape(B, S, N_BRANCH, 2, G, hd).transpose(2, 3, 0, 4, 1, 5)
    pieces = kv[0].reshape(2, B, G, S // CMP_STRIDE, CMP_STRIDE, hd)
    blocks = jnp.concatenate([pieces[:, :, :, :-1], pieces[:, :, :, 1:]], axis=4)
    blocks = blocks + cmp_pe[:, None, None, None]
    n_cmp = blocks.shape[3]
    flat = blocks.reshape(2, B, G, n_cmp, CMP_LEN * hd)
    hid = jax.nn.gelu(jnp.einsum('cbgnf,cfh->cbgnh', flat, cmp_w1) + cmp_b1[:, None, None, None])
    comp = jnp.einsum('cbgnh,chd->cbgnd', hid, cmp_w2)
    slc = kv[1].reshape(2, B, G, S // SLC_BLOCK, SLC_BLOCK, hd)
    win = jnp.pad(kv[2], ((0, 0), (0, 0), (0, 0), (WINDOW, 0), (0, 0)))
    return comp[0], comp[1], slc[0], slc[1], win[0], win[1]


def cmp_to_slc_matrix(n_cmp, n_slc):
    i = np.arange(n_cmp)[:, None] * CMP_STRIDE
    j = np.arange(n_slc)[None, :] * SLC_BLOCK
    ov = np.minimum(i + CMP_LEN, j + SLC_BLOCK) - np.maximum(i, j)
    return jnp.asarray(np.clip(ov, 0, None) / CMP_LEN, dtype=jnp.float32)


def nsa(h, w_in, w_o, k_cmp, v_cmp, k_slc, v_slc, k_win, v_win):
    B, S, _ = h.shape
    G, R, hd, QB = NSA_GROUPS, NSA_REP, NSA_HEAD_DIM, Q_BLOCK
    f32 = jnp.float32
    n_cmp = k_cmp.shape[2]
    n_slc = k_slc.shape[2]
    n_sel = min(SLC_TOPK, n_slc)
    proj = h @ w_in
    q = proj[..., :NSA_HEADS * hd].reshape(B, S, G, R, hd) * (hd ** -0.5)
    gate = jax.nn.sigmoid(proj[..., NSA_HEADS * hd:].astype(f32)).reshape(B, S, G, R, N_BRANCH)
    nqb = S // QB
    q_blocks = q.reshape(B, nqb, QB, G, R, hd).transpose(1, 0, 3, 4, 2, 5)
    g_blocks = gate.reshape(B, nqb, QB, G, R, N_BRANCH).transpose(1, 0, 3, 4, 2, 5)
    starts = jnp.arange(nqb, dtype=jnp.int32) * QB
    c2s = cmp_to_slc_matrix(n_cmp, n_slc)
    cmp_end = jnp.arange(n_cmp) * CMP_STRIDE + CMP_LEN - 1
    slc_idx = jnp.arange(n_slc)
    slc_start = slc_idx * SLC_BLOCK
    bi = jnp.arange(B)[:, None, None, None]
    gi = jnp.arange(G)[None, :, None, None]

    def block(args):
        qb, gb, start = args
        t = start + jnp.arange(QB)
        s = jnp.einsum('bgrqd,bgnd->bgrqn', qb, k_cmp, preferred_element_type=f32)
        p_cmp = masked_softmax(s, cmp_end[None, :] <= t[:, None])
        o_cmp = jnp.einsum('bgrqn,bgnd->bgrqd', p_cmp.astype(v_cmp.dtype), v_cmp)
        imp = jnp.einsum('bgqn,nj->bgqj', p_cmp.sum(axis=2), c2s)
        cur = t // SLC_BLOCK
        forced = (slc_idx[None] == 0) | (slc_idx[None] == cur[:, None]) | (slc_idx[None] == cur[:, None] - 1)
        avail = slc_start[None] <= t[:, None]
        score = jnp.where(avail, imp + jnp.where(forced, FORCE_BONUS, 0.0), -1.0)
        top_s, top_i = lax.top_k(score, n_sel)
        ks = k_slc[bi, gi, top_i]
        vs = v_slc[bi, gi, top_i]
        tok = top_i[..., None] * SLC_BLOCK + jnp.arange(SLC_BLOCK)
        smask = (tok <= t[:, None, None]) & (top_s >= 0.0)[..., None]
        s = jnp.einsum('bgrqd,bgqnkd->bgrqnk', qb, ks, preferred_element_type=f32)
        p = masked_softmax(s.reshape(B, G, R, QB, n_sel * SLC_BLOCK),
                           smask.reshape(B, G, 1, QB, n_sel * SLC_BLOCK))
        o_slc = jnp.einsum('bgrqm,bgqmd->bgrqd', p.astype(v_slc.dtype),
                           vs.reshape(B, G, QB, n_sel * SLC_BLOCK, hd))
        kw = lax.dynamic_slice_in_dim(k_win, start, WINDOW + QB, axis=2)
        vw = lax.dynamic_slice_in_dim(v_win, start, WINDOW + QB, axis=2)
        kp = start - WINDOW + jnp.arange(WINDOW + QB)
        wmask = (kp[None] <= t[:, None]) & (kp[None] > t[:, None] - WINDOW) & (kp[None] >= 0)
        s = jnp.einsum('bgrqd,bgkd->bgrqk', qb, kw, preferred_element_type=f32)
        o_win = jnp.einsum('bgrqk,bgkd->bgrqd', masked_softmax(s, wmask).astype(vw.dtype), vw)
        out = gb[..., 0:1] * o_cmp + gb[..., 1:2] * o_slc + gb[..., 2:3] * o_win
        return out.astype(qb.dtype)

    o = lax.map(block, (q_blocks, g_blocks, starts))
    o = o.transpose(1, 0, 4, 2, 3, 5).reshape(B, S, NSA_HEADS * hd)
    return o @ w_o


def peer(h, w_q, sub_keys, u_tab, v_tab):
    B, S, D = h.shape
    f32 = jnp.float32
    xt = h.reshape(-1, PEER_TOKENS, D)

    def chunk_fn(xc):
        T = xc.shape[0]
        q = (xc @ w_q).reshape(T, PEER_HEADS, 2, PEER_QDIM // 2)
        s = jnp.einsum('thcd,hcnd->thcn', q, sub_keys, preferred_element_type=f32)
        sv, si = lax.top_k(s, PEER_TOPK)
        comb = (sv[:, :, 0, :, None] + sv[:, :, 1, None, :]).reshape(T, PEER_HEADS, PEER_TOPK * PEER_TOPK)
        cv, ci = lax.top_k(comb, PEER_TOPK)
        i1 = jnp.take_along_axis(si[:, :, 0], ci // PEER_TOPK, axis=-1)
        i2 = jnp.take_along_axis(si[:, :, 1], ci % PEER_TOPK, axis=-1)
        eidx = i1 * PEER_NKEYS + i2
        w = jax.nn.softmax(cv, axis=-1)
        u = u_tab[eidx]
        a = jax.nn.gelu(jnp.einsum('td,thkd->thk', xc, u, preferred_element_type=f32)) * w
        return jnp.einsum('thk,thkd->td', a.astype(v_tab.dtype), v_tab[eidx])

    return lax.map(chunk_fn, xt).reshape(B, S, D)


def setup_inputs(seed: int = 0) -> dict:
    key = jax.random.key(seed)
    ks = jax.random.split(key, 24)
    D = D_MODEL
    nrm = lambda k, shape, std: jax.random.normal(k, shape, jnp.float32) * std
    return {
        "x": nrm(ks[0], (BATCH, SEQ, D), 1.0),
        "c": nrm(ks[1], (BATCH, D), 1.0),
        "ada_w": nrm(ks[2], (DEPTH, D, 6 * D), 0.5 * D ** -0.5),
        "ada_b": nrm(ks[3], (DEPTH, 6 * D), 0.02),
        "ln_g": 1.0 + nrm(ks[4], (DEPTH, 2, D), 0.02),
        "ln_b": nrm(ks[5], (DEPTH, 2, D), 0.02),
        "ret_w_in": nrm(ks[6], (N_A_LAYERS, D, RET_IN), D ** -0.5),
        "ret_w_o": nrm(ks[7], (N_A_LAYERS, RET_HEADS * RET_V_DIM, D), BETA * (RET_HEADS * RET_V_DIM) ** -0.5),
        "kv_ada_w": nrm(ks[8], (D, 2 * D), 0.5 * D ** -0.5),
        "kv_ada_b": nrm(ks[9], (2 * D,), 0.02),
        "nsa_w_kv": nrm(ks[10], (D, NSA_KV), D ** -0.5),
        "cmp_pe": nrm(ks[11], (2, CMP_LEN, NSA_HEAD_DIM), 0.1),
        "cmp_w1": nrm(ks[12], (2, CMP_LEN * NSA_HEAD_DIM, CMP_HIDDEN), (CMP_LEN * NSA_HEAD_DIM) ** -0.5),
        "cmp_b1": nrm(ks[13], (2, CMP_HIDDEN), 0.02),
        "cmp_w2": nrm(ks[14], (2, CMP_HIDDEN, NSA_HEAD_DIM), CMP_HIDDEN ** -0.5),
        "nsa_w_in": nrm(ks[15], (N_B_LAYERS, D, NSA_IN), D ** -0.5),
        "nsa_w_o": nrm(ks[16], (N_B_LAYERS, NSA_HEADS * NSA_HEAD_DIM, D), BETA * (NSA_HEADS * NSA_HEAD_DIM) ** -0.5),
        "peer_w_q": nrm(ks[17], (DEPTH, D, PEER_HEADS * PEER_QDIM), D ** -0.5),
        "peer_keys": nrm(ks[18], (DEPTH, PEER_HEADS, 2, PEER_NKEYS, PEER_QDIM // 2), (PEER_QDIM // 2) ** -0.5),
        "peer_u": nrm(ks[19], (DEPTH, PEER_EXPERTS, D), D ** -0.5),
        "peer_v": nrm(ks[20], (DEPTH, PEER_EXPERTS, D), BETA),
    }


def reference(x, c, ada_w, ada_b, ln_g, ln_b, ret_w_in, ret_w_o, kv_ada_w, kv_ada_b, nsa_w_kv,
              cmp_pe, cmp_w1, cmp_b1, cmp_w2, nsa_w_in, nsa_w_o, peer_w_q, peer_keys, peer_u, peer_v):
    c_act = jax.nn.silu(c)
    shared = None
    for layer in range(DEPTH):
        mod = (c_act @ ada_w[layer] + ada_b[layer])[:, None, :]
        sh1, sc1, g1, sh2, sc2, g2 = jnp.split(mod, 6, axis=-1)
        h = x * (1.0 + sc1) + sh1
        if layer < N_A_LAYERS:
            y = retention(h, ret_w_in[layer], ret_w_o[layer])
        else:
            lb = layer - N_A_LAYERS
            y = nsa(h, nsa_w_in[lb], nsa_w_o[lb], *shared)
        x = layer_norm(ALPHA * x + g1 * y, ln_g[layer, 0], ln_b[layer, 0])
        h = x * (1.0 + sc2) + sh2
        y = peer(h, peer_w_q[layer], peer_keys[layer], peer_u[layer], peer_v[layer])
        x = layer_norm(ALPHA * x + g2 * y, ln_g[layer, 1], ln_b[layer, 1])
        if layer == N_A_LAYERS - 1:
            kv_mod = (c_act @ kv_ada_w + kv_ada_b)[:, None, :]
            kv_sh, kv_sc = jnp.split(kv_mod, 2, axis=-1)
            shared = nsa_shared_kv(x * (1.0 + kv_sc) + kv_sh, nsa_w_kv, cmp_pe, cmp_w1, cmp_b1, cmp_w2)
    return x
```

```python
from contextlib import contextmanager, ExitStack
import numpy as np
import concourse.bass as bass
import concourse.mybir as mybir
from concourse.bass_utils import run_bass_kernel_spmd

F32 = mybir.dt.float32
BF16 = mybir.dt.bfloat16
I32 = mybir.dt.int32
U32 = mybir.dt.uint32
ALU = mybir.AluOpType
AF = mybir.ActivationFunctionType
AX = mybir.AxisListType

D = 1024
B = 2
S = 8192
DEPTH = 4
ALPHA = (2.0 * DEPTH) ** 0.25
LN_EPS = 1e-5
NCORES = 8
TOK = 2048
GT = 256

WRITE_KW = ("out", "accum_out", "out_max", "out_indices", "ap")


class _Buf:
    __slots__ = ("w", "r", "ws")

    def __init__(self):
        self.w = None
        self.r = {}
        self.ws = None


class Prog:
    ENGS = ("pe", "dve", "act", "pool", "sp")

    def __init__(self, nc, n_dma_sems=48):
        self.nc = nc
        self.eng = dict(pe=nc.tensor, dve=nc.vector, act=nc.scalar, pool=nc.gpsimd, sp=nc.sync)
        self.sem = {e: nc.alloc_semaphore("sem_" + e) for e in self.ENGS}
        self.cnt = {e: 0 for e in self.ENGS}
        self.seen = {}
        self.dsem = [nc.alloc_semaphore("dsem%d" % i) for i in range(n_dma_sems)]
        self.dcnt = [0] * n_dma_sems
        self.drr = 0
        self.dseen = {}
        self.bufs = {}
        self.untracked = set()
        self.multi = set()
        self.csem = [nc.alloc_semaphore("csem%d" % i) for i in range(4)]
        self.ccnt = [0] * 4
        self.crr = 0
        self.n_inst = 0
        self.uid = 0
        self.stacks = [ExitStack()]

    def sb(self, name, shape, dtype=F32):
        self.uid += 1
        t = self.stacks[-1].enter_context(
            self.nc.sbuf_tensor("%s_%d" % (name, self.uid), list(shape), dtype))
        return t.ap()

    def ps(self, name, shape=(128, 512), dtype=F32):
        self.uid += 1
        t = self.stacks[-1].enter_context(
            self.nc.psum_tensor("%s_%d" % (name, self.uid), list(shape), dtype))
        return t.ap()

    @contextmanager
    def scope(self):
        st = ExitStack()
        self.stacks.append(st)
        try:
            yield
        finally:
            self.barrier()
            self.stacks.pop()
            st.close()

    def dram(self, name, shape, dtype=F32, kind="Internal", track=True):
        if kind == "Internal":
            t = self.nc.dram_tensor(name, list(shape), dtype)
        else:
            t = self.nc.dram_tensor(name, list(shape), dtype, kind=kind)
        ap = t.ap()
        if not track:
            self.untracked.add(ap.tensor.name)
        elif kind == "Internal":
            self.multi.add(ap.tensor.name)
        return ap

    def _buf(self, ap):
        n = ap.tensor.name
        if n in self.untracked:
            return None
        b = self.bufs.get(n)
        if b is None:
            b = self.bufs[n] = _Buf()
            if n in self.multi:
                b.ws = []
        return b

    def _wait(self, e, ev):
        if ev[0] == "cc":
            _, si, v = ev
            if self.dseen.get((e, "c", si), 0) >= v:
                return
            self.dseen[(e, "c", si)] = v
            self.eng[e].wait_ge(self.csem[si], v)
            return
        if ev[0] == "eng":
            _, pe, n = ev
            if e == "pe" and pe == "pe":
                return
            if self.seen.get((e, pe), 0) >= n:
                return
            self.seen[(e, pe)] = n
            self.eng[e].wait_ge(self.sem[pe], n)
        else:
            _, si, v = ev
            if self.dseen.get((e, si), 0) >= v:
                return
            self.dseen[(e, si)] = v
            self.eng[e].wait_ge(self.dsem[si], v)

    def _deps(self, e, reads, writes):
        evs = []
        for ap in reads:
            b = self._buf(ap)
            if b is None:
                continue
            if b.ws is not None:
                evs.extend(b.ws)
            elif b.w is not None:
                evs.append(b.w)
        for ap in writes:
            b = self._buf(ap)
            if b is None:
                continue
            if b.ws is None and b.w is not None:
                evs.append(b.w)
            evs.extend(b.r.values())
        for ev in evs:
            self._wait(e, ev)

    def _record(self, ev, key, reads, writes):
        for ap in reads:
            b = self._buf(ap)
            if b is not None:
                b.r[key] = ev
        for ap in writes:
            b = self._buf(ap)
            if b is not None:
                if b.ws is not None:
                    b.ws.append(ev)
                else:
                    b.w = ev
                b.r = {}

    def collective(self, kind, groups, in_ap, out_ap):
        self._deps("pool", [in_ap], [out_ap])
        si = self.crr
        self.crr = (self.crr + 1) % len(self.csem)
        if self.ccnt[si] > 0:
            self._wait("pool", ("cc", si, self.ccnt[si]))
        ins = self.nc.gpsimd.collective_compute(kind, ALU.bypass, replica_groups=groups,
                                                ins=[in_ap.opt()], outs=[out_ap.opt()])
        self.ccnt[si] += 1
        ins.then_inc(self.csem[si], 1)
        ev = ("cc", si, self.ccnt[si])
        self._record(ev, ("c", si), [in_ap], [out_ap])
        self.n_inst += 1
        return ev

    @staticmethod
    def _split(args, kw):
        reads, writes = [], []
        for k, v in kw.items():
            if isinstance(v, bass.AP):
                (writes if k in WRITE_KW else reads).append(v)
        for v in args:
            if isinstance(v, bass.AP):
                reads.append(v)
        return reads, writes

    def op(self, e, method, *args, **kw):
        nowaw = kw.pop("_nowaw", False)
        reads, writes = self._split(args, kw)
        if nowaw:
            evs = []
            for ap in reads:
                b = self._buf(ap)
                if b is None:
                    continue
                if b.ws is not None:
                    evs.extend(b.ws)
                elif b.w is not None:
                    evs.append(b.w)
            for ap in writes:
                b = self._buf(ap)
                if b is None:
                    continue
                if b.ws is None and b.w is not None and not (b.w[0] == "eng" and b.w[1] == e):
                    evs.append(b.w)
                for ev in b.r.values():
                    if not (ev[0] == "eng" and ev[1] == e):
                        evs.append(ev)
            for ev in evs:
                self._wait(e, ev)
        else:
            self._deps(e, reads, writes)
        ins = getattr(self.eng[e], method)(*args, **kw)
        self.cnt[e] += 1
        ins.then_inc(self.sem[e], 1)
        ev = ("eng", e, self.cnt[e])
        self._record(ev, e, reads, writes)
        self.n_inst += 1
        return ev

    def dma(self, q, out, in_, **kw):
        reads, writes = [in_], [out]
        self._deps(q, reads, writes)
        si = self.drr
        self.drr = (self.drr + 1) % len(self.dsem)
        if self.dcnt[si] > 0:
            self._wait(q, ("dma", si, self.dcnt[si]))
        ins = self.eng[q].dma_start(out=out, in_=in_, **kw)
        self.dcnt[si] += 16
        ins.then_inc(self.dsem[si], 16)
        ev = ("dma", si, self.dcnt[si])
        self._record(ev, ("d", si), reads, writes)
        self.n_inst += 1
        return ev

    def barrier(self, engs=None, final=False):
        sp = "sp"
        for si, v in enumerate(self.dcnt):
            if v > 0 and not any(self.dseen.get((e, si), 0) >= v for e in self.ENGS):
                self._wait(sp, ("dma", si, v))
        for si, v in enumerate(self.ccnt):
            if final and v > 0 and not any(self.dseen.get((e, "c", si), 0) >= v for e in self.ENGS):
                self._wait(sp, ("cc", si, v))
        for pe in self.ENGS:
            if pe != sp and self.cnt[pe] > 0:
                self._wait(sp, ("eng", pe, self.cnt[pe]))
        ins = self.eng[sp].nop()
        self.cnt[sp] += 1
        ins.then_inc(self.sem[sp], 1)
        for e in self.ENGS:
            if e != sp:
                self._wait(e, ("eng", sp, self.cnt[sp]))
        for e in self.ENGS:
            for pe in self.ENGS:
                self.seen[(e, pe)] = self.cnt[pe]
            for si, v in enumerate(self.dcnt):
                self.dseen[(e, si)] = v
            if final:
                for si, v in enumerate(self.ccnt):
                    self.dseen[(e, "c", si)] = v
        for b in self.bufs.values():
            if b.w is not None and b.w[0] != "cc":
                b.w = None
            b.r = {k: ev for k, ev in b.r.items() if ev[0] == "cc"}
            if b.ws is not None:
                b.ws = [ev for ev in b.ws if ev[0] == "cc"]

    def finish(self):
        self.barrier(final=True)

    def mm(self, out, lhsT, rhs, start=True, stop=True):
        return self.op("pe", "matmul", out=out, lhsT=lhsT, rhs=rhs, start=start, stop=stop)

    def tr(self, out, in_, ident):
        return self.op("pe", "transpose", out=out, in_=in_, identity=ident)

    def tt(self, out, in0, in1, op, e="dve", nowaw=False):
        return self.op(e, "tensor_tensor", out=out, in0=in0, in1=in1, op=op, _nowaw=nowaw)

    def ts(self, out, in0, s1, op0, s2=None, op1=None, e="dve", nowaw=False):
        if op1 is None:
            return self.op(e, "tensor_scalar", out=out, in0=in0, scalar1=s1, scalar2=None, op0=op0, _nowaw=nowaw)
        return self.op(e, "tensor_scalar", out=out, in0=in0, scalar1=s1, scalar2=s2, op0=op0, op1=op1,
                       _nowaw=nowaw)

    def cp(self, out, in_, e="dve", nowaw=False):
        return self.op(e, "tensor_copy", out=out, in_=in_, _nowaw=nowaw)

    def act(self, out, in_, func, nowaw=False, **kw):
        return self.op("act", "activation", out=out, in_=in_, func=func, _nowaw=nowaw, **kw)


def emit_mod(P, cT_d, w_d, b_d, ident, col0, ncols, pbank, pbank2, mod_bc, modT=None):
    with P.scope():
        cT = P.sb("cT", [128, 8])
        P.dma("sp", out=cT, in_=cT_d)
        ca = P.sb("ca", [128, 8])
        P.act(ca, cT, AF.Silu)
        crep = P.sb("crep", [128, 8, 128])
        for kc in range(8):
            P.cp(crep[:, kc, :], ca[:, kc:kc + 1].to_broadcast([128, 128]))
        wv = w_d.rearrange("(kc p) n -> p kc n", p=128)
        awb = [P.sb("aw%d" % i, [128, 8, 512]) for i in range(2)]
        bbb = [P.sb("bb%d" % i, [128, 512]) for i in range(2)]
        pbs = [pbank, pbank2]
        for ci in range(ncols // 512):
            n0 = col0 + ci * 512
            aw = awb[ci % 2]
            bb = bbb[ci % 2]
            pb = pbs[ci % 2]
            P.dma("sp", out=aw, in_=wv[:, :, n0:n0 + 512])
            P.dma("sp", out=bb, in_=b_d[0:1, n0:n0 + 512].to_broadcast([128, 512]))
            for kc in range(8):
                P.mm(pb, crep[:, kc, :], aw[:, kc, :], start=(kc == 0), stop=(kc == 7))
            P.tt(mod_bc[:, ci * 512:(ci + 1) * 512], pb, bb, ALU.add)
        if modT is not None:
            for j in range(ncols // 128):
                pb = pbs[j % 2]
                P.tr(pb[:, 0:128], mod_bc[:, j * 128:(j + 1) * 128], ident)
                P.cp(modT[:, j:j + 1], pb[:, 0:1])


def emit_ln(P, out, pre, g_bc, b_bc, scr):
    st, mv, rs = scr
    P.op("dve", "bn_stats", out=st[:, 0:6], in_=pre[:, 0:512])
    P.op("dve", "bn_stats", out=st[:, 6:12], in_=pre[:, 512:1024])
    P.op("dve", "bn_aggr", out=mv, in_=st)
    P.ts(rs, mv[:, 1:2], LN_EPS, ALU.add)
    P.act(rs, rs, AF.Sqrt)
    P.op("dve", "reciprocal", out=rs, in_=rs)
    P.ts(out, pre, mv[:, 0:1], ALU.subtract, rs[:, 0:1], ALU.mult)
    P.tt(out, out, g_bc, ALU.mult)
    P.tt(out, out, b_bc, ALU.add)


def emit_top16(P, probs):
    for (vals, idxs, src, tmp) in probs:
        P.op("dve", "max", out=vals[:, 0:8], in_=src, _nowaw=True)
    for (vals, idxs, src, tmp) in probs:
        P.op("dve", "max_index", out=idxs[:, 0:8], in_max=vals[:, 0:8], in_values=src, _nowaw=True)
    for (vals, idxs, src, tmp) in probs:
        P.op("dve", "match_replace", out=tmp, in_to_replace=vals[:, 0:8], in_values=src, imm_value=-1e30,
             _nowaw=True)
    for (vals, idxs, src, tmp) in probs:
        P.op("dve", "max", out=vals[:, 8:16], in_=tmp, _nowaw=True)
    for (vals, idxs, src, tmp) in probs:
        P.op("dve", "max_index", out=idxs[:, 8:16], in_max=vals[:, 8:16], in_values=tmp, _nowaw=True)


def emit_tok(P, pb, ident, iota, A, KO, with_kv, ngroups=TOK // GT):
  with P.scope():
    KC = KO // 128
    cT_d, aw_d, ab_d, lng_d, lnb_d = A["cT"], A["ada_w"], A["ada_b"], A["ln_g"], A["ln_b"]
    wo_d, wq_d, keysT_d = A["w_o"], A["w_q"], A["keysT"]
    if with_kv:
        kvw_d, kvb_d, wkv_d = A["kv_ada_w"], A["kv_ada_b"], A["w_kv"]
    keysT = P.sb("keysT", [128, 16, 128])
    P.dma("sp", out=keysT, in_=keysT_d)
    lng = [P.sb("lng%d" % i, [128, D]) for i in range(2)]
    lnb = [P.sb("lnb%d" % i, [128, D]) for i in range(2)]
    for i in range(2):
        P.dma("sp", out=lng[i], in_=lng_d[i:i + 1, :].to_broadcast([128, D]))
        P.dma("sp", out=lnb[i], in_=lnb_d[i:i + 1, :].to_broadcast([128, D]))
    mod_bc = P.sb("mod_bc", [128, 4096])
    modT = P.sb("modT", [128, 32])
    emit_mod(P, cT_d, aw_d, ab_d, ident, 2048, 4096, pb[0], pb[1], mod_bc, modT)
    g1_bc = mod_bc[:, 0:1024]
    g2_bc = mod_bc[:, 3072:4096]
    sh2T = modT[:, 8:16]
    sc2T = P.sb("sc2T", [128, 8])
    P.ts(sc2T, modT[:, 16:24], 1.0, ALU.add)
    if with_kv:
        kvm_bc = P.sb("kvm_bc", [128, 2048])
        kvmT = P.sb("kvmT", [128, 16])
        emit_mod(P, cT_d, kvw_d, kvb_d, ident, 0, 2048, pb[0], pb[1], kvm_bc, kvmT)
        kvshT = kvmT[:, 0:8]
        kvscT = P.sb("kvscT", [128, 8])
        P.ts(kvscT, kvmT[:, 8:16], 1.0, ALU.add)

    lnscr = (P.sb("lnst", [128, 12]), P.sb("lnmv", [128, 2]), P.sb("lnrs", [128, 1]))
    x1 = [P.sb("x1_%d" % i, [128, D]) for i in range(2)]
    h2T = P.sb("h2T", [128, 8, GT])
    h2bf = P.sb("h2bf", [128, 8, GT], BF16)
    i1T = P.sb("i1T", [128, GT], BF16)
    i2T = P.sb("i2T", [128, GT], BF16)
    wT = P.sb("wT", [128, GT], BF16)
    iota_bf = P.sb("iota_bf", [128, 128], BF16)
    P.cp(iota_bf, iota)

    P.uid += 1
    wo_bf = P.dram("wo_bf_%d" % P.uid, [KO, D], BF16)
    wq_bf = P.dram("wq_bf_%d" % P.uid, [D, 2048], BF16)
    for kb in range(KC // 4):
        P.dma("pool", out=wo_bf[kb * 512:(kb + 1) * 512, :], in_=wo_d[kb * 512:(kb + 1) * 512, :])
    for kb in range(4):
        P.dma("pool", out=wq_bf[kb * 256:(kb + 1) * 256, :], in_=wq_d[kb * 256:(kb + 1) * 256, :])
    wo_v = wo_bf.rearrange("(kc p) n -> p kc n", p=128)
    wq_v = wq_bf.rearrange("(kc p) n -> p kc n", p=128)
    xt = [P.sb("xt%d" % i, [128, D]) for i in range(2)]
    ot = [P.sb("ot%d" % i, [128, KO]) for i in range(2)]
    wob0 = P.sb("wob0", [128, 4, D], BF16)

    def prefetch(g):
        for tt in range(2):
            P.dma("sp", out=xt[tt], in_=A["x_tile"](g * 2 + tt))
            for (oap, c0, wd) in A["o_pieces"](g * 2 + tt):
                P.dma("sp", out=ot[tt][:, c0:c0 + wd], in_=oap)
        P.dma("sp", out=wob0, in_=wo_v[:, 0:4, :])

    prefetch(0)

    for grp in range(ngroups):
        t0 = grp * GT
        with P.scope():
            oT = P.sb("oT", [128, KC, GT], BF16)
            wob = [P.sb("wob%d" % i, [128, 4, D], BF16) for i in range(2)]
            pre = P.sb("pre", [128, D])
            for tt in range(2):
                for k4 in range(KC // 4):
                    bank = pb[4 + (k4 % 2)]
                    for q in range(4):
                        kc = k4 * 4 + q
                        P.tr(bank[:, q * 128:(q + 1) * 128], ot[tt][:, kc * 128:(kc + 1) * 128], ident)
                    P.cp(oT[:, k4 * 4:(k4 + 1) * 4, tt * 128:(tt + 1) * 128],
                         bank.rearrange("p (q t) -> p q t", q=4), nowaw=True)
            for kb in range(KC // 4):
                if kb == 0:
                    wb = wob0
                else:
                    wb = wob[kb % 2]
                    P.dma("sp", out=wb, in_=wo_v[:, kb * 4:(kb + 1) * 4, :])
                for tt in range(2):
                    for half in range(2):
                        for q in range(4):
                            kc = kb * 4 + q
                            P.mm(pb[tt * 2 + half], oT[:, kc, tt * 128:(tt + 1) * 128],
                                 wb[:, q, half * 512:(half + 1) * 512], start=(kc == 0), stop=(kc == KC - 1))
            for tt in range(2):
                for half in range(2):
                    sl = slice(half * 512, (half + 1) * 512)
                    P.tt(pre[:, sl], pb[tt * 2 + half], g1_bc[:, sl], ALU.mult)
                P.op("dve", "scalar_tensor_tensor", out=pre, in0=xt[tt], scalar=ALPHA, in1=pre,
                     op0=ALU.mult, op1=ALU.add)
                emit_ln(P, x1[tt], pre, lng[0], lnb[0], lnscr)
        for tt in range(2):
            for k4 in range(2):
                bank = pb[4 + k4]
                for q in range(4):
                    kc = k4 * 4 + q
                    P.tr(bank[:, q * 128:(q + 1) * 128], x1[tt][:, kc * 128:(kc + 1) * 128], ident)
                for q in range(4):
                    kc = k4 * 4 + q
                    P.ts(h2T[:, kc, tt * 128:(tt + 1) * 128], bank[:, q * 128:(q + 1) * 128],
                         sc2T[:, kc:kc + 1], ALU.mult, sh2T[:, kc:kc + 1], ALU.add, nowaw=True)
        P.cp(h2bf, h2T)
        with P.scope():
            wqb = [P.sb("wqb%d" % i, [128, 8, 128], BF16) for i in range(3)]
            qT = P.sb("qT", [128, 16, GT])
            for g in range(16):
                wb = wqb[g % 3]
                P.dma("sp", out=wb, in_=wq_v[:, :, g * 128:(g + 1) * 128])
                bank = pb[4 + g % 2]
                for kc in range(8):
                    P.mm(bank[:, 0:GT], wb[:, kc, :], h2bf[:, kc, :], start=(kc == 0), stop=(kc == 7))
                if g % 2 == 0:
                    P.cp(qT[:, g, :], bank[:, 0:GT])
                else:
                    P.act(qT[:, g, :], bank[:, 0:GT], AF.Copy)
            s_sb = P.sb("s_sb", [128, 16, 128])
            tmp = P.sb("tk_tmp", [128, 16, 128])
            tmpc = P.sb("tk_tmpc", [128, 8, 256])
            sv = P.sb("sv", [128, 16, 16])
            si = P.sb("si", [128, 16, 16], U32)
            sif = P.sb("sif", [128, 16, 16])
            comb = P.sb("comb", [128, 8, 256])
            cv = P.sb("cv", [128, 8, 16])
            ci = P.sb("ci", [128, 8, 16], U32)
            chi = P.sb("chi", [128, 8, 16], U32)
            clo = P.sb("clo", [128, 8, 16], U32)
            chif = P.sb("chif", [128, 8, 16])
            clof = P.sb("clof", [128, 8, 16])
            ee = P.sb("ee", [128, 8, 16])
            zz = P.sb("zz", [128, 8])
            eq = P.sb("eq", [128, 8, 16, 16])
            i1f = P.sb("i1f", [128, 8, 16])
            i2f = P.sb("i2f", [128, 8, 16])
            ww = P.sb("ww", [128, 8, 16])
            for tt in range(2):
                tsl = slice(tt * 128, (tt + 1) * 128)
                for g4 in range(4):
                    bank = pb[g4]
                    for q in range(4):
                        g = g4 * 4 + q
                        P.mm(bank[:, q * 128:(q + 1) * 128], qT[:, g, tsl], keysT[:, g, :])
                    P.cp(s_sb[:, g4 * 4:(g4 + 1) * 4, :], bank.rearrange("p (q n) -> p q n", q=4), nowaw=True)
                emit_top16(P, [(sv[:, g, :], si[:, g, :], s_sb[:, g, :], tmp[:, g, :]) for g in range(16)])
                P.cp(sif, si)
                svv = sv.rearrange("p (h c) k -> p h c k", c=2)
                sfv = sif.rearrange("p (h c) k -> p h c k", c=2)
                c4 = comb.rearrange("p h (i j) -> p h i j", j=16)
                P.tt(c4, svv[:, :, 0, :].unsqueeze(3).to_broadcast([128, 8, 16, 16]),
                     svv[:, :, 1, :].unsqueeze(2).to_broadcast([128, 8, 16, 16]), ALU.add)
                emit_top16(P, [(cv[:, h, :], ci[:, h, :], comb[:, h, :], tmpc[:, h, :]) for h in range(8)])
                P.tt(ee, cv, cv[:, :, 0:1].to_broadcast([128, 8, 16]), ALU.subtract)
                P.act(ee, ee, AF.Exp)
                P.op("dve", "tensor_reduce", out=zz, in_=ee, axis=AX.X, op=ALU.add)
                P.op("dve", "reciprocal", out=zz, in_=zz)
                P.tt(ww, ee, zz.unsqueeze(2).to_broadcast([128, 8, 16]), ALU.mult)
                P.ts(chi, ci, 4, ALU.logical_shift_right)
                P.ts(clo, ci, 15, ALU.bitwise_and)
                P.cp(chif, chi)
                P.cp(clof, clo)
                io16 = iota[:, 0:16].unsqueeze(1).unsqueeze(1).to_broadcast([128, 8, 16, 16])
                for (cf, cc, dst) in ((chif, 0, i1f), (clof, 1, i2f)):
                    P.tt(eq, cf.unsqueeze(3).to_broadcast([128, 8, 16, 16]), io16, ALU.is_equal)
                    P.tt(eq, eq, sfv[:, :, cc, :].unsqueeze(2).to_broadcast([128, 8, 16, 16]), ALU.mult)
                    P.op("dve", "tensor_reduce", out=dst, in_=eq, axis=AX.X, op=ALU.add)
                for (src, dstT, bank) in ((i1f, i1T, pb[4]), (i2f, i2T, pb[5]), (ww, wT, pb[6])):
                    P.tr(bank[:, 0:128], src.rearrange("p h k -> p (h k)"), ident)
                    P.cp(dstT[:, tsl], bank[:, 0:128])
        wt_scope = P.scope()
        wt_scope.__enter__()
        Wt = P.sb("Wt", [128, GT, 128], BF16)
        with P.scope():
            SBK = 32
            d1 = [P.sb("d1_%d" % i, [128, SBK, 128], BF16) for i in range(2)]
            d2 = [P.sb("d2_%d" % i, [128, SBK, 128], BF16) for i in range(2)]
            for sbk in range(GT // SBK):
                a1 = d1[sbk % 2]
                a2 = d2[sbk % 2]
                ts0 = sbk * SBK
                for t in range(SBK):
                    P.op("dve", "tensor_scalar", out=a1[:, t, :], in0=iota_bf, scalar1=i1T[:, ts0 + t:ts0 + t + 1],
                         scalar2=wT[:, ts0 + t:ts0 + t + 1], op0=ALU.is_equal, op1=ALU.mult, _nowaw=True)
                    P.op("dve", "tensor_scalar", out=a2[:, t, :], in0=iota_bf, scalar1=i2T[:, ts0 + t:ts0 + t + 1],
                         scalar2=None, op0=ALU.is_equal, _nowaw=True)
                for t4 in range(SBK // 4):
                    bank = pb[4 + (t4 % 4)]
                    for q in range(4):
                        t = t4 * 4 + q
                        P.mm(bank[:, q * 128:(q + 1) * 128], a2[:, t, :], a1[:, t, :])
                    tg = ts0 + t4 * 4
                    dst = Wt[:, tg:tg + 4, :]
                    src = bank.rearrange("p (t n) -> p t n", t=4)
                    P.act(dst, src, AF.Copy, nowaw=True)
        with P.scope():
            NBP = 3
            LA3 = 3
            utb = [P.sb("utb%d" % i, [128, 2, 8, 128], BF16) for i in range(NBP)]
            vtb = [P.sb("vtb%d" % i, [128, 2, D], BF16) for i in range(NBP)]
            gl = [P.sb("gl%d" % i, [128, GT]) for i in range(4)]
            cf = [P.sb("cf%d" % i, [128, GT], BF16) for i in range(4)]

            def c3_s1(n1):
                q_, c_ = n1 // 2, n1 % 2
                if c_ == 0:
                    P.dma(A.get("tab_q", "pool"), out=utb[q_ % NBP].rearrange("p c k n -> p c (k n)"),
                          in_=A["uT_pair"](q_))
                    P.dma(A.get("tab_q", "pool"), out=vtb[q_ % NBP], in_=A["v_pair"](q_))
                ub = utb[q_ % NBP][:, c_]
                pa = pb[4 + n1 % 4]
                for kc in range(8):
                    P.mm(pa[:, 0:GT], ub[:, kc, :], h2bf[:, kc, :], start=(kc == 0), stop=(kc == 7))
                P.act(gl[n1 % 4], pa[:, 0:GT], AF.Gelu_apprx_tanh)
                P.tt(cf[n1 % 4], gl[n1 % 4], Wt[:, :, n1], ALU.mult)

            def c3_s2(n1):
                c_ = cf[n1 % 4]
                vb = vtb[(n1 // 2) % NBP][:, n1 % 2, :]
                for tt in range(2):
                    for half in range(2):
                        P.mm(pb[tt * 2 + half], c_[:, tt * 128:(tt + 1) * 128],
                             vb[:, half * 512:(half + 1) * 512], start=(n1 == 0), stop=(n1 == 127))

            for it in range(128 + LA3):
                if it == 8 and grp + 1 < ngroups:
                    prefetch(grp + 1)
                if it < 128:
                    c3_s1(it)
                if it >= LA3:
                    c3_s2(it - LA3)
        wt_scope.__exit__(None, None, None)
        with P.scope():
            pre = P.sb("pre2", [128, D])
            x2 = [P.sb("x2_%d" % i, [128, D]) for i in range(2)]
            for tt in range(2):
                for half in range(2):
                    sl = slice(half * 512, (half + 1) * 512)
                    P.tt(pre[:, sl], pb[tt * 2 + half], g2_bc[:, sl], ALU.mult)
                P.op("dve", "scalar_tensor_tensor", out=pre, in0=x1[tt], scalar=ALPHA, in1=pre,
                     op0=ALU.mult, op1=ALU.add)
                emit_ln(P, x2[tt], pre, lng[1], lnb[1], lnscr)
                for xo in A["x_out"](grp * 2 + tt):
                    P.dma("sp", out=xo, in_=x2[tt])
            if "after_group" in A:
                A["after_group"](grp)
            if with_kv:
                hkT = P.sb("hkT", [128, 8, GT])
                wkv = P.sb("wkv", [128, 8, 1536])
                P.dma("sp", out=wkv, in_=wkv_d.rearrange("(kc p) n -> p kc n", p=128))
                for tt in range(2):
                    for k4 in range(2):
                        bank = pb[4 + k4]
                        for q in range(4):
                            kc = k4 * 4 + q
                            P.tr(bank[:, q * 128:(q + 1) * 128], x2[tt][:, kc * 128:(kc + 1) * 128], ident)
                        for q in range(4):
                            kc = k4 * 4 + q
                            P.ts(hkT[:, kc, tt * 128:(tt + 1) * 128], bank[:, q * 128:(q + 1) * 128],
                                 kvscT[:, kc:kc + 1], ALU.mult, kvshT[:, kc:kc + 1], ALU.add)
                for tt in range(2):
                    vt_sb = P.sb("vt_sb%d" % tt, [128, 2, 256])
                    for wi, c0 in enumerate((768, 1280)):
                        bank = pb[wi]
                        for kc in range(8):
                            P.mm(bank[:, 0:256], hkT[:, kc, tt * 128:(tt + 1) * 128], wkv[:, kc, c0:c0 + 256],
                                 start=(kc == 0), stop=(kc == 7))
                        P.cp(vt_sb[:, wi, :], bank[:, 0:256])
                        dst = A["vtok_out"](wi)[:, t0 + tt * 128:t0 + (tt + 1) * 128, :].rearrange("g t d -> t g d")
                        P.dma("sp", out=dst, in_=vt_sb[:, wi, :].rearrange("p (g d) -> p g d", g=4))
                for wi, c0 in enumerate((0, 256, 512, 1024)):
                    for cb in range(2):
                        bank = pb[2 + (wi * 2 + cb) % 2]
                        for kc in range(8):
                            P.mm(bank[:, 0:GT], wkv[:, kc, c0 + cb * 128:c0 + (cb + 1) * 128], hkT[:, kc, :],
                                 start=(kc == 0), stop=(kc == 7))
                        kt_sb = P.sb("kt_sb%d_%d" % (wi, cb), [128, GT])
                        P.cp(kt_sb, bank[:, 0:GT])
                        for gg in range(2):
                            P.dma("sp", out=A["kvT_out"](wi, cb * 2 + gg)[:, t0:t0 + GT],
                                  in_=kt_sb[gg * 64:(gg + 1) * 64, :])


def emit_hT(P, hT, xt, scT, shT, ident, bank0, bank1):
    for k4 in range(2):
        bank = (bank0, bank1)[k4]
        for q in range(4):
            kc = k4 * 4 + q
            P.tr(bank[:, q * 128:(q + 1) * 128], xt[:, kc * 128:(kc + 1) * 128], ident)
        for q in range(4):
            kc = k4 * 4 + q
            P.ts(hT[:, kc, :], bank[:, q * 128:(q + 1) * 128], scT[:, kc:kc + 1], ALU.mult,
                 shT[:, kc:kc + 1], ALU.add, nowaw=True)


def emit_retmix(P, pb, ident, A, nchunks=S // 128):
  with P.scope():
    cT_d, aw_d, ab_d = A["cT"], A["ada_w"], A["ada_b"]
    wqk_d, wv_d, wg_d, cos_d, sin_d = A["wqk"], A["wv"], A["wg"], A["cos"], A["sin"]
    decT = P.sb("decT", [128, 128])
    P.dma("sp", out=decT, in_=A["decT"])
    cols = P.sb("cols", [128, 4])
    P.dma("sp", out=cols, in_=A["cols"])
    mod_bc = P.sb("mod_bc", [128, 2048])
    modT = P.sb("modT", [128, 16])
    emit_mod(P, cT_d, aw_d, ab_d, ident, 0, 2048, pb[0], pb[1], mod_bc, modT)
    shT = modT[:, 0:8]
    scT = P.sb("scT", [128, 8])
    P.ts(scT, modT[:, 8:16], 1.0, ALU.add)
    wqk = P.sb("wqk", [128, 8, 512], BF16)
    wv = P.sb("wv", [128, 8, 512], BF16)
    wg = P.sb("wg", [128, 8, 512], BF16)
    for (w, wd) in ((wqk, wqk_d), (wv, wv_d), (wg, wg_d)):
        P.dma("pool", out=w, in_=wd.rearrange("(kc p) n -> p kc n", p=128))
    state = P.sb("state", [128, 2, 512])
    P.op("dve", "memset", ap=state, constant=0.0)
    NBUF = 2
    mk = lambda n, s: [P.sb("%s%d" % (n, i), s) for i in range(NBUF)]
    xt_, cs_, sn_ = mk("xt", [128, D]), mk("cs", [128, 128]), mk("sn", [128, 128])
    hT_ = [P.sb("hT%d" % i, [128, 8, 128], BF16) for i in range(NBUF)]
    rot_, t1_, t2_ = mk("rot", [128, 2, 2, 128]), mk("t1", [128, 2, 128]), mk("t2", [128, 2, 128])
    qd_, kd_, ks_ = mk("qd", [128, 256]), mk("kd", [128, 256]), mk("ks", [128, 256])
    v_, qkT_, PT_ = mk("v", [128, 512]), mk("qkT", [128, 4, 128]), mk("PT", [128, 128])
    on_, sg_ = mk("on", [128, 512]), mk("sg", [128, 512])
    st_, mv_, rs_ = mk("st", [128, 6]), mk("mv", [128, 2]), mk("rs", [128, 1])
    def ret_load(m):
        j_ = m % NBUF
        rws = slice(m * 128, (m + 1) * 128)
        P.dma("sp", out=xt_[j_], in_=A["x_tile"](m))
        P.dma("sp", out=cs_[j_], in_=cos_d[rws, :])
        P.dma("sp", out=sn_[j_], in_=sin_d[rws, :])

    for n in range(nchunks):
        i = n % NBUF
        xt, hT, cs, sn, rot, t1, t2 = xt_[i], hT_[i], cs_[i], sn_[i], rot_[i], t1_[i], t2_[i]
        qd, kd, ks, v, qkT, PT, on, sg = qd_[i], kd_[i], ks_[i], v_[i], qkT_[i], PT_[i], on_[i], sg_[i]
        st, mv, rs = st_[i], mv_[i], rs_[i]
        rows = slice(n * 128, (n + 1) * 128)
        if "bg" in A:
            A["bg"](n)
        if n == 0:
            ret_load(0)
        emit_hT(P, hT, xt, scT, shT, ident, pb[0], pb[1])
        if n + 1 < nchunks:
            ret_load(n + 1)
        for (bank, w) in ((pb[2], wqk), (pb[3], wv), (pb[4], wg)):
            for kc in range(8):
                P.mm(bank, hT[:, kc, :], w[:, kc, :], start=(kc == 0), stop=(kc == 7))
        P.act(v, pb[3], AF.Copy)
        P.act(sg, pb[4], AF.Silu)
        qk4 = pb[2].rearrange("p (a h d) -> p a h d", a=2, h=2)
        x1 = qk4[:, :, 0, :]
        x2 = qk4[:, :, 1, :]
        csb = cs.unsqueeze(1).to_broadcast([128, 2, 128])
        snb = sn.unsqueeze(1).to_broadcast([128, 2, 128])
        P.tt(t1, x1, csb, ALU.mult)
        P.tt(t2, x2, snb, ALU.mult)
        P.tt(rot[:, :, 0, :], t1, t2, ALU.subtract)
        P.tt(t1, x1, snb, ALU.mult)
        P.tt(t2, x2, csb, ALU.mult)
        P.tt(rot[:, :, 1, :], t1, t2, ALU.add)
        qr = rot[:, 0, :, :].rearrange("p h d -> p (h d)")
        kr = rot[:, 1, :, :].rearrange("p h d -> p (h d)")
        P.ts(qd, qr, cols[:, 0:1], ALU.mult)
        P.ts(kd, kr, cols[:, 1:2], ALU.mult)
        P.ts(ks, kr, 1.0 / 16.0, ALU.mult)
        for dc in range(2):
            P.tr(pb[0][:, dc * 128:(dc + 1) * 128], qd[:, dc * 128:(dc + 1) * 128], ident)
            P.tr(pb[0][:, (2 + dc) * 128:(3 + dc) * 128], ks[:, dc * 128:(dc + 1) * 128], ident)
        P.cp(qkT, pb[0].rearrange("p (a t) -> p a t", a=4))
        for dc in range(2):
            P.mm(pb[1][:, 0:128], qkT[:, 2 + dc, :], qkT[:, dc, :], start=(dc == 0), stop=(dc == 1))
        P.tt(PT, pb[1][:, 0:128], decT, ALU.mult)
        P.mm(pb[5], PT, v, start=True, stop=False)
        for dc in range(2):
            P.mm(pb[5], qkT[:, dc, :], state[:, dc, :], start=False, stop=(dc == 1))
        for dc in range(2):
            P.mm(pb[6 + dc], kd[:, dc * 128:(dc + 1) * 128], v)
        for dc in range(2):
            P.op("dve", "scalar_tensor_tensor", out=state[:, dc, :], in0=state[:, dc, :], scalar=cols[:, 2:3],
                 in1=pb[6 + dc], op0=ALU.mult, op1=ALU.add)
        P.op("dve", "bn_stats", out=st, in_=pb[5])
        P.op("dve", "bn_aggr", out=mv, in_=st)
        P.ts(rs, mv[:, 1:2], LN_EPS, ALU.add)
        P.act(rs, rs, AF.Sqrt)
        P.op("dve", "reciprocal", out=rs, in_=rs)
        P.ts(on, pb[5], mv[:, 0:1], ALU.subtract, rs[:, 0:1], ALU.mult)
        P.tt(on, on, sg, ALU.mult)
        P.dma("sp", out=A["o_out"](n), in_=on)
        if "after_out" in A:
            A["after_out"](n)


def emit_nsamix(P, pb, ident, A, nqb=S // 128):
  with P.scope():
    SE = nqb * 128
    NCP = SE // 16
    NC = NCP - 1
    NCH = max(1, NCP // 128)
    NR = SE // TOK
    cT_d, aw_d, ab_d, wq_d, wgt_d = A["cT"], A["ada_w"], A["ada_b"], A["wq"], A["wgate"]
    peT_d, w1_d, b1T_d, w2_d, c2s_d, E_d = A["peT"], A["w1"], A["b1T"], A["w2l"], A["c2s"], A["Eall"]
    tri_d, low_d, A_d, av_d, fb_d = A["tri"], A["low"], A["Acmp"], A["availW"], A["fbW"]
    ld = lambda name, shape, src: (lambda t: (P.dma("sp", out=t, in_=src), t)[1])(P.sb(name, shape))
    tri = ld("tri", [128, 128], tri_d)
    low = ld("low", [128, 128], low_d)
    Acmp = ld("Acmp", [128, 128], A_d)
    availW = ld("availW", [128, 256], av_d)
    fbW = ld("fbW", [128, 256], fb_d)
    c2s = ld("c2s", [128, NCH, 128], c2s_d)
    b1T = ld("b1T", [128, 2, 2], b1T_d)
    w2l = ld("w2l", [128, 2, 2, 64], w2_d)
    wq = P.sb("wq", [128, 8, 256], BF16)
    wgt = P.sb("wgt", [128, 8, 12], BF16)
    P.dma("pool", out=wq, in_=wq_d.rearrange("(kc p) n -> p kc n", p=128))
    P.dma("pool", out=wgt, in_=wgt_d.rearrange("(kc p) n -> p kc n", p=128))
    mod_bc = P.sb("mod_bc", [128, 2048])
    modT = P.sb("modT", [128, 16])
    emit_mod(P, cT_d, aw_d, ab_d, ident, 0, 2048, pb[0], pb[1], mod_bc, modT)
    shT = modT[:, 0:8]
    scT = P.sb("scT", [128, 8])
    P.ts(scT, modT[:, 8:16], 1.0, ALU.add)

    kcmpT = P.sb("kcmpT", [64, NCH * 128])
    vcmp = P.sb("vcmp", [128, NCH, 65])
    P.op("dve", "memset", ap=kcmpT, constant=0.0)
    P.op("dve", "memset", ap=vcmp, constant=0.0)
    P.op("dve", "memset", ap=vcmp[:, :, 64:65], constant=1.0)
    with P.scope():
        rawT = P.sb("rawT", [64, SE])
        peT = P.sb("peT", [64, 32])
        w1b = [P.sb("w1b%d" % i, [64, 256]) for i in range(3)]
        hid = P.sb("hid", [128, 2, NCH * 128])
        biasT = P.sb("biasT", [128, 2])
        P.op("dve", "memset", ap=hid, constant=0.0)
        for c in range(2):
            for rk in range(NR):
                P.dma("sp", out=rawT[:, rk * TOK:(rk + 1) * TOK], in_=A["kvT"](c, rk))
            P.dma("sp", out=peT, in_=peT_d[c])
            rv = rawT.rearrange("d (n s) -> d n s", s=16)
            for p in range(32):
                wb = w1b[p % 3]
                P.dma("sp", out=wb, in_=w1_d[c, p * 64:(p + 1) * 64, :])
                xp = rv[:, 0:NC, p] if p < 16 else rv[:, 1:NC + 1, p - 16]
                for hc in range(2):
                    P.mm(pb[hc][:, 0:NC], wb[:, hc * 128:(hc + 1) * 128], xp, start=(p == 0), stop=(p == 31))
                    P.mm(pb[2 + hc][:, 0:1], wb[:, hc * 128:(hc + 1) * 128], peT[:, p:p + 1],
                         start=(p == 0), stop=(p == 31))
            for hc in range(2):
                P.tt(biasT[:, hc:hc + 1], pb[2 + hc][:, 0:1], b1T[:, c, hc:hc + 1], ALU.add)
                P.act(hid[:, hc, 0:NC], pb[hc][:, 0:NC], AF.Gelu_apprx_tanh, bias=biasT[:, hc:hc + 1])
            if c == 0:
                for hc in range(2):
                    P.mm(pb[4][0:64, 0:NC], w2l[:, 0, hc, :], hid[:, hc, 0:NC], start=(hc == 0), stop=(hc == 1))
                P.cp(kcmpT[:, 0:NC], pb[4][0:64, 0:NC])
            else:
                for ch in range(NCH):
                    for hc in range(2):
                        P.mm(pb[4][:, ch * 64:(ch + 1) * 64], hid[:, hc, ch * 128:(ch + 1) * 128], w2l[:, 1, hc, :],
                             start=(hc == 0), stop=(hc == 1))
                    P.cp(vcmp[:, ch, 0:64], pb[4][:, ch * 64:(ch + 1) * 64])

    import os
    KD = BF16 if os.environ.get("NSA_BF", "1") == "1" else F32
    kslcT = P.sb("kslcT", [64, SE], KD)
    kwinT = P.sb("kwinT", [64, SE], KD)
    vslc = P.sb("vslc", [128, nqb, 66], KD)
    vwin = P.sb("vwin", [128, nqb, 66], KD)
    Eall = P.sb("Eall", [128, nqb, 128], KD)
    CPR = TOK // 128
    with P.scope():
        stg = P.sb("stg", [128, nqb * 128])
        for (dst, w) in ((kslcT, 2), (kwinT, 3)):
            for rk in range(NR):
                P.dma("sp", out=stg[0:64, rk * TOK:(rk + 1) * TOK], in_=A["kvT"](w, rk))
            P.cp(dst, stg[0:64, 0:SE])
        for (vt, bi) in ((vslc, 0), (vwin, 1)):
            sv_ = stg[:, 0:nqb * 64].rearrange("p (c d) -> p c d", d=64)
            for rk in range(NR):
                P.dma("sp", out=sv_[:, rk * CPR:(rk + 1) * CPR, :],
                      in_=A["vtok"](bi, rk).rearrange("(c p) d -> p c d", p=128))
            P.op("dve", "memset", ap=vt[:, :, 64:65], constant=1.0)
            P.cp(vt[:, :, 0:64], sv_)
        P.dma("sp", out=stg.rearrange("p (c k) -> p c k", k=128), in_=E_d)
        P.cp(Eall, stg.rearrange("p (c k) -> p c k", k=128))
    NB = 2
    LA = int(os.environ.get('NSA_LA', '2'))
    PSM = os.environ.get('NSA_PSM', '1') == '1'
    mk = lambda n, s, dt=F32: [P.sb("%s%d" % (n, i), s, dt) for i in range(NB)]
    xt_, qT_, gate_ = mk("xt", [128, D]), mk("qT", [64, 512]), mk("gate", [128, 12])
    hT_ = mk("hT", [128, 8, 128], BF16)
    qTb_ = mk("qTb", [64, 512], KD)
    NE = LA + 2
    eT_ = [P.sb("eT%d" % i, [128, 4, 128]) for i in range(NE)]
    pT_ = [P.sb("pT%d" % i, [128, 4, 128]) for i in range(NE)]
    eTb_ = [P.sb("eTb%d" % i, [128, 4, 128], KD) for i in range(NE)]
    pTb_ = [P.sb("pTb%d" % i, [128, 4, 128], KD) for i in range(NE)]
    m_ = [P.sb("m%d" % i, [128, 128]) for i in range(NE)]
    imp_, sc_, tmp_ = mk("imp", [128, 128]), mk("sc", [128, 128]), mk("tmp", [128, 128])
    vals_, thr_, sel_ = mk("vals", [128, 16]), mk("thr", [128, 1]), mk("sel", [128, 128])
    selT_ = mk("selT", [128, 128], KD)
    negsel_ = mk("negsel", [128, 4, 128], KD)
    negsel_cur = [None]
    zc_, gs_, out_ = mk("zc", [128, 4]), mk("gs", [128, 4]), mk("out", [128, 4, 64])
    st_banks = [pb[2], pb[3], pb[0]]
    mk_banks = [pb[4], pb[1], pb[5]]
    ecnt = [0]

    def pipeline(items):
        n = len(items)
        srcs = [None] * n
        for i in range(n + LA):
            if i < n:
                srcs[i] = items[i][0]()
            if i >= LA:
                items[i - LA][1](srcs[i - LA])

    def stage1(q_ap, kT_chunk, bf, mask=None, maskE=None, tri_too=False):
        i = ecnt[0] % NE
        st = st_banks[ecnt[0] % 3]
        ecnt[0] += 1
        fold = maskE is not None and not tri_too and (ecnt[0] % 2 == 1)
        if fold:
            P.mm(st, kT_chunk, q_ap, start=True, stop=False)
            P.mm(st, maskE, negsel_cur[0].rearrange("p r t -> p (r t)"), start=False, stop=True)
            maskE = None
        else:
            P.mm(st, kT_chunk, q_ap)
        eT = (eTb_ if bf else eT_)[i]
        P.act(eT, st.rearrange("p (r t) -> p r t", r=4), AF.Exp)
        if maskE is not None:
            mb = mk_banks[i % 3]
            P.mm(mb[:, 0:128], maskE, selT_cur[0])
            if tri_too:
                P.tt(m_[i], mb[:, 0:128], tri, ALU.mult)
                mask = m_[i]
            elif PSM:
                mask = mb[:, 0:128]
            else:
                P.cp(m_[i], mb[:, 0:128])
                mask = m_[i]
        if mask is None:
            return eT
        pT = (pTb_ if bf else pT_)[i]
        P.tt(pT, eT, mask.unsqueeze(1).to_broadcast([128, 4, 128]), ALU.mult)
        return pT

    def stage2(src, vext_chunk, acc, first, last, extra=None):
        for r in range(4):
            P.mm(acc[:, r * 65:(r + 1) * 65], src[:, r, :], vext_chunk,
                 start=(first and r == 0), stop=(last and r == 3))
        if extra is not None:
            bank, rhs = extra
            for r in range(4):
                P.mm(bank[:, r * 128:(r + 1) * 128], src[:, r, :], rhs,
                     start=(first and r == 0), stop=(last and r == 3))

    selT_cur = [None]
    for qb in range(nqb):
        i = qb % NB
        xt, hT, qT, qTb, gate = xt_[i], hT_[i], qT_[i], qTb_[i], gate_[i]
        imp, sc, tmp, vals, thr, sel, selT = imp_[i], sc_[i], tmp_[i], vals_[i], thr_[i], sel_[i], selT_[i]
        zc, gs, out = zc_[i], gs_[i], out_[i]
        negsel = negsel_[i]
        if "bg" in A:
            A["bg"](qb)
        if qb == 0:
            P.dma("sp", out=xt_[0], in_=A["x_tile"](0))
        emit_hT(P, hT, xt, scT, shT, ident, pb[0], pb[1])
        if qb + 1 < nqb:
            P.dma("sp", out=xt_[(qb + 1) % NB], in_=A["x_tile"](qb + 1))
        for r in range(4):
            for kc in range(8):
                P.mm(pb[0][0:64, r * 128:(r + 1) * 128], wq[:, kc, r * 64:(r + 1) * 64], hT[:, kc, :],
                     start=(kc == 0), stop=(kc == 7))
        P.ts(qT, pb[0][0:64, :], 0.125, ALU.mult)
        P.ts(qTb, pb[0][0:64, :], 0.125, ALU.mult)
        for kc in range(8):
            P.mm(pb[1][:, 0:12], hT[:, kc, :], wgt[:, kc, :], start=(kc == 0), stop=(kc == 7))
        P.act(gate, pb[1][:, 0:12], AF.Sigmoid)
        chunks = []
        for c in range(NCH):
            th = 128 * qb - 31 - 2048 * c
            if th < -127:
                continue
            chunks.append((c, None if th >= 2032 else th))
        items = []
        for k, (c, th) in enumerate(chunks):
            first, last = (k == 0), (k == len(chunks) - 1)

            def s1(c=c, th=th, k=k):
                mask = None
                if th is not None:
                    mask = m_[k % NE]
                    P.ts(mask, Acmp, float(th), ALU.is_le)
                return stage1(qT, kcmpT[:, c * 128:(c + 1) * 128], False, mask=mask)

            def s2(src, c=c, first=first, last=last):
                stage2(src, vcmp[:, c, :], pb[6], first, last, extra=(pb[7], c2s[:, c, :]))
            items.append((s1, s2))
        pipeline(items)
        acc = pb[6][:, 0:260].rearrange("p (r e) -> p r e", e=65)
        P.ts(zc, acc[:, :, 64], 1e-30, ALU.max)
        P.op("dve", "reciprocal", out=zc, in_=zc)
        P.tt(gs, zc, gate.rearrange("p (r b) -> p r b", b=3)[:, :, 0], ALU.mult)
        for r in range(4):
            P.ts(out[:, r, :], acc[:, r, 0:64], gs[:, r:r + 1], ALU.mult, nowaw=True)
        for r in range(4):
            if r == 0:
                P.ts(imp, pb[7][:, 0:128], zc[:, 0:1], ALU.mult)
            else:
                P.op("dve", "scalar_tensor_tensor", out=imp, in0=pb[7][:, r * 128:(r + 1) * 128],
                     scalar=zc[:, r:r + 1], in1=imp, op0=ALU.mult, op1=ALU.add)
        items = []
        wch = list(range(max(0, qb - 4), qb + 1))
        for k, c in enumerate(wch):
            mask = tri if c == qb else (low if c == qb - 4 else None)

            def s1(c=c, mask=mask):
                return stage1(qTb, kwinT[:, c * 128:(c + 1) * 128], True, mask=mask)

            def s2(src, c=c, k=k):
                stage2(src, vwin[:, c, 0:65], pb[7], k == 0, k == len(wch) - 1)
            items.append((s1, s2))
        pipeline(items)
        off = 126 - 2 * qb
        P.tt(sc, imp, availW[:, off:off + 128], ALU.mult)
        P.tt(sc, sc, fbW[:, off:off + 128], ALU.add)
        P.ts(sc[:, 0:1], sc[:, 0:1], 100.0, ALU.add)
        P.op("dve", "max", out=vals[:, 0:8], in_=sc)
        P.op("dve", "match_replace", out=tmp, in_to_replace=vals[:, 0:8], in_values=sc, imm_value=-1e30)
        P.op("dve", "max", out=vals[:, 8:16], in_=tmp)
        P.ts(thr, vals[:, 15:16], 0.0, ALU.max)
        P.ts(sel, sc, thr[:, 0:1], ALU.is_ge)
        P.tr(pb[1][:, 0:128], sel, ident)
        P.cp(selT, pb[1][:, 0:128])
        P.ts(negsel, pb[1][:, 0:128].unsqueeze(1).to_broadcast([128, 4, 128]), 1.0, ALU.subtract, 30000.0, ALU.mult)
        selT_cur[0] = selT
        negsel_cur[0] = negsel
        items = []
        for c in range(qb + 1):
            def s1(c=c):
                return stage1(qTb, kslcT[:, c * 128:(c + 1) * 128], True, maskE=Eall[:, c, :], tri_too=(c == qb))

            def s2(src, c=c):
                stage2(src, vslc[:, c, 0:65], pb[6], c == 0, c == qb)
            items.append((s1, s2))
        pipeline(items)
        for (bank, br) in ((pb[6], 1), (pb[7], 2)):
            acc = bank[:, 0:260].rearrange("p (r e) -> p r e", e=65)
            P.ts(zc, acc[:, :, 64], 1e-30, ALU.max)
            P.op("dve", "reciprocal", out=zc, in_=zc)
            P.tt(gs, zc, gate.rearrange("p (r b) -> p r b", b=3)[:, :, br], ALU.mult)
            for r in range(4):
                P.op("dve", "scalar_tensor_tensor", out=out[:, r, :], in0=acc[:, r, 0:64], scalar=gs[:, r:r + 1],
                     in1=out[:, r, :], op0=ALU.mult, op1=ALU.add, _nowaw=True)
        P.dma("sp", out=A["o_out"](qb), in_=out.rearrange("p r d -> p (r d)"))
        if "after_out" in A:
            A["after_out"](qb)


GROUPS = [[0, 1, 2, 3], [4, 5, 6, 7]]
CC_BYTES = 1 << 20


def build_fused():
    nc = bass.Bass("TRN2", target_bir_lowering=False)
    P = Prog(nc)
    di = lambda n, s, dt=F32: P.dram(n, s, dt, kind="ExternalInput", track=False)
    I = {}
    for n, s in (("x_full", [S, D]), ("xs0", [TOK, D]), ("cT", [128, 8]), ("ada_w", [DEPTH, D, 6 * D]),
                 ("ada_b", [DEPTH, 1, 6 * D]), ("ln_g", [DEPTH, 2, D]), ("ln_b", [DEPTH, 2, D]),
                 ("wqk", [2, D, 512]), ("wv", [2, D, 512]), ("wg", [2, D, 512]), ("ret_w_o", [2, 2048, D]),
                 ("cos", [S, 128]), ("sin", [S, 128]), ("decT", [128, 128]), ("cols", [128, 4]),
                 ("kv_ada_w", [D, 2 * D]), ("kv_ada_b", [1, 2 * D]), ("w_kv", [D, 1536]),
                 ("wq", [2, D, 256]), ("wgate", [2, D, 12]), ("nsa_w_o", [2, D, D]),
                 ("peT", [2, 64, 32]), ("w1", [2, 2048, 256]), ("b1T", [128, 2, 2]), ("w2l", [128, 2, 2, 64]),
                 ("c2s", [128, 4, 128]), ("Eall", [128, 64, 128]), ("tri", [128, 128]), ("low", [128, 128]),
                 ("Acmp", [128, 128]), ("availW", [128, 256]), ("fbW", [128, 256]),
                 ("w_q", [DEPTH, D, 2048]), ("keysT", [DEPTH, 128, 16, 128]), ("uT", [DEPTH, 128, 128, 8, 128]),
                 ("v", [DEPTH, 16384, D]), ("ident", [128, 128]), ("iota", [128, 128])):
        I[n] = di(n, s)
    x_out = P.dram("x_out", [TOK, D], F32, kind="ExternalOutput", track=False)

    pb = [P.ps("pb%d" % i) for i in range(8)]
    ident = P.sb("ident", [128, 128])
    P.dma("sp", out=ident, in_=I["ident"])
    iota = P.sb("iota", [128, 128])
    P.dma("sp", out=iota, in_=I["iota"])
    jr = nc.sync.partition_id() % 4

    x_loc = [P.dram("x_loc%d" % l, [TOK, D]) for l in range(3)]
    x_gat = [P.dram("x_gat%d" % l, [S, D]) for l in range(3)]
    kvT_loc = P.dram("kvT_loc", [16 * 64, TOK])
    kvT_gat = P.dram("kvT_gat", [16 * 4 * 64, TOK])
    vt_loc = P.dram("vt_loc", [8 * TOK, 64])
    vt_gat = P.dram("vt_gat", [8 * 4 * TOK, 64])
    vt_loc4 = vt_loc.rearrange("(w g t) d -> w g t d", w=2, g=4)
    kvT_mine = P.dram("kvT_mine", [4 * 256, TOK])
    vt_mine = P.dram("vt_mine", [2 * S, 64])

    def x_tile_from(l):
        if l == 0:
            return lambda n: I["x_full"][n * 128:(n + 1) * 128, :]

        def f(n):
            rank, r = n // 16, (n % 16) * 128
            ch, off = r // 256, r % 256
            row = (ch * 4 + rank) * 256 + off
            return x_gat[l - 1][row:row + 128, :]
        return f

    for l in range(DEPTH):
        Wd = 512 if l < 2 else 256
        RPC = CC_BYTES // (Wd * 4)
        NCHK = S // RPC
        CPJ = TOK // RPC
        o_loc = P.dram("o_loc%d" % l, [S, Wd])
        o_gat = P.dram("o_gat%d" % l, [NCHK * 4 * RPC, Wd])
        uT_bf = P.dram("uT_bf%d" % l, [64, 128, 2, 1024], BF16)
        v_bf = P.dram("v_bf%d" % l, [64, 128, 2, D], BF16)
        def bg(n, l=l, uT_bf=uT_bf, v_bf=v_bf):
            if P.cnt["pe"] > 0:
                P._wait("pool", ("eng", "pe", P.cnt["pe"]))
            P.dma("pool", out=uT_bf[n], in_=I["uT"][l, 2 * n:2 * n + 2].rearrange("c p k n -> p c (k n)"))
            P.dma("pool", out=v_bf[n], in_=I["v"][l, 2 * n * 128:(2 * n + 2) * 128, :].rearrange("(c p) d -> p c d", p=128))
        def after_out(n, o_loc=o_loc, o_gat=o_gat, RPC=RPC):
            per = RPC // 128
            if (n + 1) % per == 0:
                ch = (n + 1) // per - 1
                P.collective("AllGather", GROUPS, o_loc[ch * RPC:(ch + 1) * RPC, :],
                             o_gat[ch * 4 * RPC:(ch + 1) * 4 * RPC, :])

        base = dict(cT=I["cT"], ada_w=I["ada_w"][l], ada_b=I["ada_b"][l], x_tile=x_tile_from(l), bg=bg,
                    after_out=after_out,
                    o_out=lambda n, o_loc=o_loc: o_loc[n * 128:(n + 1) * 128, :])
        if l < 2:
            A = dict(base, wqk=I["wqk"][l], wv=I["wv"][l], wg=I["wg"][l], cos=I["cos"], sin=I["sin"],
                     decT=I["decT"], cols=I["cols"])
            emit_retmix(P, pb, ident, A)
        else:
            A = dict(base, wq=I["wq"][l - 2], wgate=I["wgate"][l - 2],
                     kvT=lambda w, rk: kvT_mine[w * 256 + rk * 64:w * 256 + (rk + 1) * 64, :],
                     vtok=lambda w, rk: vt_mine[w * S + rk * TOK:w * S + (rk + 1) * TOK, :],
                     **{k: I[k] for k in ("peT", "w1", "b1T", "w2l", "c2s", "Eall", "tri", "low", "Acmp",
                                          "availW", "fbW")})
            emit_nsamix(P, pb, ident, A)

        o_mine = P.dram("o_mine%d" % l, [S, Wd])
        o_gat_v = o_gat.rearrange("(j q) w -> j q w", j=4)
        NSPL = 2
        for sp_ in range(NSPL):
            rs = slice(sp_ * (S // NSPL), (sp_ + 1) * (S // NSPL))
            P.dma("sp", out=o_mine[rs, :].rearrange("(o r) w -> o (r w)", o=1),
                  in_=o_gat_v[bass.ds(jr, 1), rs, :].rearrange("o r w -> o (r w)"))

        def o_pieces(tile, o_mine=o_mine, RPC=RPC, Wd=Wd):
            r = tile * 128
            sc, off = r // RPC, r % RPC
            return [(o_mine[(sc * 4 + h) * RPC + off:(sc * 4 + h) * RPC + off + 128, :], h * Wd, Wd)
                    for h in range(4)]

        rows = lambda t: slice(t * 128, (t + 1) * 128)
        A = dict(cT=I["cT"], ada_w=I["ada_w"][l], ada_b=I["ada_b"][l], ln_g=I["ln_g"][l], ln_b=I["ln_b"][l],
                 w_o=(I["ret_w_o"][l] if l < 2 else I["nsa_w_o"][l - 2]), w_q=I["w_q"][l], keysT=I["keysT"][l],
                 uT_pair=lambda q, uT_bf=uT_bf: uT_bf[q], v_pair=lambda q, v_bf=v_bf: v_bf[q], tab_q="sp",
                 o_pieces=o_pieces,
                 x_tile=(lambda t: I["xs0"][rows(t), :]) if l == 0 else (lambda t, xl=x_loc[l - 1]: xl[rows(t), :]),
                 x_out=(lambda t: [x_out[rows(t), :]]) if l == DEPTH - 1 else (lambda t, xl=x_loc[l]: [xl[rows(t), :]]))
        if l == 1:
            A.update(kv_ada_w=I["kv_ada_w"], kv_ada_b=I["kv_ada_b"], w_kv=I["w_kv"],
                     kvT_out=lambda w, g: kvT_loc[(w * 4 + g) * 64:(w * 4 + g + 1) * 64, :],
                     vtok_out=lambda w: vt_loc4[w])
        if l < DEPTH - 1:
            A["after_group"] = lambda ch, l=l: P.collective("AllGather", GROUPS, x_loc[l][ch * 256:(ch + 1) * 256, :],
                                                            x_gat[l][ch * 1024:(ch + 1) * 1024, :])
        emit_tok(P, pb, ident, iota, A, 4 * Wd, l == 1)
        if l == 1:
            for wg_ in range(16):
                P.collective("AllGather", GROUPS, kvT_loc[wg_ * 64:(wg_ + 1) * 64, :],
                             kvT_gat[wg_ * 256:(wg_ + 1) * 256, :])
            for wg_ in range(8):
                P.collective("AllGather", GROUPS, vt_loc[wg_ * TOK:(wg_ + 1) * TOK, :],
                             vt_gat[wg_ * 4 * TOK:(wg_ + 1) * 4 * TOK, :])
            kg = kvT_gat.rearrange("(w g r) t -> w g r t", w=4, g=4)
            P.dma("sp", out=kvT_mine.rearrange("(w r) t -> w (r t)", w=4),
                  in_=kg[:, bass.ds(jr, 1), :, :].rearrange("w o r t -> w (o r t)"))
            vg = vt_gat.rearrange("(w g r) d -> w g r d", w=2, g=4)
            P.dma("sp", out=vt_mine.rearrange("(w r) d -> w (r d)", w=2),
                  in_=vg[:, bass.ds(jr, 1), :, :].rearrange("w o r d -> w (o r d)"))
    P.finish()
    return nc, P


_CONST = {}


def _consts():
    if _CONST:
        return _CONST
    a = np.arange(128)
    c = _CONST
    c["ident"] = np.eye(128, dtype=np.float32)
    c["iota"] = np.tile(np.arange(128, dtype=np.float32)[None, :], (128, 1))
    import jax
    import jax.numpy as jnp
    with jax.default_device(jax.devices("cpu")[0]):
        pos = jnp.arange(S, dtype=jnp.float32)
        theta = 1.0 / (10000.0 ** jnp.linspace(0.0, 1.0, 128, dtype=jnp.float32))
        ang = pos[:, None] * theta[None, :]
        c["cos"] = np.asarray(jnp.cos(ang))
        c["sin"] = np.asarray(jnp.sin(ang))
    idx = np.arange(128, dtype=np.float64)
    for h in range(4):
        lg = np.log1p(-np.exp2(-5.0 - h))
        qdec = np.exp((idx + 1) * lg)
        kdec = np.exp((127 - idx) * lg) / 16.0
        cdec = np.exp(128 * lg)
        decT = np.where(idx[:, None] <= idx[None, :], np.exp(-(idx[:, None] + 1) * lg), 0.0)
        c["decT%d" % h] = decT.astype(np.float32)
        c["cols%d" % h] = np.stack([qdec, kdec, np.full(128, cdec), np.zeros(128)], axis=1).astype(np.float32)
    nqb = S // 128
    NCP = S // 16
    i = np.arange(NCP)[:, None] * 16
    j = np.arange(128)[None, :] * 64
    ov = np.minimum(i + 32, j + 64) - np.maximum(i, j)
    c2s = (np.clip(ov, 0, None) / 32.0).astype(np.float32)
    c2s[NCP - 1:] = 0.0
    c["c2s"] = np.ascontiguousarray(c2s.reshape(NCP // 128, 128, 128).transpose(1, 0, 2))
    jj = np.arange(128)[:, None, None]
    cc = np.arange(nqb)[None, :, None]
    kk = np.arange(128)[None, None, :]
    c["Eall"] = (jj == 2 * cc + kk // 64).astype(np.float32)
    c["tri"] = (a[:, None] <= a[None, :]).astype(np.float32)
    c["low"] = (a[:, None] > a[None, :]).astype(np.float32)
    c["Acmp"] = (16.0 * a[:, None] - a[None, :]).astype(np.float32)
    tl = a[:, None]
    jrel = np.arange(256)[None, :] - 126
    cur = (tl >= 64).astype(np.int64)
    c["availW"] = (jrel <= cur).astype(np.float32)
    c["fbW"] = np.where((jrel == cur) | (jrel == cur - 1), 100.0, np.where(jrel > cur, -1.0, 0.0)).astype(np.float32)
    return c


_PROGS = {}


def _ca(a):
    return np.ascontiguousarray(a, dtype=np.float32)


def kernel(x, c, ada_w, ada_b, ln_g, ln_b, ret_w_in, ret_w_o, kv_ada_w, kv_ada_b, nsa_w_kv,
           cmp_pe, cmp_w1, cmp_b1, cmp_w2, nsa_w_in, nsa_w_o, peer_w_q, peer_keys, peer_u, peer_v):
    f = lambda a: np.asarray(a, dtype=np.float32)
    x, c, ada_w, ada_b, ln_g, ln_b = f(x), f(c), f(ada_w), f(ada_b), f(ln_g), f(ln_b)
    ret_w_in, ret_w_o, kv_ada_w, kv_ada_b, nsa_w_kv = f(ret_w_in), f(ret_w_o), f(kv_ada_w), f(kv_ada_b), f(nsa_w_kv)
    cmp_pe, cmp_w1, cmp_b1, cmp_w2 = f(cmp_pe), f(cmp_w1), f(cmp_b1), f(cmp_w2)
    nsa_w_in, nsa_w_o, peer_w_q, peer_keys, peer_u, peer_v = (f(nsa_w_in), f(nsa_w_o), f(peer_w_q), f(peer_keys),
                                                              f(peer_u), f(peer_v))
    K = _consts()
    shared = dict(
        ada_w=ada_w, ada_b=_ca(ada_b[:, None, :]), ln_g=ln_g, ln_b=ln_b, ret_w_o=ret_w_o,
        cos=K["cos"], sin=K["sin"], kv_ada_w=kv_ada_w, kv_ada_b=_ca(kv_ada_b[None, :]), w_kv=nsa_w_kv,
        nsa_w_o=nsa_w_o, peT=_ca(cmp_pe.transpose(0, 2, 1)), w1=cmp_w1,
        b1T=_ca(cmp_b1.reshape(2, 2, 128).transpose(2, 0, 1)),
        w2l=_ca(cmp_w2.reshape(2, 2, 128, 64).transpose(2, 0, 1, 3)),
        c2s=K["c2s"], Eall=K["Eall"], tri=K["tri"], low=K["low"], Acmp=K["Acmp"], availW=K["availW"], fbW=K["fbW"],
        w_q=peer_w_q, keysT=_ca(peer_keys.reshape(DEPTH, 16, 128, 128).transpose(0, 3, 1, 2)),
        uT=_ca(peer_u.reshape(DEPTH, 128, 128, 8, 128).transpose(0, 1, 4, 3, 2)), v=peer_v,
        ident=K["ident"], iota=K["iota"])
    maps = []
    for i in range(NCORES):
        b, j = divmod(i, 4)
        d = dict(shared)
        d.update(
            x_full=_ca(x[b]), xs0=_ca(x[b, j * TOK:(j + 1) * TOK]), cT=_ca(c[b].reshape(8, 128).T),
            wqk=_ca(np.concatenate([ret_w_in[:, :, j * 256:(j + 1) * 256],
                                    ret_w_in[:, :, 1024 + j * 256:1024 + (j + 1) * 256]], axis=2)),
            wv=_ca(ret_w_in[:, :, 2048 + j * 512:2048 + (j + 1) * 512]),
            wg=_ca(ret_w_in[:, :, 4096 + j * 512:4096 + (j + 1) * 512]),
            decT=K["decT%d" % j], cols=K["cols%d" % j],
            wq=_ca(nsa_w_in[:, :, j * 256:(j + 1) * 256]),
            wgate=_ca(nsa_w_in[:, :, 1024 + j * 12:1024 + (j + 1) * 12]))
        maps.append(d)
    if "fused" not in _PROGS:
        _PROGS["fused"] = build_fused()[0]
    res = run_bass_kernel_spmd(_PROGS["fused"], maps, core_ids=list(range(NCORES))).results
    out = np.stack([np.concatenate([res[4 * b + j]["x_out"] for j in range(4)], axis=0) for b in range(B)])
    return out.astype(np.float32)
```

```python
from contextlib import contextmanager, ExitStack
import numpy as np
import concourse.bass as bass
import concourse.mybir as mybir
from concourse.bass_utils import run_bass_kernel_spmd

F32 = mybir.dt.float32
BF16 = mybir.dt.bfloat16
I32 = mybir.dt.int32
U32 = mybir.dt.uint32
ALU = mybir.AluOpType
AF = mybir.ActivationFunctionType
AX = mybir.AxisListType

D = 1024
B = 2
S = 8192
DEPTH = 4
ALPHA = (2.0 * DEPTH) ** 0.25
LN_EPS = 1e-5
NCORES = 8
TOK = 2048
GT = 256

WRITE_KW = ("out", "accum_out", "out_max", "out_indices", "ap")


class _Buf:
    __slots__ = ("w", "r", "ws")

    def __init__(self):
        self.w = None
        self.r = {}
        self.ws = None


class Prog:
    ENGS = ("pe", "dve", "act", "pool", "sp")

    def __init__(self, nc, n_dma_sems=48):
        self.nc = nc
        self.eng = dict(pe=nc.tensor, dve=nc.vector, act=nc.scalar, pool=nc.gpsimd, sp=nc.sync)
        self.sem = {e: nc.alloc_semaphore("sem_" + e) for e in self.ENGS}
        self.cnt = {e: 0 for e in self.ENGS}
        self.seen = {}
        self.dsem = [nc.alloc_semaphore("dsem%d" % i) for i in range(n_dma_sems)]
        self.dcnt = [0] * n_dma_sems
        self.drr = 0
        self.dseen = {}
        self.bufs = {}
        self.untracked = set()
        self.multi = set()
        self.csem = [nc.alloc_semaphore("csem%d" % i) for i in range(4)]
        self.ccnt = [0] * 4
        self.crr = 0
        self.n_inst = 0
        self.uid = 0
        self.stacks = [ExitStack()]

    def sb(self, name, shape, dtype=F32):
        self.uid += 1
        t = self.stacks[-1].enter_context(
            self.nc.sbuf_tensor("%s_%d" % (name, self.uid), list(shape), dtype))
        return t.ap()

    def ps(self, name, shape=(128, 512), dtype=F32):
        self.uid += 1
        t = self.stacks[-1].enter_context(
            self.nc.psum_tensor("%s_%d" % (name, self.uid), list(shape), dtype))
        return t.ap()

    @contextmanager
    def scope(self):
        st = ExitStack()
        self.stacks.append(st)
        try:
            yield
        finally:
            self.barrier()
            self.stacks.pop()
            st.close()

    def dram(self, name, shape, dtype=F32, kind="Internal", track=True):
        if kind == "Internal":
            t = self.nc.dram_tensor(name, list(shape), dtype)
        else:
            t = self.nc.dram_tensor(name, list(shape), dtype, kind=kind)
        ap = t.ap()
        if not track:
            self.untracked.add(ap.tensor.name)
        elif kind == "Internal":
            self.multi.add(ap.tensor.name)
        return ap

    def _buf(self, ap):
        n = ap.tensor.name
        if n in self.untracked:
            return None
        b = self.bufs.get(n)
        if b is None:
            b = self.bufs[n] = _Buf()
            if n in self.multi:
                b.ws = []
        return b

    def _wait(self, e, ev):
        if ev[0] == "cc":
            _, si, v = ev
            if self.dseen.get((e, "c", si), 0) >= v:
                return
            self.dseen[(e, "c", si)] = v
            self.eng[e].wait_ge(self.csem[si], v)
            return
        if ev[0] == "eng":
            _, pe, n = ev
            if e == "pe" and pe == "pe":
                return
            if self.seen.get((e, pe), 0) >= n:
                return
            self.seen[(e, pe)] = n
            self.eng[e].wait_ge(self.sem[pe], n)
        else:
            _, si, v = ev
            if self.dseen.get((e, si), 0) >= v:
                return
            self.dseen[(e, si)] = v
            self.eng[e].wait_ge(self.dsem[si], v)

    def _deps(self, e, reads, writes):
        evs = []
        for ap in reads:
            b = self._buf(ap)
            if b is None:
                continue
            if b.ws is not None:
                evs.extend(b.ws)
            elif b.w is not None:
                evs.append(b.w)
        for ap in writes:
            b = self._buf(ap)
            if b is None:
                continue
            if b.ws is None and b.w is not None:
                evs.append(b.w)
            evs.extend(b.r.values())
        for ev in evs:
            self._wait(e, ev)

    def _record(self, ev, key, reads, writes):
        for ap in reads:
            b = self._buf(ap)
            if b is not None:
                b.r[key] = ev
        for ap in writes:
            b = self._buf(ap)
            if b is not None:
                if b.ws is not None:
                    b.ws.append(ev)
                else:
                    b.w = ev
                b.r = {}

    def collective(self, kind, groups, in_ap, out_ap):
        self._deps("pool", [in_ap], [out_ap])
        si = self.crr
        self.crr = (self.crr + 1) % len(self.csem)
        if self.ccnt[si] > 0:
            self._wait("pool", ("cc", si, self.ccnt[si]))
        ins = self.nc.gpsimd.collective_compute(kind, ALU.bypass, replica_groups=groups,
                                                ins=[in_ap.opt()], outs=[out_ap.opt()])
        self.ccnt[si] += 1
        ins.then_inc(self.csem[si], 1)
        ev = ("cc", si, self.ccnt[si])
        self._record(ev, ("c", si), [in_ap], [out_ap])
        self.n_inst += 1
        return ev

    @staticmethod
    def _split(args, kw):
        reads, writes = [], []
        for k, v in kw.items():
            if isinstance(v, bass.AP):
                (writes if k in WRITE_KW else reads).append(v)
        for v in args:
            if isinstance(v, bass.AP):
                reads.append(v)
        return reads, writes

    def op(self, e, method, *args, **kw):
        nowaw = kw.pop("_nowaw", False)
        reads, writes = self._split(args, kw)
        if nowaw:
            evs = []
            for ap in reads:
                b = self._buf(ap)
                if b is None:
                    continue
                if b.ws is not None:
                    evs.extend(b.ws)
                elif b.w is not None:
                    evs.append(b.w)
            for ap in writes:
                b = self._buf(ap)
                if b is None:
                    continue
                if b.ws is None and b.w is not None and not (b.w[0] == "eng" and b.w[1] == e):
                    evs.append(b.w)
                for ev in b.r.values():
                    if not (ev[0] == "eng" and ev[1] == e):
                        evs.append(ev)
            for ev in evs:
                self._wait(e, ev)
        else:
            self._deps(e, reads, writes)
        ins = getattr(self.eng[e], method)(*args, **kw)
        self.cnt[e] += 1
        ins.then_inc(self.sem[e], 1)
        ev = ("eng", e, self.cnt[e])
        self._record(ev, e, reads, writes)
        self.n_inst += 1
        return ev

    def dma(self, q, out, in_, **kw):
        reads, writes = [in_], [out]
        self._deps(q, reads, writes)
        si = self.drr
        self.drr = (self.drr + 1) % len(self.dsem)
        if self.dcnt[si] > 0:
            self._wait(q, ("dma", si, self.dcnt[si]))
        ins = self.eng[q].dma_start(out=out, in_=in_, **kw)
        self.dcnt[si] += 16
        ins.then_inc(self.dsem[si], 16)
        ev = ("dma", si, self.dcnt[si])
        self._record(ev, ("d", si), reads, writes)
        self.n_inst += 1
        return ev

    def barrier(self, engs=None, final=False):
        sp = "sp"
        for si, v in enumerate(self.dcnt):
            if v > 0 and not any(self.dseen.get((e, si), 0) >= v for e in self.ENGS):
                self._wait(sp, ("dma", si, v))
        for si, v in enumerate(self.ccnt):
            if final and v > 0 and not any(self.dseen.get((e, "c", si), 0) >= v for e in self.ENGS):
                self._wait(sp, ("cc", si, v))
        for pe in self.ENGS:
            if pe != sp and self.cnt[pe] > 0:
                self._wait(sp, ("eng", pe, self.cnt[pe]))
        ins = self.eng[sp].nop()
        self.cnt[sp] += 1
        ins.then_inc(self.sem[sp], 1)
        for e in self.ENGS:
            if e != sp:
                self._wait(e, ("eng", sp, self.cnt[sp]))
        for e in self.ENGS:
            for pe in self.ENGS:
                self.seen[(e, pe)] = self.cnt[pe]
            for si, v in enumerate(self.dcnt):
                self.dseen[(e, si)] = v
            if final:
                for si, v in enumerate(self.ccnt):
                    self.dseen[(e, "c", si)] = v
        for b in self.bufs.values():
            if b.w is not None and b.w[0] != "cc":
                b.w = None
            b.r = {k: ev for k, ev in b.r.items() if ev[0] == "cc"}
            if b.ws is not None:
                b.ws = [ev for ev in b.ws if ev[0] == "cc"]

    def finish(self):
        self.barrier(final=True)

    def mm(self, out, lhsT, rhs, start=True, stop=True):
        return self.op("pe", "matmul", out=out, lhsT=lhsT, rhs=rhs, start=start, stop=stop)

    def tr(self, out, in_, ident):
        return self.op("pe", "transpose", out=out, in_=in_, identity=ident)

    def tt(self, out, in0, in1, op, e="dve", nowaw=False):
        return self.op(e, "tensor_tensor", out=out, in0=in0, in1=in1, op=op, _nowaw=nowaw)

    def ts(self, out, in0, s1, op0, s2=None, op1=None, e="dve", nowaw=False):
        if op1 is None:
            return self.op(e, "tensor_scalar", out=out, in0=in0, scalar1=s1, scalar2=None, op0=op0, _nowaw=nowaw)
        return self.op(e, "tensor_scalar", out=out, in0=in0, scalar1=s1, scalar2=s2, op0=op0, op1=op1,
                       _nowaw=nowaw)

    def cp(self, out, in_, e="dve", nowaw=False):
        return self.op(e, "tensor_copy", out=out, in_=in_, _nowaw=nowaw)

    def act(self, out, in_, func, nowaw=False, **kw):
        return self.op("act", "activation", out=out, in_=in_, func=func, _nowaw=nowaw, **kw)


def emit_mod(P, cT_d, w_d, b_d, ident, col0, ncols, pbank, pbank2, mod_bc, modT=None):
    with P.scope():
        cT = P.sb("cT", [128, 8])
        P.dma("sp", out=cT, in_=cT_d)
        ca = P.sb("ca", [128, 8])
        P.act(ca, cT, AF.Silu)
        crep = P.sb("crep", [128, 8, 128])
        for kc in range(8):
            P.cp(crep[:, kc, :], ca[:, kc:kc + 1].to_broadcast([128, 128]))
        wv = w_d.rearrange("(kc p) n -> p kc n", p=128)
        awb = [P.sb("aw%d" % i, [128, 8, 512]) for i in range(2)]
        bbb = [P.sb("bb%d" % i, [128, 512]) for i in range(2)]
        pbs = [pbank, pbank2]
        for ci in range(ncols // 512):
            n0 = col0 + ci * 512
            aw = awb[ci % 2]
            bb = bbb[ci % 2]
            pb = pbs[ci % 2]
            P.dma("sp", out=aw, in_=wv[:, :, n0:n0 + 512])
            P.dma("sp", out=bb, in_=b_d[0:1, n0:n0 + 512].to_broadcast([128, 512]))
            for kc in range(8):
                P.mm(pb, crep[:, kc, :], aw[:, kc, :], start=(kc == 0), stop=(kc == 7))
            P.tt(mod_bc[:, ci * 512:(ci + 1) * 512], pb, bb, ALU.add)
        if modT is not None:
            for j in range(ncols // 128):
                pb = pbs[j % 2]
                P.tr(pb[:, 0:128], mod_bc[:, j * 128:(j + 1) * 128], ident)
                P.cp(modT[:, j:j + 1], pb[:, 0:1])


def emit_ln(P, out, pre, g_bc, b_bc, scr):
    st, mv, rs = scr
    P.op("dve", "bn_stats", out=st[:, 0:6], in_=pre[:, 0:512])
    P.op("dve", "bn_stats", out=st[:, 6:12], in_=pre[:, 512:1024])
    P.op("dve", "bn_aggr", out=mv, in_=st)
    P.ts(rs, mv[:, 1:2], LN_EPS, ALU.add)
    P.act(rs, rs, AF.Sqrt)
    P.op("dve", "reciprocal", out=rs, in_=rs)
    P.ts(out, pre, mv[:, 0:1], ALU.subtract, rs[:, 0:1], ALU.mult)
    P.tt(out, out, g_bc, ALU.mult)
    P.tt(out, out, b_bc, ALU.add)


def emit_top16(P, probs):
    for (vals, idxs, src, tmp) in probs:
        P.op("dve", "max", out=vals[:, 0:8], in_=src, _nowaw=True)
    for (vals, idxs, src, tmp) in probs:
        P.op("dve", "max_index", out=idxs[:, 0:8], in_max=vals[:, 0:8], in_values=src, _nowaw=True)
    for (vals, idxs, src, tmp) in probs:
        P.op("dve", "match_replace", out=tmp, in_to_replace=vals[:, 0:8], in_values=src, imm_value=-1e30,
             _nowaw=True)
    for (vals, idxs, src, tmp) in probs:
        P.op("dve", "max", out=vals[:, 8:16], in_=tmp, _nowaw=True)
    for (vals, idxs, src, tmp) in probs:
        P.op("dve", "max_index", out=idxs[:, 8:16], in_max=vals[:, 8:16], in_values=tmp, _nowaw=True)


def emit_tok(P, pb, ident, iota, A, KO, with_kv, ngroups=TOK // GT):
  with P.scope():
    KC = KO // 128
    cT_d, aw_d, ab_d, lng_d, lnb_d = A["cT"], A["ada_w"], A["ada_b"], A["ln_g"], A["ln_b"]
    wo_d, wq_d, keysT_d = A["w_o"], A["w_q"], A["keysT"]
    if with_kv:
        kvw_d, kvb_d, wkv_d = A["kv_ada_w"], A["kv_ada_b"], A["w_kv"]
    keysT = P.sb("keysT", [128, 16, 128])
    P.dma("sp", out=keysT, in_=keysT_d)
    lng = [P.sb("lng%d" % i, [128, D]) for i in range(2)]
    lnb = [P.sb("lnb%d" % i, [128, D]) for i in range(2)]
    for i in range(2):
        P.dma("sp", out=lng[i], in_=lng_d[i:i + 1, :].to_broadcast([128, D]))
        P.dma("sp", out=lnb[i], in_=lnb_d[i:i + 1, :].to_broadcast([128, D]))
    mod_bc = P.sb("mod_bc", [128, 4096])
    modT = P.sb("modT", [128, 32])
    emit_mod(P, cT_d, aw_d, ab_d, ident, 2048, 4096, pb[0], pb[1], mod_bc, modT)
    g1_bc = mod_bc[:, 0:1024]
    g2_bc = mod_bc[:, 3072:4096]
    sh2T = modT[:, 8:16]
    sc2T = P.sb("sc2T", [128, 8])
    P.ts(sc2T, modT[:, 16:24], 1.0, ALU.add)
    if with_kv:
        kvm_bc = P.sb("kvm_bc", [128, 2048])
        kvmT = P.sb("kvmT", [128, 16])
        emit_mod(P, cT_d, kvw_d, kvb_d, ident, 0, 2048, pb[0], pb[1], kvm_bc, kvmT)
        kvshT = kvmT[:, 0:8]
        kvscT = P.sb("kvscT", [128, 8])
        P.ts(kvscT, kvmT[:, 8:16], 1.0, ALU.add)

    lnscr = (P.sb("lnst", [128, 12]), P.sb("lnmv", [128, 2]), P.sb("lnrs", [128, 1]))
    x1 = [P.sb("x1_%d" % i, [128, D]) for i in range(2)]
    h2T = P.sb("h2T", [128, 8, GT])
    h2bf = P.sb("h2bf", [128, 8, GT], BF16)
    i1T = P.sb("i1T", [128, GT], BF16)
    i2T = P.sb("i2T", [128, GT], BF16)
    wT = P.sb("wT", [128, GT], BF16)
    iota_bf = P.sb("iota_bf", [128, 128], BF16)
    P.cp(iota_bf, iota)

    P.uid += 1
    wo_bf = P.dram("wo_bf_%d" % P.uid, [KO, D], BF16)
    wq_bf = P.dram("wq_bf_%d" % P.uid, [D, 2048], BF16)
    for kb in range(KC // 4):
        P.dma("pool", out=wo_bf[kb * 512:(kb + 1) * 512, :], in_=wo_d[kb * 512:(kb + 1) * 512, :])
    for kb in range(4):
        P.dma("pool", out=wq_bf[kb * 256:(kb + 1) * 256, :], in_=wq_d[kb * 256:(kb + 1) * 256, :])
    wo_v = wo_bf.rearrange("(kc p) n -> p kc n", p=128)
    wq_v = wq_bf.rearrange("(kc p) n -> p kc n", p=128)
    xt = [P.sb("xt%d" % i, [128, D]) for i in range(2)]
    ot = [P.sb("ot%d" % i, [128, KO]) for i in range(2)]
    wob0 = P.sb("wob0", [128, 4, D], BF16)

    def prefetch(g):
        for tt in range(2):
            P.dma("sp", out=xt[tt], in_=A["x_tile"](g * 2 + tt))
            for (oap, c0, wd) in A["o_pieces"](g * 2 + tt):
                P.dma("sp", out=ot[tt][:, c0:c0 + wd], in_=oap)
        P.dma("sp", out=wob0, in_=wo_v[:, 0:4, :])

    prefetch(0)

    for grp in range(ngroups):
        t0 = grp * GT
        with P.scope():
            oT = P.sb("oT", [128, KC, GT], BF16)
            wob = [P.sb("wob%d" % i, [128, 4, D], BF16) for i in range(2)]
            pre = P.sb("pre", [128, D])
            for tt in range(2):
                for k4 in range(KC // 4):
                    bank = pb[4 + (k4 % 2)]
                    for q in range(4):
                        kc = k4 * 4 + q
                        P.tr(bank[:, q * 128:(q + 1) * 128], ot[tt][:, kc * 128:(kc + 1) * 128], ident)
                    P.cp(oT[:, k4 * 4:(k4 + 1) * 4, tt * 128:(tt + 1) * 128],
                         bank.rearrange("p (q t) -> p q t", q=4), nowaw=True)
            for kb in range(KC // 4):
                if kb == 0:
                    wb = wob0
                else:
                    wb = wob[kb % 2]
                    P.dma("sp", out=wb, in_=wo_v[:, kb * 4:(kb + 1) * 4, :])
                for tt in range(2):
                    for half in range(2):
                        for q in range(4):
                            kc = kb * 4 + q
                            P.mm(pb[tt * 2 + half], oT[:, kc, tt * 128:(tt + 1) * 128],
                                 wb[:, q, half * 512:(half + 1) * 512], start=(kc == 0), stop=(kc == KC - 1))
            for tt in range(2):
                for half in range(2):
                    sl = slice(half * 512, (half + 1) * 512)
                    P.tt(pre[:, sl], pb[tt * 2 + half], g1_bc[:, sl], ALU.mult)
                P.op("dve", "scalar_tensor_tensor", out=pre, in0=xt[tt], scalar=ALPHA, in1=pre,
                     op0=ALU.mult, op1=ALU.add)
                emit_ln(P, x1[tt], pre, lng[0], lnb[0], lnscr)
        for tt in range(2):
            for k4 in range(2):
                bank = pb[4 + k4]
                for q in range(4):
                    kc = k4 * 4 + q
                    P.tr(bank[:, q * 128:(q + 1) * 128], x1[tt][:, kc * 128:(kc + 1) * 128], ident)
                for q in range(4):
                    kc = k4 * 4 + q
                    P.ts(h2T[:, kc, tt * 128:(tt + 1) * 128], bank[:, q * 128:(q + 1) * 128],
                         sc2T[:, kc:kc + 1], ALU.mult, sh2T[:, kc:kc + 1], ALU.add, nowaw=True)
        P.cp(h2bf, h2T)
        with P.scope():
            wqb = [P.sb("wqb%d" % i, [128, 8, 128], BF16) for i in range(3)]
            qT = P.sb("qT", [128, 16, GT])
            for g in range(16):
                wb = wqb[g % 3]
                P.dma("sp", out=wb, in_=wq_v[:, :, g * 128:(g + 1) * 128])
                bank = pb[4 + g % 2]
                for kc in range(8):
                    P.mm(bank[:, 0:GT], wb[:, kc, :], h2bf[:, kc, :], start=(kc == 0), stop=(kc == 7))
                if g % 2 == 0:
                    P.cp(qT[:, g, :], bank[:, 0:GT])
                else:
                    P.act(qT[:, g, :], bank[:, 0:GT], AF.Copy)
            s_sb = P.sb("s_sb", [128, 16, 128])
            tmp = P.sb("tk_tmp", [128, 16, 128])
            tmpc = P.sb("tk_tmpc", [128, 8, 256])
            sv = P.sb("sv", [128, 16, 16])
            si = P.sb("si", [128, 16, 16], U32)
            sif = P.sb("sif", [128, 16, 16])
            comb = P.sb("comb", [128, 8, 256])
            cv = P.sb("cv", [128, 8, 16])
            ci = P.sb("ci", [128, 8, 16], U32)
            chi = P.sb("chi", [128, 8, 16], U32)
            clo = P.sb("clo", [128, 8, 16], U32)
            chif = P.sb("chif", [128, 8, 16])
            clof = P.sb("clof", [128, 8, 16])
            ee = P.sb("ee", [128, 8, 16])
            zz = P.sb("zz", [128, 8])
            eq = P.sb("eq", [128, 8, 16, 16])
            i1f = P.sb("i1f", [128, 8, 16])
            i2f = P.sb("i2f", [128, 8, 16])
            ww = P.sb("ww", [128, 8, 16])
            for tt in range(2):
                tsl = slice(tt * 128, (tt + 1) * 128)
                for g4 in range(4):
                    bank = pb[g4]
                    for q in range(4):
                        g = g4 * 4 + q
                        P.mm(bank[:, q * 128:(q + 1) * 128], qT[:, g, tsl], keysT[:, g, :])
                    P.cp(s_sb[:, g4 * 4:(g4 + 1) * 4, :], bank.rearrange("p (q n) -> p q n", q=4), nowaw=True)
                emit_top16(P, [(sv[:, g, :], si[:, g, :], s_sb[:, g, :], tmp[:, g, :]) for g in range(16)])
                P.cp(sif, si)
                svv = sv.rearrange("p (h c) k -> p h c k", c=2)
                sfv = sif.rearrange("p (h c) k -> p h c k", c=2)
                c4 = comb.rearrange("p h (i j) -> p h i j", j=16)
                P.tt(c4, svv[:, :, 0, :].unsqueeze(3).to_broadcast([128, 8, 16, 16]),
                     svv[:, :, 1, :].unsqueeze(2).to_broadcast([128, 8, 16, 16]), ALU.add)
                emit_top16(P, [(cv[:, h, :], ci[:, h, :], comb[:, h, :], tmpc[:, h, :]) for h in range(8)])
                P.tt(ee, cv, cv[:, :, 0:1].to_broadcast([128, 8, 16]), ALU.subtract)
                P.act(ee, ee, AF.Exp)
                P.op("dve", "tensor_reduce", out=zz, in_=ee, axis=AX.X, op=ALU.add)
                P.op("dve", "reciprocal", out=zz, in_=zz)
                P.tt(ww, ee, zz.unsqueeze(2).to_broadcast([128, 8, 16]), ALU.mult)
                P.ts(chi, ci, 4, ALU.logical_shift_right)
                P.ts(clo, ci, 15, ALU.bitwise_and)
                P.cp(chif, chi)
                P.cp(clof, clo)
                io16 = iota[:, 0:16].unsqueeze(1).unsqueeze(1).to_broadcast([128, 8, 16, 16])
                for (cf, cc, dst) in ((chif, 0, i1f), (clof, 1, i2f)):
                    P.tt(eq, cf.unsqueeze(3).to_broadcast([128, 8, 16, 16]), io16, ALU.is_equal)
                    P.tt(eq, eq, sfv[:, :, cc, :].unsqueeze(2).to_broadcast([128, 8, 16, 16]), ALU.mult)
                    P.op("dve", "tensor_reduce", out=dst, in_=eq, axis=AX.X, op=ALU.add)
                for (src, dstT, bank) in ((i1f, i1T, pb[4]), (i2f, i2T, pb[5]), (ww, wT, pb[6])):
                    P.tr(bank[:, 0:128], src.rearrange("p h k -> p (h k)"), ident)
                    P.cp(dstT[:, tsl], bank[:, 0:128])
        wt_scope = P.scope()
        wt_scope.__enter__()
        Wt = P.sb("Wt", [128, GT, 128], BF16)
        with P.scope():
            SBK = 32
            d1 = [P.sb("d1_%d" % i, [128, SBK, 128], BF16) for i in range(2)]
            d2 = [P.sb("d2_%d" % i, [128, SBK, 128], BF16) for i in range(2)]
            for sbk in range(GT // SBK):
                a1 = d1[sbk % 2]
                a2 = d2[sbk % 2]
                ts0 = sbk * SBK
                for t in range(SBK):
                    P.op("dve", "tensor_scalar", out=a1[:, t, :], in0=iota_bf, scalar1=i1T[:, ts0 + t:ts0 + t + 1],
                         scalar2=wT[:, ts0 + t:ts0 + t + 1], op0=ALU.is_equal, op1=ALU.mult, _nowaw=True)
                    P.op("dve", "tensor_scalar", out=a2[:, t, :], in0=iota_bf, scalar1=i2T[:, ts0 + t:ts0 + t + 1],
                         scalar2=None, op0=ALU.is_equal, _nowaw=True)
                for t4 in range(SBK // 4):
                    bank = pb[4 + (t4 % 4)]
                    for q in range(4):
                        t = t4 * 4 + q
                        P.mm(bank[:, q * 128:(q + 1) * 128], a2[:, t, :], a1[:, t, :])
                    tg = ts0 + t4 * 4
                    dst = Wt[:, tg:tg + 4, :]
                    src = bank.rearrange("p (t n) -> p t n", t=4)
                    P.act(dst, src, AF.Copy, nowaw=True)
        with P.scope():
            NBP = 3
            LA3 = 2
            utb = [P.sb("utb%d" % i, [128, 2, 8, 128], BF16) for i in range(NBP)]
            vtb = [P.sb("vtb%d" % i, [128, 2, D], BF16) for i in range(NBP)]
            gl = [P.sb("gl%d" % i, [128, GT]) for i in range(4)]
            cf = [P.sb("cf%d" % i, [128, GT], BF16) for i in range(4)]

            def c3_s1(n1):
                q_, c_ = n1 // 2, n1 % 2
                if c_ == 0:
                    P.dma(A.get("tab_q", "pool"), out=utb[q_ % NBP].rearrange("p c k n -> p c (k n)"),
                          in_=A["uT_pair"](q_))
                    P.dma(A.get("tab_q", "pool"), out=vtb[q_ % NBP], in_=A["v_pair"](q_))
                ub = utb[q_ % NBP][:, c_]
                pa = pb[4 + n1 % 4]
                for kc in range(8):
                    P.mm(pa[:, 0:GT], ub[:, kc, :], h2bf[:, kc, :], start=(kc == 0), stop=(kc == 7))
                P.act(gl[n1 % 4], pa[:, 0:GT], AF.Gelu_apprx_tanh)
                P.tt(cf[n1 % 4], gl[n1 % 4], Wt[:, :, n1], ALU.mult)

            def c3_s2(n1):
                c_ = cf[n1 % 4]
                vb = vtb[(n1 // 2) % NBP][:, n1 % 2, :]
                for tt in range(2):
                    for half in range(2):
                        P.mm(pb[tt * 2 + half], c_[:, tt * 128:(tt + 1) * 128],
                             vb[:, half * 512:(half + 1) * 512], start=(n1 == 0), stop=(n1 == 127))

            for it in range(128 + LA3):
                if it == 8 and grp + 1 < ngroups:
                    prefetch(grp + 1)
                if it < 128:
                    c3_s1(it)
                if it >= LA3:
                    c3_s2(it - LA3)
        wt_scope.__exit__(None, None, None)
        with P.scope():
            pre = P.sb("pre2", [128, D])
            x2 = [P.sb("x2_%d" % i, [128, D]) for i in range(2)]
            for tt in range(2):
                for half in range(2):
                    sl = slice(half * 512, (half + 1) * 512)
                    P.tt(pre[:, sl], pb[tt * 2 + half], g2_bc[:, sl], ALU.mult)
                P.op("dve", "scalar_tensor_tensor", out=pre, in0=x1[tt], scalar=ALPHA, in1=pre,
                     op0=ALU.mult, op1=ALU.add)
                emit_ln(P, x2[tt], pre, lng[1], lnb[1], lnscr)
                for xo in A["x_out"](grp * 2 + tt):
                    P.dma("sp", out=xo, in_=x2[tt])
            if "after_group" in A:
                A["after_group"](grp)
            if with_kv:
                hkT = P.sb("hkT", [128, 8, GT])
                wkv = P.sb("wkv", [128, 8, 1536])
                P.dma("sp", out=wkv, in_=wkv_d.rearrange("(kc p) n -> p kc n", p=128))
                for tt in range(2):
                    for k4 in range(2):
                        bank = pb[4 + k4]
                        for q in range(4):
                            kc = k4 * 4 + q
                            P.tr(bank[:, q * 128:(q + 1) * 128], x2[tt][:, kc * 128:(kc + 1) * 128], ident)
                        for q in range(4):
                            kc = k4 * 4 + q
                            P.ts(hkT[:, kc, tt * 128:(tt + 1) * 128], bank[:, q * 128:(q + 1) * 128],
                                 kvscT[:, kc:kc + 1], ALU.mult, kvshT[:, kc:kc + 1], ALU.add)
                for tt in range(2):
                    vt_sb = P.sb("vt_sb%d" % tt, [128, 2, 256])
                    for wi, c0 in enumerate((768, 1280)):
                        bank = pb[wi]
                        for kc in range(8):
                            P.mm(bank[:, 0:256], hkT[:, kc, tt * 128:(tt + 1) * 128], wkv[:, kc, c0:c0 + 256],
                                 start=(kc == 0), stop=(kc == 7))
                        P.cp(vt_sb[:, wi, :], bank[:, 0:256])
                        dst = A["vtok_out"](wi)[:, t0 + tt * 128:t0 + (tt + 1) * 128, :].rearrange("g t d -> t g d")
                        P.dma("sp", out=dst, in_=vt_sb[:, wi, :].rearrange("p (g d) -> p g d", g=4))
                for wi, c0 in enumerate((0, 256, 512, 1024)):
                    for cb in range(2):
                        bank = pb[2 + (wi * 2 + cb) % 2]
                        for kc in range(8):
                            P.mm(bank[:, 0:GT], wkv[:, kc, c0 + cb * 128:c0 + (cb + 1) * 128], hkT[:, kc, :],
                                 start=(kc == 0), stop=(kc == 7))
                        kt_sb = P.sb("kt_sb%d_%d" % (wi, cb), [128, GT])
                        P.cp(kt_sb, bank[:, 0:GT])
                        for gg in range(2):
                            P.dma("sp", out=A["kvT_out"](wi, cb * 2 + gg)[:, t0:t0 + GT],
                                  in_=kt_sb[gg * 64:(gg + 1) * 64, :])


def emit_hT(P, hT, xt, scT, shT, ident, bank0, bank1):
    for k4 in range(2):
        bank = (bank0, bank1)[k4]
        for q in range(4):
            kc = k4 * 4 + q
            P.tr(bank[:, q * 128:(q + 1) * 128], xt[:, kc * 128:(kc + 1) * 128], ident)
        for q in range(4):
            kc = k4 * 4 + q
            P.ts(hT[:, kc, :], bank[:, q * 128:(q + 1) * 128], scT[:, kc:kc + 1], ALU.mult,
                 shT[:, kc:kc + 1], ALU.add, nowaw=True)


def emit_retmix(P, pb, ident, A, nchunks=S // 128):
  with P.scope():
    cT_d, aw_d, ab_d = A["cT"], A["ada_w"], A["ada_b"]
    wqk_d, wv_d, wg_d, cos_d, sin_d = A["wqk"], A["wv"], A["wg"], A["cos"], A["sin"]
    decT = P.sb("decT", [128, 128])
    P.dma("sp", out=decT, in_=A["decT"])
    cols = P.sb("cols", [128, 4])
    P.dma("sp", out=cols, in_=A["cols"])
    mod_bc = P.sb("mod_bc", [128, 2048])
    modT = P.sb("modT", [128, 16])
    emit_mod(P, cT_d, aw_d, ab_d, ident, 0, 2048, pb[0], pb[1], mod_bc, modT)
    shT = modT[:, 0:8]
    scT = P.sb("scT", [128, 8])
    P.ts(scT, modT[:, 8:16], 1.0, ALU.add)
    wqk = P.sb("wqk", [128, 8, 512], BF16)
    wv = P.sb("wv", [128, 8, 512], BF16)
    wg = P.sb("wg", [128, 8, 512], BF16)
    for (w, wd) in ((wqk, wqk_d), (wv, wv_d), (wg, wg_d)):
        P.dma("pool", out=w, in_=wd.rearrange("(kc p) n -> p kc n", p=128))
    state = P.sb("state", [128, 2, 512])
    P.op("dve", "memset", ap=state, constant=0.0)
    NBUF = 2
    mk = lambda n, s: [P.sb("%s%d" % (n, i), s) for i in range(NBUF)]
    xt_, cs_, sn_ = mk("xt", [128, D]), mk("cs", [128, 128]), mk("sn", [128, 128])
    hT_ = [P.sb("hT%d" % i, [128, 8, 128], BF16) for i in range(NBUF)]
    rot_, t1_, t2_ = mk("rot", [128, 2, 2, 128]), mk("t1", [128, 2, 128]), mk("t2", [128, 2, 128])
    qd_, kd_, ks_ = mk("qd", [128, 256]), mk("kd", [128, 256]), mk("ks", [128, 256])
    v_, qkT_, PT_ = mk("v", [128, 512]), mk("qkT", [128, 4, 128]), mk("PT", [128, 128])
    on_, sg_ = mk("on", [128, 512]), mk("sg", [128, 512])
    st_, mv_, rs_ = mk("st", [128, 6]), mk("mv", [128, 2]), mk("rs", [128, 1])
    def ret_load(m):
        j_ = m % NBUF
        rws = slice(m * 128, (m + 1) * 128)
        P.dma("sp", out=xt_[j_], in_=A["x_tile"](m))
        P.dma("sp", out=cs_[j_], in_=cos_d[rws, :])
        P.dma("sp", out=sn_[j_], in_=sin_d[rws, :])

    for n in range(nchunks):
        i = n % NBUF
        xt, hT, cs, sn, rot, t1, t2 = xt_[i], hT_[i], cs_[i], sn_[i], rot_[i], t1_[i], t2_[i]
        qd, kd, ks, v, qkT, PT, on, sg = qd_[i], kd_[i], ks_[i], v_[i], qkT_[i], PT_[i], on_[i], sg_[i]
        st, mv, rs = st_[i], mv_[i], rs_[i]
        rows = slice(n * 128, (n + 1) * 128)
        if "bg" in A:
            A["bg"](n)
        if n == 0:
            ret_load(0)
        emit_hT(P, hT, xt, scT, shT, ident, pb[0], pb[1])
        if n + 1 < nchunks:
            ret_load(n + 1)
        for (bank, w) in ((pb[2], wqk), (pb[3], wv), (pb[4], wg)):
            for kc in range(8):
                P.mm(bank, hT[:, kc, :], w[:, kc, :], start=(kc == 0), stop=(kc == 7))
        P.act(v, pb[3], AF.Copy)
        P.act(sg, pb[4], AF.Silu)
        qk4 = pb[2].rearrange("p (a h d) -> p a h d", a=2, h=2)
        x1 = qk4[:, :, 0, :]
        x2 = qk4[:, :, 1, :]
        csb = cs.unsqueeze(1).to_broadcast([128, 2, 128])
        snb = sn.unsqueeze(1).to_broadcast([128, 2, 128])
        P.tt(t1, x1, csb, ALU.mult)
        P.tt(t2, x2, snb, ALU.mult)
        P.tt(rot[:, :, 0, :], t1, t2, ALU.subtract)
        P.tt(t1, x1, snb, ALU.mult)
        P.tt(t2, x2, csb, ALU.mult)
        P.tt(rot[:, :, 1, :], t1, t2, ALU.add)
        qr = rot[:, 0, :, :].rearrange("p h d -> p (h d)")
        kr = rot[:, 1, :, :].rearrange("p h d -> p (h d)")
        P.ts(qd, qr, cols[:, 0:1], ALU.mult)
        P.ts(kd, kr, cols[:, 1:2], ALU.mult)
        P.ts(ks, kr, 1.0 / 16.0, ALU.mult)
        for dc in range(2):
            P.tr(pb[0][:, dc * 128:(dc + 1) * 128], qd[:, dc * 128:(dc + 1) * 128], ident)
            P.tr(pb[0][:, (2 + dc) * 128:(3 + dc) * 128], ks[:, dc * 128:(dc + 1) * 128], ident)
        P.cp(qkT, pb[0].rearrange("p (a t) -> p a t", a=4))
        for dc in range(2):
            P.mm(pb[1][:, 0:128], qkT[:, 2 + dc, :], qkT[:, dc, :], start=(dc == 0), stop=(dc == 1))
        P.tt(PT, pb[1][:, 0:128], decT, ALU.mult)
        P.mm(pb[5], PT, v, start=True, stop=False)
        for dc in range(2):
            P.mm(pb[5], qkT[:, dc, :], state[:, dc, :], start=False, stop=(dc == 1))
        for dc in range(2):
            P.mm(pb[6 + dc], kd[:, dc * 128:(dc + 1) * 128], v)
        for dc in range(2):
            P.op("dve", "scalar_tensor_tensor", out=state[:, dc, :], in0=state[:, dc, :], scalar=cols[:, 2:3],
                 in1=pb[6 + dc], op0=ALU.mult, op1=ALU.add)
        P.op("dve", "bn_stats", out=st, in_=pb[5])
        P.op("dve", "bn_aggr", out=mv, in_=st)
        P.ts(rs, mv[:, 1:2], LN_EPS, ALU.add)
        P.act(rs, rs, AF.Sqrt)
        P.op("dve", "reciprocal", out=rs, in_=rs)
        P.ts(on, pb[5], mv[:, 0:1], ALU.subtract, rs[:, 0:1], ALU.mult)
        P.tt(on, on, sg, ALU.mult)
        P.dma("sp", out=A["o_out"](n), in_=on)
        if "after_out" in A:
            A["after_out"](n)


def emit_nsamix(P, pb, ident, A, nqb=S // 128):
  with P.scope():
    SE = nqb * 128
    NCP = SE // 16
    NC = NCP - 1
    NCH = max(1, NCP // 128)
    NR = SE // TOK
    cT_d, aw_d, ab_d, wq_d, wgt_d = A["cT"], A["ada_w"], A["ada_b"], A["wq"], A["wgate"]
    peT_d, w1_d, b1T_d, w2_d, c2s_d, E_d = A["peT"], A["w1"], A["b1T"], A["w2l"], A["c2s"], A["Eall"]
    tri_d, low_d, A_d, av_d, fb_d = A["tri"], A["low"], A["Acmp"], A["availW"], A["fbW"]
    ld = lambda name, shape, src: (lambda t: (P.dma("sp", out=t, in_=src), t)[1])(P.sb(name, shape))
    tri = ld("tri", [128, 128], tri_d)
    low = ld("low", [128, 128], low_d)
    Acmp = ld("Acmp", [128, 128], A_d)
    availW = ld("availW", [128, 256], av_d)
    fbW = ld("fbW", [128, 256], fb_d)
    c2s = ld("c2s", [128, NCH, 128], c2s_d)
    b1T = ld("b1T", [128, 2, 2], b1T_d)
    w2l = ld("w2l", [128, 2, 2, 64], w2_d)
    wq = P.sb("wq", [128, 8, 256], BF16)
    wgt = P.sb("wgt", [128, 8, 12], BF16)
    P.dma("pool", out=wq, in_=wq_d.rearrange("(kc p) n -> p kc n", p=128))
    P.dma("pool", out=wgt, in_=wgt_d.rearrange("(kc p) n -> p kc n", p=128))
    mod_bc = P.sb("mod_bc", [128, 2048])
    modT = P.sb("modT", [128, 16])
    emit_mod(P, cT_d, aw_d, ab_d, ident, 0, 2048, pb[0], pb[1], mod_bc, modT)
    shT = modT[:, 0:8]
    scT = P.sb("scT", [128, 8])
    P.ts(scT, modT[:, 8:16], 1.0, ALU.add)

    kcmpT = P.sb("kcmpT", [64, NCH * 128])
    vcmp = P.sb("vcmp", [128, NCH, 65])
    P.op("dve", "memset", ap=kcmpT, constant=0.0)
    P.op("dve", "memset", ap=vcmp, constant=0.0)
    P.op("dve", "memset", ap=vcmp[:, :, 64:65], constant=1.0)
    with P.scope():
        rawT = P.sb("rawT", [64, SE])
        peT = P.sb("peT", [64, 32])
        w1b = [P.sb("w1b%d" % i, [64, 256]) for i in range(3)]
        hid = P.sb("hid", [128, 2, NCH * 128])
        biasT = P.sb("biasT", [128, 2])
        P.op("dve", "memset", ap=hid, constant=0.0)
        for c in range(2):
            for rk in range(NR):
                P.dma("sp", out=rawT[:, rk * TOK:(rk + 1) * TOK], in_=A["kvT"](c, rk))
            P.dma("sp", out=peT, in_=peT_d[c])
            rv = rawT.rearrange("d (n s) -> d n s", s=16)
            for p in range(32):
                wb = w1b[p % 3]
                P.dma("sp", out=wb, in_=w1_d[c, p * 64:(p + 1) * 64, :])
                xp = rv[:, 0:NC, p] if p < 16 else rv[:, 1:NC + 1, p - 16]
                for hc in range(2):
                    P.mm(pb[hc][:, 0:NC], wb[:, hc * 128:(hc + 1) * 128], xp, start=(p == 0), stop=(p == 31))
                    P.mm(pb[2 + hc][:, 0:1], wb[:, hc * 128:(hc + 1) * 128], peT[:, p:p + 1],
                         start=(p == 0), stop=(p == 31))
            for hc in range(2):
                P.tt(biasT[:, hc:hc + 1], pb[2 + hc][:, 0:1], b1T[:, c, hc:hc + 1], ALU.add)
                P.act(hid[:, hc, 0:NC], pb[hc][:, 0:NC], AF.Gelu_apprx_tanh, bias=biasT[:, hc:hc + 1])
            if c == 0:
                for hc in range(2):
                    P.mm(pb[4][0:64, 0:NC], w2l[:, 0, hc, :], hid[:, hc, 0:NC], start=(hc == 0), stop=(hc == 1))
                P.cp(kcmpT[:, 0:NC], pb[4][0:64, 0:NC])
            else:
                for ch in range(NCH):
                    for hc in range(2):
                        P.mm(pb[4][:, ch * 64:(ch + 1) * 64], hid[:, hc, ch * 128:(ch + 1) * 128], w2l[:, 1, hc, :],
                             start=(hc == 0), stop=(hc == 1))
                    P.cp(vcmp[:, ch, 0:64], pb[4][:, ch * 64:(ch + 1) * 64])

    import os
    KD = BF16 if os.environ.get("NSA_BF", "1") == "1" else F32
    kslcT = P.sb("kslcT", [64, SE], KD)
    kwinT = P.sb("kwinT", [64, SE], KD)
    vslc = P.sb("vslc", [128, nqb, 66], KD)
    vwin = P.sb("vwin", [128, nqb, 66], KD)
    Eall = P.sb("Eall", [128, nqb, 128], KD)
    CPR = TOK // 128
    with P.scope():
        stg = P.sb("stg", [128, nqb * 128])
        for (dst, w) in ((kslcT, 2), (kwinT, 3)):
            for rk in range(NR):
                P.dma("sp", out=stg[0:64, rk * TOK:(rk + 1) * TOK], in_=A["kvT"](w, rk))
            P.cp(dst, stg[0:64, 0:SE])
        for (vt, bi) in ((vslc, 0), (vwin, 1)):
            sv_ = stg[:, 0:nqb * 64].rearrange("p (c d) -> p c d", d=64)
            for rk in range(NR):
                P.dma("sp", out=sv_[:, rk * CPR:(rk + 1) * CPR, :],
                      in_=A["vtok"](bi, rk).rearrange("(c p) d -> p c d", p=128))
            P.op("dve", "memset", ap=vt[:, :, 64:65], constant=1.0)
            P.cp(vt[:, :, 0:64], sv_)
        P.dma("sp", out=stg.rearrange("p (c k) -> p c k", k=128), in_=E_d)
        P.cp(Eall, stg.rearrange("p (c k) -> p c k", k=128))
    NB = 2
    LA = int(os.environ.get('NSA_LA', '3'))
    PSM = os.environ.get('NSA_PSM', '1') == '1'
    mk = lambda n, s, dt=F32: [P.sb("%s%d" % (n, i), s, dt) for i in range(NB)]
    xt_, qT_, gate_ = mk("xt", [128, D]), mk("qT", [64, 512]), mk("gate", [128, 12])
    hT_ = mk("hT", [128, 8, 128], BF16)
    qTb_ = mk("qTb", [64, 512], KD)
    NE = LA + 2
    eT_ = [P.sb("eT%d" % i, [128, 4, 128]) for i in range(NE)]
    pT_ = [P.sb("pT%d" % i, [128, 4, 128]) for i in range(NE)]
    eTb_ = [P.sb("eTb%d" % i, [128, 4, 128], KD) for i in range(NE)]
    pTb_ = [P.sb("pTb%d" % i, [128, 4, 128], KD) for i in range(NE)]
    m_ = [P.sb("m%d" % i, [128, 128]) for i in range(NE)]
    imp_, sc_, tmp_ = mk("imp", [128, 128]), mk("sc", [128, 128]), mk("tmp", [128, 128])
    vals_, thr_, sel_ = mk("vals", [128, 16]), mk("thr", [128, 1]), mk("sel", [128, 128])
    selT_ = mk("selT", [128, 128], KD)
    negsel_ = mk("negsel", [128, 4, 128], KD)
    negsel_cur = [None]
    zc_, gs_, out_ = mk("zc", [128, 4]), mk("gs", [128, 4]), mk("out", [128, 4, 64])
    st_banks = [pb[2], pb[3], pb[0]]
    mk_banks = [pb[4], pb[1], pb[5]]
    ecnt = [0]

    def pipeline(items):
        n = len(items)
        srcs = [None] * n
        for i in range(n + LA):
            if i < n:
                srcs[i] = items[i][0]()
            if i >= LA:
                items[i - LA][1](srcs[i - LA])

    def stage1(q_ap, kT_chunk, bf, mask=None, maskE=None, tri_too=False):
        i = ecnt[0] % NE
        st = st_banks[ecnt[0] % 3]
        ecnt[0] += 1
        fold = maskE is not None and not tri_too and (ecnt[0] % 2 == 1)
        if fold:
            P.mm(st, kT_chunk, q_ap, start=True, stop=False)
            P.mm(st, maskE, negsel_cur[0].rearrange("p r t -> p (r t)"), start=False, stop=True)
            maskE = None
        else:
            P.mm(st, kT_chunk, q_ap)
        eT = (eTb_ if bf else eT_)[i]
        P.act(eT, st.rearrange("p (r t) -> p r t", r=4), AF.Exp)
        if maskE is not None:
            mb = mk_banks[i % 3]
            P.mm(mb[:, 0:128], maskE, selT_cur[0])
            if tri_too:
                P.tt(m_[i], mb[:, 0:128], tri, ALU.mult)
                mask = m_[i]
            elif PSM:
                mask = mb[:, 0:128]
            else:
                P.cp(m_[i], mb[:, 0:128])
                mask = m_[i]
        if mask is None:
            return eT
        pT = (pTb_ if bf else pT_)[i]
        P.tt(pT, eT, mask.unsqueeze(1).to_broadcast([128, 4, 128]), ALU.mult)
        return pT

    def stage2(src, vext_chunk, acc, first, last, extra=None):
        for r in range(4):
            P.mm(acc[:, r * 65:(r + 1) * 65], src[:, r, :], vext_chunk,
                 start=(first and r == 0), stop=(last and r == 3))
        if extra is not None:
            bank, rhs = extra
            for r in range(4):
                P.mm(bank[:, r * 128:(r + 1) * 128], src[:, r, :], rhs,
                     start=(first and r == 0), stop=(last and r == 3))

    selT_cur = [None]
    for qb in range(nqb):
        i = qb % NB
        xt, hT, qT, qTb, gate = xt_[i], hT_[i], qT_[i], qTb_[i], gate_[i]
        imp, sc, tmp, vals, thr, sel, selT = imp_[i], sc_[i], tmp_[i], vals_[i], thr_[i], sel_[i], selT_[i]
        zc, gs, out = zc_[i], gs_[i], out_[i]
        negsel = negsel_[i]
        if "bg" in A:
            A["bg"](qb)
        if qb == 0:
            P.dma("sp", out=xt_[0], in_=A["x_tile"](0))
        emit_hT(P, hT, xt, scT, shT, ident, pb[0], pb[1])
        if qb + 1 < nqb:
            P.dma("sp", out=xt_[(qb + 1) % NB], in_=A["x_tile"](qb + 1))
        for r in range(4):
            for kc in range(8):
                P.mm(pb[0][0:64, r * 128:(r + 1) * 128], wq[:, kc, r * 64:(r + 1) * 64], hT[:, kc, :],
                     start=(kc == 0), stop=(kc == 7))
        P.ts(qT, pb[0][0:64, :], 0.125, ALU.mult)
        P.ts(qTb, pb[0][0:64, :], 0.125, ALU.mult)
        for kc in range(8):
            P.mm(pb[1][:, 0:12], hT[:, kc, :], wgt[:, kc, :], start=(kc == 0), stop=(kc == 7))
        P.act(gate, pb[1][:, 0:12], AF.Sigmoid)
        chunks = []
        for c in range(NCH):
            th = 128 * qb - 31 - 2048 * c
            if th < -127:
                continue
            chunks.append((c, None if th >= 2032 else th))
        items = []
        for k, (c, th) in enumerate(chunks):
            first, last = (k == 0), (k == len(chunks) - 1)

            def s1(c=c, th=th, k=k):
                mask = None
                if th is not None:
                    mask = m_[k % NE]
                    P.ts(mask, Acmp, float(th), ALU.is_le)
                return stage1(qT, kcmpT[:, c * 128:(c + 1) * 128], False, mask=mask)

            def s2(src, c=c, first=first, last=last):
                stage2(src, vcmp[:, c, :], pb[6], first, last, extra=(pb[7], c2s[:, c, :]))
            items.append((s1, s2))
        pipeline(items)
        acc = pb[6][:, 0:260].rearrange("p (r e) -> p r e", e=65)
        P.ts(zc, acc[:, :, 64], 1e-30, ALU.max)
        P.op("dve", "reciprocal", out=zc, in_=zc)
        P.tt(gs, zc, gate.rearrange("p (r b) -> p r b", b=3)[:, :, 0], ALU.mult)
        for r in range(4):
            P.ts(out[:, r, :], acc[:, r, 0:64], gs[:, r:r + 1], ALU.mult, nowaw=True)
        for r in range(4):
            if r == 0:
                P.ts(imp, pb[7][:, 0:128], zc[:, 0:1], ALU.mult)
            else:
                P.op("dve", "scalar_tensor_tensor", out=imp, in0=pb[7][:, r * 128:(r + 1) * 128],
                     scalar=zc[:, r:r + 1], in1=imp, op0=ALU.mult, op1=ALU.add)
        items = []
        wch = list(range(max(0, qb - 4), qb + 1))
        for k, c in enumerate(wch):
            mask = tri if c == qb else (low if c == qb - 4 else None)

            def s1(c=c, mask=mask):
                return stage1(qTb, kwinT[:, c * 128:(c + 1) * 128], True, mask=mask)

            def s2(src, c=c, k=k):
                stage2(src, vwin[:, c, 0:65], pb[7], k == 0, k == len(wch) - 1)
            items.append((s1, s2))
        pipeline(items)
        off = 126 - 2 * qb
        P.tt(sc, imp, availW[:, off:off + 128], ALU.mult)
        P.tt(sc, sc, fbW[:, off:off + 128], ALU.add)
        P.ts(sc[:, 0:1], sc[:, 0:1], 100.0, ALU.add)
        P.op("dve", "max", out=vals[:, 0:8], in_=sc)
        P.op("dve", "match_replace", out=tmp, in_to_replace=vals[:, 0:8], in_values=sc, imm_value=-1e30)
        P.op("dve", "max", out=vals[:, 8:16], in_=tmp)
        P.ts(thr, vals[:, 15:16], 0.0, ALU.max)
        P.ts(sel, sc, thr[:, 0:1], ALU.is_ge)
        P.tr(pb[1][:, 0:128], sel, ident)
        P.cp(selT, pb[1][:, 0:128])
        P.ts(negsel, pb[1][:, 0:128].unsqueeze(1).to_broadcast([128, 4, 128]), 1.0, ALU.subtract, 30000.0, ALU.mult)
        selT_cur[0] = selT
        negsel_cur[0] = negsel
        items = []
        for c in range(qb + 1):
            def s1(c=c):
                return stage1(qTb, kslcT[:, c * 128:(c + 1) * 128], True, maskE=Eall[:, c, :], tri_too=(c == qb))

            def s2(src, c=c):
                stage2(src, vslc[:, c, 0:65], pb[6], c == 0, c == qb)
            items.append((s1, s2))
        pipeline(items)
        for (bank, br) in ((pb[6], 1), (pb[7], 2)):
            acc = bank[:, 0:260].rearrange("p (r e) -> p r e", e=65)
            P.ts(zc, acc[:, :, 64], 1e-30, ALU.max)
            P.op("dve", "reciprocal", out=zc, in_=zc)
            P.tt(gs, zc, gate.rearrange("p (r b) -> p r b", b=3)[:, :, br], ALU.mult)
            for r in range(4):
                P.op("dve", "scalar_tensor_tensor", out=out[:, r, :], in0=acc[:, r, 0:64], scalar=gs[:, r:r + 1],
                     in1=out[:, r, :], op0=ALU.mult, op1=ALU.add, _nowaw=True)
        P.dma("sp", out=A["o_out"](qb), in_=out.rearrange("p r d -> p (r d)"))
        if "after_out" in A:
            A["after_out"](qb)


GROUPS = [[0, 1, 2, 3], [4, 5, 6, 7]]
CC_BYTES = 1 << 20


def build_fused():
    nc = bass.Bass("TRN2", target_bir_lowering=False)
    P = Prog(nc)
    di = lambda n, s, dt=F32: P.dram(n, s, dt, kind="ExternalInput", track=False)
    I = {}
    for n, s in (("x_full", [S, D]), ("xs0", [TOK, D]), ("cT", [128, 8]), ("ada_w", [DEPTH, D, 6 * D]),
                 ("ada_b", [DEPTH, 1, 6 * D]), ("ln_g", [DEPTH, 2, D]), ("ln_b", [DEPTH, 2, D]),
                 ("wqk", [2, D, 512]), ("wv", [2, D, 512]), ("wg", [2, D, 512]), ("ret_w_o", [2, 2048, D]),
                 ("cos", [S, 128]), ("sin", [S, 128]), ("decT", [128, 128]), ("cols", [128, 4]),
                 ("kv_ada_w", [D, 2 * D]), ("kv_ada_b", [1, 2 * D]), ("w_kv", [D, 1536]),
                 ("wq", [2, D, 256]), ("wgate", [2, D, 12]), ("nsa_w_o", [2, D, D]),
                 ("peT", [2, 64, 32]), ("w1", [2, 2048, 256]), ("b1T", [128, 2, 2]), ("w2l", [128, 2, 2, 64]),
                 ("c2s", [128, 4, 128]), ("Eall", [128, 64, 128]), ("tri", [128, 128]), ("low", [128, 128]),
                 ("Acmp", [128, 128]), ("availW", [128, 256]), ("fbW", [128, 256]),
                 ("w_q", [DEPTH, D, 2048]), ("keysT", [DEPTH, 128, 16, 128]), ("uT", [DEPTH, 128, 128, 8, 128]),
                 ("v", [DEPTH, 16384, D]), ("ident", [128, 128]), ("iota", [128, 128])):
        I[n] = di(n, s)
    x_out = P.dram("x_out", [TOK, D], F32, kind="ExternalOutput", track=False)

    pb = [P.ps("pb%d" % i) for i in range(8)]
    ident = P.sb("ident", [128, 128])
    P.dma("sp", out=ident, in_=I["ident"])
    iota = P.sb("iota", [128, 128])
    P.dma("sp", out=iota, in_=I["iota"])
    jr = nc.sync.partition_id() % 4

    x_loc = [P.dram("x_loc%d" % l, [TOK, D]) for l in range(3)]
    x_gat = [P.dram("x_gat%d" % l, [S, D]) for l in range(3)]
    kvT_loc = P.dram("kvT_loc", [16 * 64, TOK])
    kvT_gat = P.dram("kvT_gat", [16 * 4 * 64, TOK])
    vt_loc = P.dram("vt_loc", [8 * TOK, 64])
    vt_gat = P.dram("vt_gat", [8 * 4 * TOK, 64])
    vt_loc4 = vt_loc.rearrange("(w g t) d -> w g t d", w=2, g=4)
    kvT_mine = P.dram("kvT_mine", [4 * 256, TOK])
    vt_mine = P.dram("vt_mine", [2 * S, 64])

    def x_tile_from(l):
        if l == 0:
            return lambda n: I["x_full"][n * 128:(n + 1) * 128, :]

        def f(n):
            rank, r = n // 16, (n % 16) * 128
            ch, off = r // 256, r % 256
            row = (ch * 4 + rank) * 256 + off
            return x_gat[l - 1][row:row + 128, :]
        return f

    for l in range(DEPTH):
        Wd = 512 if l < 2 else 256
        RPC = CC_BYTES // (Wd * 4)
        NCHK = S // RPC
        CPJ = TOK // RPC
        o_loc = P.dram("o_loc%d" % l, [S, Wd])
        o_gat = P.dram("o_gat%d" % l, [NCHK * 4 * RPC, Wd])
        uT_bf = P.dram("uT_bf%d" % l, [64, 128, 2, 1024], BF16)
        v_bf = P.dram("v_bf%d" % l, [64, 128, 2, D], BF16)
        def bg(n, l=l, uT_bf=uT_bf, v_bf=v_bf):
            if P.cnt["pe"] > 0:
                P._wait("pool", ("eng", "pe", P.cnt["pe"]))
            P.dma("pool", out=uT_bf[n], in_=I["uT"][l, 2 * n:2 * n + 2].rearrange("c p k n -> p c (k n)"))
            P.dma("pool", out=v_bf[n], in_=I["v"][l, 2 * n * 128:(2 * n + 2) * 128, :].rearrange("(c p) d -> p c d", p=128))
        def after_out(n, o_loc=o_loc, o_gat=o_gat, RPC=RPC):
            per = RPC // 128
            if (n + 1) % per == 0:
                ch = (n + 1) // per - 1
                P.collective("AllGather", GROUPS, o_loc[ch * RPC:(ch + 1) * RPC, :],
                             o_gat[ch * 4 * RPC:(ch + 1) * 4 * RPC, :])

        base = dict(cT=I["cT"], ada_w=I["ada_w"][l], ada_b=I["ada_b"][l], x_tile=x_tile_from(l), bg=bg,
                    after_out=after_out,
                    o_out=lambda n, o_loc=o_loc: o_loc[n * 128:(n + 1) * 128, :])
        if l < 2:
            A = dict(base, wqk=I["wqk"][l], wv=I["wv"][l], wg=I["wg"][l], cos=I["cos"], sin=I["sin"],
                     decT=I["decT"], cols=I["cols"])
            emit_retmix(P, pb, ident, A)
        else:
            A = dict(base, wq=I["wq"][l - 2], wgate=I["wgate"][l - 2],
                     kvT=lambda w, rk: kvT_mine[w * 256 + rk * 64:w * 256 + (rk + 1) * 64, :],
                     vtok=lambda w, rk: vt_mine[w * S + rk * TOK:w * S + (rk + 1) * TOK, :],
                     **{k: I[k] for k in ("peT", "w1", "b1T", "w2l", "c2s", "Eall", "tri", "low", "Acmp",
                                          "availW", "fbW")})
            emit_nsamix(P, pb, ident, A)

        o_mine = P.dram("o_mine%d" % l, [S, Wd])
        o_gat_v = o_gat.rearrange("(j q) w -> j q w", j=4)
        NSPL = 2
        for sp_ in range(NSPL):
            rs = slice(sp_ * (S // NSPL), (sp_ + 1) * (S // NSPL))
            P.dma("sp", out=o_mine[rs, :].rearrange("(o r) w -> o (r w)", o=1),
                  in_=o_gat_v[bass.ds(jr, 1), rs, :].rearrange("o r w -> o (r w)"))

        def o_pieces(tile, o_mine=o_mine, RPC=RPC, Wd=Wd):
            r = tile * 128
            sc, off = r // RPC, r % RPC
            return [(o_mine[(sc * 4 + h) * RPC + off:(sc * 4 + h) * RPC + off + 128, :], h * Wd, Wd)
                    for h in range(4)]

        rows = lambda t: slice(t * 128, (t + 1) * 128)
        A = dict(cT=I["cT"], ada_w=I["ada_w"][l], ada_b=I["ada_b"][l], ln_g=I["ln_g"][l], ln_b=I["ln_b"][l],
                 w_o=(I["ret_w_o"][l] if l < 2 else I["nsa_w_o"][l - 2]), w_q=I["w_q"][l], keysT=I["keysT"][l],
                 uT_pair=lambda q, uT_bf=uT_bf: uT_bf[q], v_pair=lambda q, v_bf=v_bf: v_bf[q], tab_q="sp",
                 o_pieces=o_pieces,
                 x_tile=(lambda t: I["xs0"][rows(t), :]) if l == 0 else (lambda t, xl=x_loc[l - 1]: xl[rows(t), :]),
                 x_out=(lambda t: [x_out[rows(t), :]]) if l == DEPTH - 1 else (lambda t, xl=x_loc[l]: [xl[rows(t), :]]))
        if l == 1:
            A.update(kv_ada_w=I["kv_ada_w"], kv_ada_b=I["kv_ada_b"], w_kv=I["w_kv"],
                     kvT_out=lambda w, g: kvT_loc[(w * 4 + g) * 64:(w * 4 + g + 1) * 64, :],
                     vtok_out=lambda w: vt_loc4[w])
        if l < DEPTH - 1:
            A["after_group"] = lambda ch, l=l: P.collective("AllGather", GROUPS, x_loc[l][ch * 256:(ch + 1) * 256, :],
                                                            x_gat[l][ch * 1024:(ch + 1) * 1024, :])
        emit_tok(P, pb, ident, iota, A, 4 * Wd, l == 1)
        if l == 1:
            for wg_ in range(16):
                P.collective("AllGather", GROUPS, kvT_loc[wg_ * 64:(wg_ + 1) * 64, :],
                             kvT_gat[wg_ * 256:(wg_ + 1) * 256, :])
            for wg_ in range(8):
                P.collective("AllGather", GROUPS, vt_loc[wg_ * TOK:(wg_ + 1) * TOK, :],
                             vt_gat[wg_ * 4 * TOK:(wg_ + 1) * 4 * TOK, :])
            kg = kvT_gat.rearrange("(w g r) t -> w g r t", w=4, g=4)
            P.dma("sp", out=kvT_mine.rearrange("(w r) t -> w (r t)", w=4),
                  in_=kg[:, bass.ds(jr, 1), :, :].rearrange("w o r t -> w (o r t)"))
            vg = vt_gat.rearrange("(w g r) d -> w g r d", w=2, g=4)
            P.dma("sp", out=vt_mine.rearrange("(w r) d -> w (r d)", w=2),
                  in_=vg[:, bass.ds(jr, 1), :, :].rearrange("w o r d -> w (o r d)"))
    P.finish()
    return nc, P


_CONST = {}


def _consts():
    if _CONST:
        return _CONST
    a = np.arange(128)
    c = _CONST
    c["ident"] = np.eye(128, dtype=np.float32)
    c["iota"] = np.tile(np.arange(128, dtype=np.float32)[None, :], (128, 1))
    import jax
    import jax.numpy as jnp
    with jax.default_device(jax.devices("cpu")[0]):
        pos = jnp.arange(S, dtype=jnp.float32)
        theta = 1.0 / (10000.0 ** jnp.linspace(0.0, 1.0, 128, dtype=jnp.float32))
        ang = pos[:, None] * theta[None, :]
        c["cos"] = np.asarray(jnp.cos(ang))
        c["sin"] = np.asarray(jnp.sin(ang))
    idx = np.arange(128, dtype=np.float64)
    for h in range(4):
        lg = np.log1p(-np.exp2(-5.0 - h))
        qdec = np.exp((idx + 1) * lg)
        kdec = np.exp((127 - idx) * lg) / 16.0
        cdec = np.exp(128 * lg)
        decT = np.where(idx[:, None] <= idx[None, :], np.exp(-(idx[:, None] + 1) * lg), 0.0)
        c["decT%d" % h] = decT.astype(np.float32)
        c["cols%d" % h] = np.stack([qdec, kdec, np.full(128, cdec), np.zeros(128)], axis=1).astype(np.float32)
    nqb = S // 128
    NCP = S // 16
    i = np.arange(NCP)[:, None] * 16
    j = np.arange(128)[None, :] * 64
    ov = np.minimum(i + 32, j + 64) - np.maximum(i, j)
    c2s = (np.clip(ov, 0, None) / 32.0).astype(np.float32)
    c2s[NCP - 1:] = 0.0
    c["c2s"] = np.ascontiguousarray(c2s.reshape(NCP // 128, 128, 128).transpose(1, 0, 2))
    jj = np.arange(128)[:, None, None]
    cc = np.arange(nqb)[None, :, None]
    kk = np.arange(128)[None, None, :]
    c["Eall"] = (jj == 2 * cc + kk // 64).astype(np.float32)
    c["tri"] = (a[:, None] <= a[None, :]).astype(np.float32)
    c["low"] = (a[:, None] > a[None, :]).astype(np.float32)
    c["Acmp"] = (16.0 * a[:, None] - a[None, :]).astype(np.float32)
    tl = a[:, None]
    jrel = np.arange(256)[None, :] - 126
    cur = (tl >= 64).astype(np.int64)
    c["availW"] = (jrel <= cur).astype(np.float32)
    c["fbW"] = np.where((jrel == cur) | (jrel == cur - 1), 100.0, np.where(jrel > cur, -1.0, 0.0)).astype(np.float32)
    return c


_PROGS = {}


def _ca(a):
    return np.ascontiguousarray(a, dtype=np.float32)


def kernel(x, c, ada_w, ada_b, ln_g, ln_b, ret_w_in, ret_w_o, kv_ada_w, kv_ada_b, nsa_w_kv,
           cmp_pe, cmp_w1, cmp_b1, cmp_w2, nsa_w_in, nsa_w_o, peer_w_q, peer_keys, peer_u, peer_v):
    f = lambda a: np.asarray(a, dtype=np.float32)
    x, c, ada_w, ada_b, ln_g, ln_b = f(x), f(c), f(ada_w), f(ada_b), f(ln_g), f(ln_b)
    ret_w_in, ret_w_o, kv_ada_w, kv_ada_b, nsa_w_kv = f(ret_w_in), f(ret_w_o), f(kv_ada_w), f(kv_ada_b), f(nsa_w_kv)
    cmp_pe, cmp_w1, cmp_b1, cmp_w2 = f(cmp_pe), f(cmp_w1), f(cmp_b1), f(cmp_w2)
    nsa_w_in, nsa_w_o, peer_w_q, peer_keys, peer_u, peer_v = (f(nsa_w_in), f(nsa_w_o), f(peer_w_q), f(peer_keys),
                                                              f(peer_u), f(peer_v))
    K = _consts()
    shared = dict(
        ada_w=ada_w, ada_b=_ca(ada_b[:, None, :]), ln_g=ln_g, ln_b=ln_b, ret_w_o=ret_w_o,
        cos=K["cos"], sin=K["sin"], kv_ada_w=kv_ada_w, kv_ada_b=_ca(kv_ada_b[None, :]), w_kv=nsa_w_kv,
        nsa_w_o=nsa_w_o, peT=_ca(cmp_pe.transpose(0, 2, 1)), w1=cmp_w1,
        b1T=_ca(cmp_b1.reshape(2, 2, 128).transpose(2, 0, 1)),
        w2l=_ca(cmp_w2.reshape(2, 2, 128, 64).transpose(2, 0, 1, 3)),
        c2s=K["c2s"], Eall=K["Eall"], tri=K["tri"], low=K["low"], Acmp=K["Acmp"], availW=K["availW"], fbW=K["fbW"],
        w_q=peer_w_q, keysT=_ca(peer_keys.reshape(DEPTH, 16, 128, 128).transpose(0, 3, 1, 2)),
        uT=_ca(peer_u.reshape(DEPTH, 128, 128, 8, 128).transpose(0, 1, 4, 3, 2)), v=peer_v,
        ident=K["ident"], iota=K["iota"])
    maps = []
    for i in range(NCORES):
        b, j = divmod(i, 4)
        d = dict(shared)
        d.update(
            x_full=_ca(x[b]), xs0=_ca(x[b, j * TOK:(j + 1) * TOK]), cT=_ca(c[b].reshape(8, 128).T),
            wqk=_ca(np.concatenate([ret_w_in[:, :, j * 256:(j + 1) * 256],
                                    ret_w_in[:, :, 1024 + j * 256:1024 + (j + 1) * 256]], axis=2)),
            wv=_ca(ret_w_in[:, :, 2048 + j * 512:2048 + (j + 1) * 512]),
            wg=_ca(ret_w_in[:, :, 4096 + j * 512:4096 + (j + 1) * 512]),
            decT=K["decT%d" % j], cols=K["cols%d" % j],
            wq=_ca(nsa_w_in[:, :, j * 256:(j + 1) * 256]),
            wgate=_ca(nsa_w_in[:, :, 1024 + j * 12:1024 + (j + 1) * 12]))
        maps.append(d)
    if "fused" not in _PROGS:
        _PROGS["fused"] = build_fused()[0]
    res = run_bass_kernel_spmd(_PROGS["fused"], maps, core_ids=list(range(NCORES))).results
    out = np.stack([np.concatenate([res[4 * b + j]["x_out"] for j in range(4)], axis=0) for b in range(B)])
    return out.astype(np.float32)
```

```python
from contextlib import contextmanager, ExitStack
import numpy as np
import concourse.bass as bass
import concourse.mybir as mybir
from concourse.bass_utils import run_bass_kernel_spmd

F32 = mybir.dt.float32
BF16 = mybir.dt.bfloat16
I32 = mybir.dt.int32
U32 = mybir.dt.uint32
ALU = mybir.AluOpType
AF = mybir.ActivationFunctionType
AX = mybir.AxisListType

D = 1024
B = 2
S = 8192
DEPTH = 4
ALPHA = (2.0 * DEPTH) ** 0.25
LN_EPS = 1e-5
NCORES = 8
TOK = 2048
GT = 256

WRITE_KW = ("out", "accum_out", "out_max", "out_indices", "ap")


class _Buf:
    __slots__ = ("w", "r", "ws")

    def __init__(self):
        self.w = None
        self.r = {}
        self.ws = None


class Prog:
    ENGS = ("pe", "dve", "act", "pool", "sp")

    def __init__(self, nc, n_dma_sems=48):
        self.nc = nc
        self.eng = dict(pe=nc.tensor, dve=nc.vector, act=nc.scalar, pool=nc.gpsimd, sp=nc.sync)
        self.sem = {e: nc.alloc_semaphore("sem_" + e) for e in self.ENGS}
        self.cnt = {e: 0 for e in self.ENGS}
        self.seen = {}
        self.dsem = [nc.alloc_semaphore("dsem%d" % i) for i in range(n_dma_sems)]
        self.dcnt = [0] * n_dma_sems
        self.drr = 0
        self.dseen = {}
        self.bufs = {}
        self.untracked = set()
        self.multi = set()
        self.csem = [nc.alloc_semaphore("csem%d" % i) for i in range(4)]
        self.ccnt = [0] * 4
        self.crr = 0
        self.n_inst = 0
        self.uid = 0
        self.stacks = [ExitStack()]

    def sb(self, name, shape, dtype=F32):
        self.uid += 1
        t = self.stacks[-1].enter_context(
            self.nc.sbuf_tensor("%s_%d" % (name, self.uid), list(shape), dtype))
        return t.ap()

    def ps(self, name, shape=(128, 512), dtype=F32):
        self.uid += 1
        t = self.stacks[-1].enter_context(
            self.nc.psum_tensor("%s_%d" % (name, self.uid), list(shape), dtype))
        return t.ap()

    @contextmanager
    def scope(self):
        st = ExitStack()
        self.stacks.append(st)
        try:
            yield
        finally:
            self.barrier()
            self.stacks.pop()
            st.close()

    def dram(self, name, shape, dtype=F32, kind="Internal", track=True):
        if kind == "Internal":
            t = self.nc.dram_tensor(name, list(shape), dtype)
        else:
            t = self.nc.dram_tensor(name, list(shape), dtype, kind=kind)
        ap = t.ap()
        if not track:
            self.untracked.add(ap.tensor.name)
        elif kind == "Internal":
            self.multi.add(ap.tensor.name)
        return ap

    def _buf(self, ap):
        n = ap.tensor.name
        if n in self.untracked:
            return None
        b = self.bufs.get(n)
        if b is None:
            b = self.bufs[n] = _Buf()
            if n in self.multi:
                b.ws = []
        return b

    def _wait(self, e, ev):
        if ev[0] == "cc":
            _, si, v = ev
            if self.dseen.get((e, "c", si), 0) >= v:
                return
            self.dseen[(e, "c", si)] = v
            self.eng[e].wait_ge(self.csem[si], v)
            return
        if ev[0] == "eng":
            _, pe, n = ev
            if e == "pe" and pe == "pe":
                return
            if self.seen.get((e, pe), 0) >= n:
                return
            self.seen[(e, pe)] = n
            self.eng[e].wait_ge(self.sem[pe], n)
        else:
            _, si, v = ev
            if self.dseen.get((e, si), 0) >= v:
                return
            self.dseen[(e, si)] = v
            self.eng[e].wait_ge(self.dsem[si], v)

    def _deps(self, e, reads, writes):
        evs = []
        for ap in reads:
            b = self._buf(ap)
            if b is None:
                continue
            if b.ws is not None:
                evs.extend(b.ws)
            elif b.w is not None:
                evs.append(b.w)
        for ap in writes:
            b = self._buf(ap)
            if b is None:
                continue
            if b.ws is None and b.w is not None:
                evs.append(b.w)
            evs.extend(b.r.values())
        for ev in evs:
            self._wait(e, ev)

    def _record(self, ev, key, reads, writes):
        for ap in reads:
            b = self._buf(ap)
            if b is not None:
                b.r[key] = ev
        for ap in writes:
            b = self._buf(ap)
            if b is not None:
                if b.ws is not None:
                    b.ws.append(ev)
                else:
                    b.w = ev
                b.r = {}

    def collective(self, kind, groups, in_ap, out_ap):
        self._deps("pool", [in_ap], [out_ap])
        si = self.crr
        self.crr = (self.crr + 1) % len(self.csem)
        if self.ccnt[si] > 0:
            self._wait("pool", ("cc", si, self.ccnt[si]))
        ins = self.nc.gpsimd.collective_compute(kind, ALU.bypass, replica_groups=groups,
                                                ins=[in_ap.opt()], outs=[out_ap.opt()])
        self.ccnt[si] += 1
        ins.then_inc(self.csem[si], 1)
        ev = ("cc", si, self.ccnt[si])
        self._record(ev, ("c", si), [in_ap], [out_ap])
        self.n_inst += 1
        return ev

    @staticmethod
    def _split(args, kw):
        reads, writes = [], []
        for k, v in kw.items():
            if isinstance(v, bass.AP):
                (writes if k in WRITE_KW else reads).append(v)
        for v in args:
            if isinstance(v, bass.AP):
                reads.append(v)
        return reads, writes

    def op(self, e, method, *args, **kw):
        nowaw = kw.pop("_nowaw", False)
        reads, writes = self._split(args, kw)
        if nowaw:
            evs = []
            for ap in reads:
                b = self._buf(ap)
                if b is None:
                    continue
                if b.ws is not None:
                    evs.extend(b.ws)
                elif b.w is not None:
                    evs.append(b.w)
            for ap in writes:
                b = self._buf(ap)
                if b is None:
                    continue
                if b.ws is None and b.w is not None and not (b.w[0] == "eng" and b.w[1] == e):
                    evs.append(b.w)
                for ev in b.r.values():
                    if not (ev[0] == "eng" and ev[1] == e):
                        evs.append(ev)
            for ev in evs:
                self._wait(e, ev)
        else:
            self._deps(e, reads, writes)
        ins = getattr(self.eng[e], method)(*args, **kw)
        self.cnt[e] += 1
        ins.then_inc(self.sem[e], 1)
        ev = ("eng", e, self.cnt[e])
        self._record(ev, e, reads, writes)
        self.n_inst += 1
        return ev

    def dma(self, q, out, in_, **kw):
        reads, writes = [in_], [out]
        self._deps(q, reads, writes)
        si = self.drr
        self.drr = (self.drr + 1) % len(self.dsem)
        if self.dcnt[si] > 0:
            self._wait(q, ("dma", si, self.dcnt[si]))
        ins = self.eng[q].dma_start(out=out, in_=in_, **kw)
        self.dcnt[si] += 16
        ins.then_inc(self.dsem[si], 16)
        ev = ("dma", si, self.dcnt[si])
        self._record(ev, ("d", si), reads, writes)
        self.n_inst += 1
        return ev

    def barrier(self, engs=None, final=False):
        sp = "sp"
        for si, v in enumerate(self.dcnt):
            if v > 0 and not any(self.dseen.get((e, si), 0) >= v for e in self.ENGS):
                self._wait(sp, ("dma", si, v))
        for si, v in enumerate(self.ccnt):
            if final and v > 0 and not any(self.dseen.get((e, "c", si), 0) >= v for e in self.ENGS):
                self._wait(sp, ("cc", si, v))
        for pe in self.ENGS:
            if pe != sp and self.cnt[pe] > 0:
                self._wait(sp, ("eng", pe, self.cnt[pe]))
        ins = self.eng[sp].nop()
        self.cnt[sp] += 1
        ins.then_inc(self.sem[sp], 1)
        for e in self.ENGS:
            if e != sp:
                self._wait(e, ("eng", sp, self.cnt[sp]))
        for e in self.ENGS:
            for pe in self.ENGS:
                self.seen[(e, pe)] = self.cnt[pe]
            for si, v in enumerate(self.dcnt):
                self.dseen[(e, si)] = v
            if final:
                for si, v in enumerate(self.ccnt):
                    self.dseen[(e, "c", si)] = v
        for b in self.bufs.values():
            if b.w is not None and b.w[0] != "cc":
                b.w = None
            b.r = {k: ev for k, ev in b.r.items() if ev[0] == "cc"}
            if b.ws is not None:
                b.ws = [ev for ev in b.ws if ev[0] == "cc"]

    def finish(self):
        self.barrier(final=True)

    def mm(self, out, lhsT, rhs, start=True, stop=True):
        return self.op("pe", "matmul", out=out, lhsT=lhsT, rhs=rhs, start=start, stop=stop)

    def tr(self, out, in_, ident):
        return self.op("pe", "transpose", out=out, in_=in_, identity=ident)

    def tt(self, out, in0, in1, op, e="dve", nowaw=False):
        return self.op(e, "tensor_tensor", out=out, in0=in0, in1=in1, op=op, _nowaw=nowaw)

    def ts(self, out, in0, s1, op0, s2=None, op1=None, e="dve", nowaw=False):
        if op1 is None:
            return self.op(e, "tensor_scalar", out=out, in0=in0, scalar1=s1, scalar2=None, op0=op0, _nowaw=nowaw)
        return self.op(e, "tensor_scalar", out=out, in0=in0, scalar1=s1, scalar2=s2, op0=op0, op1=op1,
                       _nowaw=nowaw)

    def cp(self, out, in_, e="dve", nowaw=False):
        return self.op(e, "tensor_copy", out=out, in_=in_, _nowaw=nowaw)

    def act(self, out, in_, func, nowaw=False, **kw):
        return self.op("act", "activation", out=out, in_=in_, func=func, _nowaw=nowaw, **kw)


def emit_mod(P, cT_d, w_d, b_d, ident, col0, ncols, pbank, pbank2, mod_bc, modT=None):
    with P.scope():
        cT = P.sb("cT", [128, 8])
        P.dma("sp", out=cT, in_=cT_d)
        ca = P.sb("ca", [128, 8])
        P.act(ca, cT, AF.Silu)
        crep = P.sb("crep", [128, 8, 128])
        for kc in range(8):
            P.cp(crep[:, kc, :], ca[:, kc:kc + 1].to_broadcast([128, 128]))
        wv = w_d.rearrange("(kc p) n -> p kc n", p=128)
        awb = [P.sb("aw%d" % i, [128, 8, 512]) for i in range(2)]
        bbb = [P.sb("bb%d" % i, [128, 512]) for i in range(2)]
        pbs = [pbank, pbank2]
        for ci in range(ncols // 512):
            n0 = col0 + ci * 512
            aw = awb[ci % 2]
            bb = bbb[ci % 2]
            pb = pbs[ci % 2]
            P.dma("sp", out=aw, in_=wv[:, :, n0:n0 + 512])
            P.dma("sp", out=bb, in_=b_d[0:1, n0:n0 + 512].to_broadcast([128, 512]))
            for kc in range(8):
                P.mm(pb, crep[:, kc, :], aw[:, kc, :], start=(kc == 0), stop=(kc == 7))
            P.tt(mod_bc[:, ci * 512:(ci + 1) * 512], pb, bb, ALU.add)
        if modT is not None:
            for j in range(ncols // 128):
                pb = pbs[j % 2]
                P.tr(pb[:, 0:128], mod_bc[:, j * 128:(j + 1) * 128], ident)
                P.cp(modT[:, j:j + 1], pb[:, 0:1])


def emit_ln(P, out, pre, g_bc, b_bc, scr):
    st, mv, rs = scr
    P.op("dve", "bn_stats", out=st[:, 0:6], in_=pre[:, 0:512])
    P.op("dve", "bn_stats", out=st[:, 6:12], in_=pre[:, 512:1024])
    P.op("dve", "bn_aggr", out=mv, in_=st)
    P.ts(rs, mv[:, 1:2], LN_EPS, ALU.add)
    P.act(rs, rs, AF.Sqrt)
    P.op("dve", "reciprocal", out=rs, in_=rs)
    P.ts(out, pre, mv[:, 0:1], ALU.subtract, rs[:, 0:1], ALU.mult)
    P.tt(out, out, g_bc, ALU.mult)
    P.tt(out, out, b_bc, ALU.add)


def emit_top16(P, probs):
    for (vals, idxs, src, tmp) in probs:
        P.op("dve", "max", out=vals[:, 0:8], in_=src, _nowaw=True)
    for (vals, idxs, src, tmp) in probs:
        P.op("dve", "max_index", out=idxs[:, 0:8], in_max=vals[:, 0:8], in_values=src, _nowaw=True)
    for (vals, idxs, src, tmp) in probs:
        P.op("dve", "match_replace", out=tmp, in_to_replace=vals[:, 0:8], in_values=src, imm_value=-1e30,
             _nowaw=True)
    for (vals, idxs, src, tmp) in probs:
        P.op("dve", "max", out=vals[:, 8:16], in_=tmp, _nowaw=True)
    for (vals, idxs, src, tmp) in probs:
        P.op("dve", "max_index", out=idxs[:, 8:16], in_max=vals[:, 8:16], in_values=tmp, _nowaw=True)


def emit_tok(P, pb, ident, iota, A, KO, with_kv, ngroups=TOK // GT):
  with P.scope():
    KC = KO // 128
    cT_d, aw_d, ab_d, lng_d, lnb_d = A["cT"], A["ada_w"], A["ada_b"], A["ln_g"], A["ln_b"]
    wo_d, wq_d, keysT_d = A["w_o"], A["w_q"], A["keysT"]
    if with_kv:
        kvw_d, kvb_d, wkv_d = A["kv_ada_w"], A["kv_ada_b"], A["w_kv"]
    keysT = P.sb("keysT", [128, 16, 128])
    P.dma("sp", out=keysT, in_=keysT_d)
    lng = [P.sb("lng%d" % i, [128, D]) for i in range(2)]
    lnb = [P.sb("lnb%d" % i, [128, D]) for i in range(2)]
    for i in range(2):
        P.dma("sp", out=lng[i], in_=lng_d[i:i + 1, :].to_broadcast([128, D]))
        P.dma("sp", out=lnb[i], in_=lnb_d[i:i + 1, :].to_broadcast([128, D]))
    mod_bc = P.sb("mod_bc", [128, 4096])
    modT = P.sb("modT", [128, 32])
    emit_mod(P, cT_d, aw_d, ab_d, ident, 2048, 4096, pb[0], pb[1], mod_bc, modT)
    g1_bc = mod_bc[:, 0:1024]
    g2_bc = mod_bc[:, 3072:4096]
    sh2T = modT[:, 8:16]
    sc2T = P.sb("sc2T", [128, 8])
    P.ts(sc2T, modT[:, 16:24], 1.0, ALU.add)
    if with_kv:
        kvm_bc = P.sb("kvm_bc", [128, 2048])
        kvmT = P.sb("kvmT", [128, 16])
        emit_mod(P, cT_d, kvw_d, kvb_d, ident, 0, 2048, pb[0], pb[1], kvm_bc, kvmT)
        kvshT = kvmT[:, 0:8]
        kvscT = P.sb("kvscT", [128, 8])
        P.ts(kvscT, kvmT[:, 8:16], 1.0, ALU.add)

    lnscr = (P.sb("lnst", [128, 12]), P.sb("lnmv", [128, 2]), P.sb("lnrs", [128, 1]))
    x1 = [P.sb("x1_%d" % i, [128, D]) for i in range(2)]
    h2T = P.sb("h2T", [128, 8, GT])
    h2bf = P.sb("h2bf", [128, 8, GT], BF16)
    i1T = P.sb("i1T", [128, GT], BF16)
    i2T = P.sb("i2T", [128, GT], BF16)
    wT = P.sb("wT", [128, GT], BF16)
    iota_bf = P.sb("iota_bf", [128, 128], BF16)
    P.cp(iota_bf, iota)

    P.uid += 1
    wo_bf = P.dram("wo_bf_%d" % P.uid, [KO, D], BF16)
    wq_bf = P.dram("wq_bf_%d" % P.uid, [D, 2048], BF16)
    for kb in range(KC // 4):
        P.dma("pool", out=wo_bf[kb * 512:(kb + 1) * 512, :], in_=wo_d[kb * 512:(kb + 1) * 512, :])
    for kb in range(4):
        P.dma("pool", out=wq_bf[kb * 256:(kb + 1) * 256, :], in_=wq_d[kb * 256:(kb + 1) * 256, :])
    wo_v = wo_bf.rearrange("(kc p) n -> p kc n", p=128)
    wq_v = wq_bf.rearrange("(kc p) n -> p kc n", p=128)
    xt = [P.sb("xt%d" % i, [128, D]) for i in range(2)]
    ot = [P.sb("ot%d" % i, [128, KO]) for i in range(2)]
    wob0 = P.sb("wob0", [128, 4, D], BF16)

    def prefetch(g):
        for tt in range(2):
            P.dma("sp", out=xt[tt], in_=A["x_tile"](g * 2 + tt))
            for (oap, c0, wd) in A["o_pieces"](g * 2 + tt):
                P.dma("sp", out=ot[tt][:, c0:c0 + wd], in_=oap)
        P.dma("sp", out=wob0, in_=wo_v[:, 0:4, :])

    prefetch(0)

    for grp in range(ngroups):
        t0 = grp * GT
        with P.scope():
            oT = P.sb("oT", [128, KC, GT], BF16)
            wob = [P.sb("wob%d" % i, [128, 4, D], BF16) for i in range(2)]
            pre = P.sb("pre", [128, D])
            for tt in range(2):
                for k4 in range(KC // 4):
                    bank = pb[4 + (k4 % 2)]
                    for q in range(4):
                        kc = k4 * 4 + q
                        P.tr(bank[:, q * 128:(q + 1) * 128], ot[tt][:, kc * 128:(kc + 1) * 128], ident)
                    P.cp(oT[:, k4 * 4:(k4 + 1) * 4, tt * 128:(tt + 1) * 128],
                         bank.rearrange("p (q t) -> p q t", q=4), nowaw=True)
            for kb in range(KC // 4):
                if kb == 0:
                    wb = wob0
                else:
                    wb = wob[kb % 2]
                    P.dma("sp", out=wb, in_=wo_v[:, kb * 4:(kb + 1) * 4, :])
                for tt in range(2):
                    for half in range(2):
                        for q in range(4):
                            kc = kb * 4 + q
                            P.mm(pb[tt * 2 + half], oT[:, kc, tt * 128:(tt + 1) * 128],
                                 wb[:, q, half * 512:(half + 1) * 512], start=(kc == 0), stop=(kc == KC - 1))
            for tt in range(2):
                for half in range(2):
                    sl = slice(half * 512, (half + 1) * 512)
                    P.tt(pre[:, sl], pb[tt * 2 + half], g1_bc[:, sl], ALU.mult)
                P.op("dve", "scalar_tensor_tensor", out=pre, in0=xt[tt], scalar=ALPHA, in1=pre,
                     op0=ALU.mult, op1=ALU.add)
                emit_ln(P, x1[tt], pre, lng[0], lnb[0], lnscr)
        for tt in range(2):
            for k4 in range(2):
                bank = pb[4 + k4]
                for q in range(4):
                    kc = k4 * 4 + q
                    P.tr(bank[:, q * 128:(q + 1) * 128], x1[tt][:, kc * 128:(kc + 1) * 128], ident)
                for q in range(4):
                    kc = k4 * 4 + q
                    P.ts(h2T[:, kc, tt * 128:(tt + 1) * 128], bank[:, q * 128:(q + 1) * 128],
                         sc2T[:, kc:kc + 1], ALU.mult, sh2T[:, kc:kc + 1], ALU.add, nowaw=True)
        P.cp(h2bf, h2T)
        with P.scope():
            wqb = [P.sb("wqb%d" % i, [128, 8, 128], BF16) for i in range(3)]
            qT = P.sb("qT", [128, 16, GT])
            for g in range(16):
                wb = wqb[g % 3]
                P.dma("sp", out=wb, in_=wq_v[:, :, g * 128:(g + 1) * 128])
                bank = pb[4 + g % 2]
                for kc in range(8):
                    P.mm(bank[:, 0:GT], wb[:, kc, :], h2bf[:, kc, :], start=(kc == 0), stop=(kc == 7))
                if g % 2 == 0:
                    P.cp(qT[:, g, :], bank[:, 0:GT])
                else:
                    P.act(qT[:, g, :], bank[:, 0:GT], AF.Copy)
            s_sb = P.sb("s_sb", [128, 16, 128])
            tmp = P.sb("tk_tmp", [128, 16, 128])
            tmpc = P.sb("tk_tmpc", [128, 8, 256])
            sv = P.sb("sv", [128, 16, 16])
            si = P.sb("si", [128, 16, 16], U32)
            sif = P.sb("sif", [128, 16, 16])
            comb = P.sb("comb", [128, 8, 256])
            cv = P.sb("cv", [128, 8, 16])
            ci = P.sb("ci", [128, 8, 16], U32)
            chi = P.sb("chi", [128, 8, 16], U32)
            clo = P.sb("clo", [128, 8, 16], U32)
            chif = P.sb("chif", [128, 8, 16])
            clof = P.sb("clof", [128, 8, 16])
            ee = P.sb("ee", [128, 8, 16])
            zz = P.sb("zz", [128, 8])
            eq = P.sb("eq", [128, 8, 16, 16])
            i1f = P.sb("i1f", [128, 8, 16])
            i2f = P.sb("i2f", [128, 8, 16])
            ww = P.sb("ww", [128, 8, 16])
            for tt in range(2):
                tsl = slice(tt * 128, (tt + 1) * 128)
                for g4 in range(4):
                    bank = pb[g4]
                    for q in range(4):
                        g = g4 * 4 + q
                        P.mm(bank[:, q * 128:(q + 1) * 128], qT[:, g, tsl], keysT[:, g, :])
                    P.cp(s_sb[:, g4 * 4:(g4 + 1) * 4, :], bank.rearrange("p (q n) -> p q n", q=4), nowaw=True)
                emit_top16(P, [(sv[:, g, :], si[:, g, :], s_sb[:, g, :], tmp[:, g, :]) for g in range(16)])
                P.cp(sif, si)
                svv = sv.rearrange("p (h c) k -> p h c k", c=2)
                sfv = sif.rearrange("p (h c) k -> p h c k", c=2)
                c4 = comb.rearrange("p h (i j) -> p h i j", j=16)
                P.tt(c4, svv[:, :, 0, :].unsqueeze(3).to_broadcast([128, 8, 16, 16]),
                     svv[:, :, 1, :].unsqueeze(2).to_broadcast([128, 8, 16, 16]), ALU.add)
                emit_top16(P, [(cv[:, h, :], ci[:, h, :], comb[:, h, :], tmpc[:, h, :]) for h in range(8)])
                P.tt(ee, cv, cv[:, :, 0:1].to_broadcast([128, 8, 16]), ALU.subtract)
                P.act(ee, ee, AF.Exp)
                P.op("dve", "tensor_reduce", out=zz, in_=ee, axis=AX.X, op=ALU.add)
                P.op("dve", "reciprocal", out=zz, in_=zz)
                P.tt(ww, ee, zz.unsqueeze(2).to_broadcast([128, 8, 16]), ALU.mult)
                P.ts(chi, ci, 4, ALU.logical_shift_right)
                P.ts(clo, ci, 15, ALU.bitwise_and)
                P.cp(chif, chi)
                P.cp(clof, clo)
                io16 = iota[:, 0:16].unsqueeze(1).unsqueeze(1).to_broadcast([128, 8, 16, 16])
                for (cf, cc, dst) in ((chif, 0, i1f), (clof, 1, i2f)):
                    P.tt(eq, cf.unsqueeze(3).to_broadcast([128, 8, 16, 16]), io16, ALU.is_equal)
                    P.tt(eq, eq, sfv[:, :, cc, :].unsqueeze(2).to_broadcast([128, 8, 16, 16]), ALU.mult)
                    P.op("dve", "tensor_reduce", out=dst, in_=eq, axis=AX.X, op=ALU.add)
                for (src, dstT, bank) in ((i1f, i1T, pb[4]), (i2f, i2T, pb[5]), (ww, wT, pb[6])):
                    P.tr(bank[:, 0:128], src.rearrange("p h k -> p (h k)"), ident)
                    P.cp(dstT[:, tsl], bank[:, 0:128])
        wt_scope = P.scope()
        wt_scope.__enter__()
        Wt = P.sb("Wt", [128, GT, 128], BF16)
        with P.scope():
            SBK = 32
            d1 = [P.sb("d1_%d" % i, [128, SBK, 128], BF16) for i in range(2)]
            d2 = [P.sb("d2_%d" % i, [128, SBK, 128], BF16) for i in range(2)]
            for sbk in range(GT // SBK):
                a1 = d1[sbk % 2]
                a2 = d2[sbk % 2]
                ts0 = sbk * SBK
                for t in range(SBK):
                    P.op("dve", "tensor_scalar", out=a1[:, t, :], in0=iota_bf, scalar1=i1T[:, ts0 + t:ts0 + t + 1],
                         scalar2=wT[:, ts0 + t:ts0 + t + 1], op0=ALU.is_equal, op1=ALU.mult, _nowaw=True)
                    P.op("dve", "tensor_scalar", out=a2[:, t, :], in0=iota_bf, scalar1=i2T[:, ts0 + t:ts0 + t + 1],
                         scalar2=None, op0=ALU.is_equal, _nowaw=True)
                for t4 in range(SBK // 4):
                    bank = pb[4 + (t4 % 4)]
                    for q in range(4):
                        t = t4 * 4 + q
                        P.mm(bank[:, q * 128:(q + 1) * 128], a2[:, t, :], a1[:, t, :])
                    tg = ts0 + t4 * 4
                    dst = Wt[:, tg:tg + 4, :]
                    src = bank.rearrange("p (t n) -> p t n", t=4)
                    P.act(dst, src, AF.Copy, nowaw=True)
        with P.scope():
            NBP = 3
            LA3 = 2
            utb = [P.sb("utb%d" % i, [128, 2, 8, 128], BF16) for i in range(NBP)]
            vtb = [P.sb("vtb%d" % i, [128, 2, D], BF16) for i in range(NBP)]
            gl = [P.sb("gl%d" % i, [128, GT]) for i in range(4)]
            cf = [P.sb("cf%d" % i, [128, GT], BF16) for i in range(4)]

            def c3_s1(n1):
                q_, c_ = n1 // 2, n1 % 2
                if c_ == 0:
                    P.dma(A.get("tab_q", "pool"), out=utb[q_ % NBP].rearrange("p c k n -> p c (k n)"),
                          in_=A["uT_pair"](q_))
                    P.dma(A.get("tab_q", "pool"), out=vtb[q_ % NBP], in_=A["v_pair"](q_))
                ub = utb[q_ % NBP][:, c_]
                pa = pb[4 + n1 % 4]
                for kc in range(8):
                    P.mm(pa[:, 0:GT], ub[:, kc, :], h2bf[:, kc, :], start=(kc == 0), stop=(kc == 7))
                P.act(gl[n1 % 4], pa[:, 0:GT], AF.Gelu_apprx_tanh)
                P.tt(cf[n1 % 4], gl[n1 % 4], Wt[:, :, n1], ALU.mult)

            def c3_s2(n1):
                c_ = cf[n1 % 4]
                vb = vtb[(n1 // 2) % NBP][:, n1 % 2, :]
                for tt in range(2):
                    for half in range(2):
                        P.mm(pb[tt * 2 + half], c_[:, tt * 128:(tt + 1) * 128],
                             vb[:, half * 512:(half + 1) * 512], start=(n1 == 0), stop=(n1 == 127))

            for it in range(128 + LA3):
                if it == 8 and grp + 1 < ngroups:
                    prefetch(grp + 1)
                if it < 128:
                    c3_s1(it)
                if it >= LA3:
                    c3_s2(it - LA3)
        wt_scope.__exit__(None, None, None)
        with P.scope():
            pre = P.sb("pre2", [128, D])
            x2 = [P.sb("x2_%d" % i, [128, D]) for i in range(2)]
            for tt in range(2):
                for half in range(2):
                    sl = slice(half * 512, (half + 1) * 512)
                    P.tt(pre[:, sl], pb[tt * 2 + half], g2_bc[:, sl], ALU.mult)
                P.op("dve", "scalar_tensor_tensor", out=pre, in0=x1[tt], scalar=ALPHA, in1=pre,
                     op0=ALU.mult, op1=ALU.add)
                emit_ln(P, x2[tt], pre, lng[1], lnb[1], lnscr)
                for xo in A["x_out"](grp * 2 + tt):
                    P.dma("sp", out=xo, in_=x2[tt])
            if "after_group" in A:
                A["after_group"](grp)
            if with_kv:
                hkT = P.sb("hkT", [128, 8, GT])
                wkv = P.sb("wkv", [128, 8, 1536])
                P.dma("sp", out=wkv, in_=wkv_d.rearrange("(kc p) n -> p kc n", p=128))
                for tt in range(2):
                    for k4 in range(2):
                        bank = pb[4 + k4]
                        for q in range(4):
                            kc = k4 * 4 + q
                            P.tr(bank[:, q * 128:(q + 1) * 128], x2[tt][:, kc * 128:(kc + 1) * 128], ident)
                        for q in range(4):
                            kc = k4 * 4 + q
                            P.ts(hkT[:, kc, tt * 128:(tt + 1) * 128], bank[:, q * 128:(q + 1) * 128],
                                 kvscT[:, kc:kc + 1], ALU.mult, kvshT[:, kc:kc + 1], ALU.add)
                for tt in range(2):
                    vt_sb = P.sb("vt_sb%d" % tt, [128, 2, 256])
                    for wi, c0 in enumerate((768, 1280)):
                        bank = pb[wi]
                        for kc in range(8):
                            P.mm(bank[:, 0:256], hkT[:, kc, tt * 128:(tt + 1) * 128], wkv[:, kc, c0:c0 + 256],
                                 start=(kc == 0), stop=(kc == 7))
                        P.cp(vt_sb[:, wi, :], bank[:, 0:256])
                        dst = A["vtok_out"](wi)[:, t0 + tt * 128:t0 + (tt + 1) * 128, :].rearrange("g t d -> t g d")
                        P.dma("sp", out=dst, in_=vt_sb[:, wi, :].rearrange("p (g d) -> p g d", g=4))
                for wi, c0 in enumerate((0, 256, 512, 1024)):
                    for cb in range(2):
                        bank = pb[2 + (wi * 2 + cb) % 2]
                        for kc in range(8):
                            P.mm(bank[:, 0:GT], wkv[:, kc, c0 + cb * 128:c0 + (cb + 1) * 128], hkT[:, kc, :],
                                 start=(kc == 0), stop=(kc == 7))
                        kt_sb = P.sb("kt_sb%d_%d" % (wi, cb), [128, GT])
                        P.cp(kt_sb, bank[:, 0:GT])
                        for gg in range(2):
                            P.dma("sp", out=A["kvT_out"](wi, cb * 2 + gg)[:, t0:t0 + GT],
                                  in_=kt_sb[gg * 64:(gg + 1) * 64, :])


def emit_hT(P, hT, xt, scT, shT, ident, bank0, bank1):
    for k4 in range(2):
        bank = (bank0, bank1)[k4]
        for q in range(4):
            kc = k4 * 4 + q
            P.tr(bank[:, q * 128:(q + 1) * 128], xt[:, kc * 128:(kc + 1) * 128], ident)
        for q in range(4):
            kc = k4 * 4 + q
            P.ts(hT[:, kc, :], bank[:, q * 128:(q + 1) * 128], scT[:, kc:kc + 1], ALU.mult,
                 shT[:, kc:kc + 1], ALU.add, nowaw=True)


def emit_retmix(P, pb, ident, A, nchunks=S // 128):
  with P.scope():
    cT_d, aw_d, ab_d = A["cT"], A["ada_w"], A["ada_b"]
    wqk_d, wv_d, wg_d, cos_d, sin_d = A["wqk"], A["wv"], A["wg"], A["cos"], A["sin"]
    decT = P.sb("decT", [128, 128])
    P.dma("sp", out=decT, in_=A["decT"])
    cols = P.sb("cols", [128, 4])
    P.dma("sp", out=cols, in_=A["cols"])
    mod_bc = P.sb("mod_bc", [128, 2048])
    modT = P.sb("modT", [128, 16])
    emit_mod(P, cT_d, aw_d, ab_d, ident, 0, 2048, pb[0], pb[1], mod_bc, modT)
    shT = modT[:, 0:8]
    scT = P.sb("scT", [128, 8])
    P.ts(scT, modT[:, 8:16], 1.0, ALU.add)
    wqk = P.sb("wqk", [128, 8, 512], BF16)
    wv = P.sb("wv", [128, 8, 512], BF16)
    wg = P.sb("wg", [128, 8, 512], BF16)
    for (w, wd) in ((wqk, wqk_d), (wv, wv_d), (wg, wg_d)):
        P.dma("pool", out=w, in_=wd.rearrange("(kc p) n -> p kc n", p=128))
    state = P.sb("state", [128, 2, 512])
    P.op("dve", "memset", ap=state, constant=0.0)
    NBUF = 2
    mk = lambda n, s: [P.sb("%s%d" % (n, i), s) for i in range(NBUF)]
    xt_, cs_, sn_ = mk("xt", [128, D]), mk("cs", [128, 128]), mk("sn", [128, 128])
    hT_ = [P.sb("hT%d" % i, [128, 8, 128], BF16) for i in range(NBUF)]
    rot_, t1_, t2_ = mk("rot", [128, 2, 2, 128]), mk("t1", [128, 2, 128]), mk("t2", [128, 2, 128])
    qd_, kd_, ks_ = mk("qd", [128, 256]), mk("kd", [128, 256]), mk("ks", [128, 256])
    v_, qkT_, PT_ = mk("v", [128, 512]), mk("qkT", [128, 4, 128]), mk("PT", [128, 128])
    on_, sg_ = mk("on", [128, 512]), mk("sg", [128, 512])
    st_, mv_, rs_ = mk("st", [128, 6]), mk("mv", [128, 2]), mk("rs", [128, 1])
    def ret_load(m):
        j_ = m % NBUF
        rws = slice(m * 128, (m + 1) * 128)
        P.dma("sp", out=xt_[j_], in_=A["x_tile"](m))
        P.dma("sp", out=cs_[j_], in_=cos_d[rws, :])
        P.dma("sp", out=sn_[j_], in_=sin_d[rws, :])

    for n in range(nchunks):
        i = n % NBUF
        xt, hT, cs, sn, rot, t1, t2 = xt_[i], hT_[i], cs_[i], sn_[i], rot_[i], t1_[i], t2_[i]
        qd, kd, ks, v, qkT, PT, on, sg = qd_[i], kd_[i], ks_[i], v_[i], qkT_[i], PT_[i], on_[i], sg_[i]
        st, mv, rs = st_[i], mv_[i], rs_[i]
        rows = slice(n * 128, (n + 1) * 128)
        if "bg" in A:
            A["bg"](n)
        if n == 0:
            ret_load(0)
        emit_hT(P, hT, xt, scT, shT, ident, pb[0], pb[1])
        if n + 1 < nchunks:
            ret_load(n + 1)
        for (bank, w) in ((pb[2], wqk), (pb[3], wv), (pb[4], wg)):
            for kc in range(8):
                P.mm(bank, hT[:, kc, :], w[:, kc, :], start=(kc == 0), stop=(kc == 7))
        P.act(v, pb[3], AF.Copy)
        P.act(sg, pb[4], AF.Silu)
        qk4 = pb[2].rearrange("p (a h d) -> p a h d", a=2, h=2)
        x1 = qk4[:, :, 0, :]
        x2 = qk4[:, :, 1, :]
        csb = cs.unsqueeze(1).to_broadcast([128, 2, 128])
        snb = sn.unsqueeze(1).to_broadcast([128, 2, 128])
        P.tt(t1, x1, csb, ALU.mult)
        P.tt(t2, x2, snb, ALU.mult)
        P.tt(rot[:, :, 0, :], t1, t2, ALU.subtract)
        P.tt(t1, x1, snb, ALU.mult)
        P.tt(t2, x2, csb, ALU.mult)
        P.tt(rot[:, :, 1, :], t1, t2, ALU.add)
        qr = rot[:, 0, :, :].rearrange("p h d -> p (h d)")
        kr = rot[:, 1, :, :].rearrange("p h d -> p (h d)")
        P.ts(qd, qr, cols[:, 0:1], ALU.mult)
        P.ts(kd, kr, cols[:, 1:2], ALU.mult)
        P.ts(ks, kr, 1.0 / 16.0, ALU.mult)
        for dc in range(2):
            P.tr(pb[0][:, dc * 128:(dc + 1) * 128], qd[:, dc * 128:(dc + 1) * 128], ident)
            P.tr(pb[0][:, (2 + dc) * 128:(3 + dc) * 128], ks[:, dc * 128:(dc + 1) * 128], ident)
        P.cp(qkT, pb[0].rearrange("p (a t) -> p a t", a=4))
        for dc in range(2):
            P.mm(pb[1][:, 0:128], qkT[:, 2 + dc, :], qkT[:, dc, :], start=(dc == 0), stop=(dc == 1))
        P.tt(PT, pb[1][:, 0:128], decT, ALU.mult)
        P.mm(pb[5], PT, v, start=True, stop=False)
        for dc in range(2):
            P.mm(pb[5], qkT[:, dc, :], state[:, dc, :], start=False, stop=(dc == 1))
        for dc in range(2):
            P.mm(pb[6 + dc], kd[:, dc * 128:(dc + 1) * 128], v)
        for dc in range(2):
            P.op("dve", "scalar_tensor_tensor", out=state[:, dc, :], in0=state[:, dc, :], scalar=cols[:, 2:3],
                 in1=pb[6 + dc], op0=ALU.mult, op1=ALU.add)
        P.op("dve", "bn_stats", out=st, in_=pb[5])
        P.op("dve", "bn_aggr", out=mv, in_=st)
        P.ts(rs, mv[:, 1:2], LN_EPS, ALU.add)
        P.act(rs, rs, AF.Sqrt)
        P.op("dve", "reciprocal", out=rs, in_=rs)
        P.ts(on, pb[5], mv[:, 0:1], ALU.subtract, rs[:, 0:1], ALU.mult)
        P.tt(on, on, sg, ALU.mult)
        P.dma("sp", out=A["o_out"](n), in_=on)
        if "after_out" in A:
            A["after_out"](n)


def emit_nsamix(P, pb, ident, A, nqb=S // 128):
  with P.scope():
    SE = nqb * 128
    NCP = SE // 16
    NC = NCP - 1
    NCH = max(1, NCP // 128)
    NR = SE // TOK
    cT_d, aw_d, ab_d, wq_d, wgt_d = A["cT"], A["ada_w"], A["ada_b"], A["wq"], A["wgate"]
    peT_d, w1_d, b1T_d, w2_d, c2s_d, E_d = A["peT"], A["w1"], A["b1T"], A["w2l"], A["c2s"], A["Eall"]
    tri_d, low_d, A_d, av_d, fb_d = A["tri"], A["low"], A["Acmp"], A["availW"], A["fbW"]
    ld = lambda name, shape, src: (lambda t: (P.dma("sp", out=t, in_=src), t)[1])(P.sb(name, shape))
    tri = ld("tri", [128, 128], tri_d)
    low = ld("low", [128, 128], low_d)
    Acmp = ld("Acmp", [128, 128], A_d)
    availW = ld("availW", [128, 256], av_d)
    fbW = ld("fbW", [128, 256], fb_d)
    c2s = ld("c2s", [128, NCH, 128], c2s_d)
    b1T = ld("b1T", [128, 2, 2], b1T_d)
    w2l = ld("w2l", [128, 2, 2, 64], w2_d)
    wq = P.sb("wq", [128, 8, 256], BF16)
    wgt = P.sb("wgt", [128, 8, 12], BF16)
    P.dma("pool", out=wq, in_=wq_d.rearrange("(kc p) n -> p kc n", p=128))
    P.dma("pool", out=wgt, in_=wgt_d.rearrange("(kc p) n -> p kc n", p=128))
    mod_bc = P.sb("mod_bc", [128, 2048])
    modT = P.sb("modT", [128, 16])
    emit_mod(P, cT_d, aw_d, ab_d, ident, 0, 2048, pb[0], pb[1], mod_bc, modT)
    shT = modT[:, 0:8]
    scT = P.sb("scT", [128, 8])
    P.ts(scT, modT[:, 8:16], 1.0, ALU.add)

    kcmpT = P.sb("kcmpT", [64, NCH * 128])
    vcmp = P.sb("vcmp", [128, NCH, 65])
    P.op("dve", "memset", ap=kcmpT, constant=0.0)
    P.op("dve", "memset", ap=vcmp, constant=0.0)
    P.op("dve", "memset", ap=vcmp[:, :, 64:65], constant=1.0)
    with P.scope():
        rawT = P.sb("rawT", [64, SE])
        peT = P.sb("peT", [64, 32])
        w1b = [P.sb("w1b%d" % i, [64, 256]) for i in range(3)]
        hid = P.sb("hid", [128, 2, NCH * 128])
        biasT = P.sb("biasT", [128, 2])
        P.op("dve", "memset", ap=hid, constant=0.0)
        for c in range(2):
            for rk in range(NR):
                P.dma("sp", out=rawT[:, rk * TOK:(rk + 1) * TOK], in_=A["kvT"](c, rk))
            P.dma("sp", out=peT, in_=peT_d[c])
            rv = rawT.rearrange("d (n s) -> d n s", s=16)
            for p in range(32):
                wb = w1b[p % 3]
                P.dma("sp", out=wb, in_=w1_d[c, p * 64:(p + 1) * 64, :])
                xp = rv[:, 0:NC, p] if p < 16 else rv[:, 1:NC + 1, p - 16]
                for hc in range(2):
                    P.mm(pb[hc][:, 0:NC], wb[:, hc * 128:(hc + 1) * 128], xp, start=(p == 0), stop=(p == 31))
                    P.mm(pb[2 + hc][:, 0:1], wb[:, hc * 128:(hc + 1) * 128], peT[:, p:p + 1],
                         start=(p == 0), stop=(p == 31))
            for hc in range(2):
                P.tt(biasT[:, hc:hc + 1], pb[2 + hc][:, 0:1], b1T[:, c, hc:hc + 1], ALU.add)
                P.act(hid[:, hc, 0:NC], pb[hc][:, 0:NC], AF.Gelu_apprx_tanh, bias=biasT[:, hc:hc + 1])
            if c == 0:
                for hc in range(2):
                    P.mm(pb[4][0:64, 0:NC], w2l[:, 0, hc, :], hid[:, hc, 0:NC], start=(hc == 0), stop=(hc == 1))
                P.cp(kcmpT[:, 0:NC], pb[4][0:64, 0:NC])
            else:
                for ch in range(NCH):
                    for hc in range(2):
                        P.mm(pb[4][:, ch * 64:(ch + 1) * 64], hid[:, hc, ch * 128:(ch + 1) * 128], w2l[:, 1, hc, :],
                             start=(hc == 0), stop=(hc == 1))
                    P.cp(vcmp[:, ch, 0:64], pb[4][:, ch * 64:(ch + 1) * 64])

    import os
    KD = BF16 if os.environ.get("NSA_BF", "1") == "1" else F32
    kslcT = P.sb("kslcT", [64, SE], KD)
    kwinT = P.sb("kwinT", [64, SE], KD)
    vslc = P.sb("vslc", [128, nqb, 66], KD)
    vwin = P.sb("vwin", [128, nqb, 66], KD)
    Eall = P.sb("Eall", [128, nqb, 128], KD)
    CPR = TOK // 128
    with P.scope():
        stg = P.sb("stg", [128, nqb * 128])
        for (dst, w) in ((kslcT, 2), (kwinT, 3)):
            for rk in range(NR):
                P.dma("sp", out=stg[0:64, rk * TOK:(rk + 1) * TOK], in_=A["kvT"](w, rk))
            P.cp(dst, stg[0:64, 0:SE])
        for (vt, bi) in ((vslc, 0), (vwin, 1)):
            sv_ = stg[:, 0:nqb * 64].rearrange("p (c d) -> p c d", d=64)
            for rk in range(NR):
                P.dma("sp", out=sv_[:, rk * CPR:(rk + 1) * CPR, :],
                      in_=A["vtok"](bi, rk).rearrange("(c p) d -> p c d", p=128))
            P.op("dve", "memset", ap=vt[:, :, 64:65], constant=1.0)
            P.cp(vt[:, :, 0:64], sv_)
        P.dma("sp", out=stg.rearrange("p (c k) -> p c k", k=128), in_=E_d)
        P.cp(Eall, stg.rearrange("p (c k) -> p c k", k=128))
    NB = 2
    LA = int(os.environ.get('NSA_LA', '4'))
    PSM = os.environ.get('NSA_PSM', '1') == '1'
    mk = lambda n, s, dt=F32: [P.sb("%s%d" % (n, i), s, dt) for i in range(NB)]
    xt_, qT_, gate_ = mk("xt", [128, D]), mk("qT", [64, 512]), mk("gate", [128, 12])
    hT_ = mk("hT", [128, 8, 128], BF16)
    qTb_ = mk("qTb", [64, 512], KD)
    NE = LA + 2
    eT_ = [P.sb("eT%d" % i, [128, 4, 128]) for i in range(NE)]
    pT_ = [P.sb("pT%d" % i, [128, 4, 128]) for i in range(NE)]
    eTb_ = [P.sb("eTb%d" % i, [128, 4, 128], KD) for i in range(NE)]
    pTb_ = [P.sb("pTb%d" % i, [128, 4, 128], KD) for i in range(NE)]
    m_ = [P.sb("m%d" % i, [128, 128]) for i in range(NE)]
    imp_, sc_, tmp_ = mk("imp", [128, 128]), mk("sc", [128, 128]), mk("tmp", [128, 128])
    vals_, thr_, sel_ = mk("vals", [128, 16]), mk("thr", [128, 1]), mk("sel", [128, 128])
    selT_ = mk("selT", [128, 128], KD)
    negsel_ = mk("negsel", [128, 4, 128], KD)
    negsel_cur = [None]
    zc_, gs_, out_ = mk("zc", [128, 4]), mk("gs", [128, 4]), mk("out", [128, 4, 64])
    st_banks = [pb[2], pb[3], pb[0]]
    mk_banks = [pb[4], pb[1], pb[5]]
    ecnt = [0]

    def pipeline(items):
        n = len(items)
        srcs = [None] * n
        for i in range(n + LA):
            if i < n:
                srcs[i] = items[i][0]()
            if i >= LA:
                items[i - LA][1](srcs[i - LA])

    def stage1(q_ap, kT_chunk, bf, mask=None, maskE=None, tri_too=False):
        i = ecnt[0] % NE
        st = st_banks[ecnt[0] % 3]
        ecnt[0] += 1
        fold = maskE is not None and not tri_too and (ecnt[0] % 2 == 1)
        if fold:
            P.mm(st, kT_chunk, q_ap, start=True, stop=False)
            P.mm(st, maskE, negsel_cur[0].rearrange("p r t -> p (r t)"), start=False, stop=True)
            maskE = None
        else:
            P.mm(st, kT_chunk, q_ap)
        eT = (eTb_ if bf else eT_)[i]
        P.act(eT, st.rearrange("p (r t) -> p r t", r=4), AF.Exp)
        if maskE is not None:
            mb = mk_banks[i % 3]
            P.mm(mb[:, 0:128], maskE, selT_cur[0])
            if tri_too:
                P.tt(m_[i], mb[:, 0:128], tri, ALU.mult)
                mask = m_[i]
            elif PSM:
                mask = mb[:, 0:128]
            else:
                P.cp(m_[i], mb[:, 0:128])
                mask = m_[i]
        if mask is None:
            return eT
        pT = (pTb_ if bf else pT_)[i]
        P.tt(pT, eT, mask.unsqueeze(1).to_broadcast([128, 4, 128]), ALU.mult)
        return pT

    def stage2(src, vext_chunk, acc, first, last, extra=None):
        for r in range(4):
            P.mm(acc[:, r * 65:(r + 1) * 65], src[:, r, :], vext_chunk,
                 start=(first and r == 0), stop=(last and r == 3))
        if extra is not None:
            bank, rhs = extra
            for r in range(4):
                P.mm(bank[:, r * 128:(r + 1) * 128], src[:, r, :], rhs,
                     start=(first and r == 0), stop=(last and r == 3))

    selT_cur = [None]
    for qb in range(nqb):
        i = qb % NB
        xt, hT, qT, qTb, gate = xt_[i], hT_[i], qT_[i], qTb_[i], gate_[i]
        imp, sc, tmp, vals, thr, sel, selT = imp_[i], sc_[i], tmp_[i], vals_[i], thr_[i], sel_[i], selT_[i]
        zc, gs, out = zc_[i], gs_[i], out_[i]
        negsel = negsel_[i]
        if "bg" in A:
            A["bg"](qb)
        if qb == 0:
            P.dma("sp", out=xt_[0], in_=A["x_tile"](0))
        emit_hT(P, hT, xt, scT, shT, ident, pb[0], pb[1])
        if qb + 1 < nqb:
            P.dma("sp", out=xt_[(qb + 1) % NB], in_=A["x_tile"](qb + 1))
        for r in range(4):
            for kc in range(8):
                P.mm(pb[0][0:64, r * 128:(r + 1) * 128], wq[:, kc, r * 64:(r + 1) * 64], hT[:, kc, :],
                     start=(kc == 0), stop=(kc == 7))
        P.ts(qT, pb[0][0:64, :], 0.125, ALU.mult)
        P.ts(qTb, pb[0][0:64, :], 0.125, ALU.mult)
        for kc in range(8):
            P.mm(pb[1][:, 0:12], hT[:, kc, :], wgt[:, kc, :], start=(kc == 0), stop=(kc == 7))
        P.act(gate, pb[1][:, 0:12], AF.Sigmoid)
        chunks = []
        for c in range(NCH):
            th = 128 * qb - 31 - 2048 * c
            if th < -127:
                continue
            chunks.append((c, None if th >= 2032 else th))
        items = []
        for k, (c, th) in enumerate(chunks):
            first, last = (k == 0), (k == len(chunks) - 1)

            def s1(c=c, th=th, k=k):
                mask = None
                if th is not None:
                    mask = m_[k % NE]
                    P.ts(mask, Acmp, float(th), ALU.is_le)
                return stage1(qT, kcmpT[:, c * 128:(c + 1) * 128], False, mask=mask)

            def s2(src, c=c, first=first, last=last):
                stage2(src, vcmp[:, c, :], pb[6], first, last, extra=(pb[7], c2s[:, c, :]))
            items.append((s1, s2))
        pipeline(items)
        acc = pb[6][:, 0:260].rearrange("p (r e) -> p r e", e=65)
        P.ts(zc, acc[:, :, 64], 1e-30, ALU.max)
        P.op("dve", "reciprocal", out=zc, in_=zc)
        P.tt(gs, zc, gate.rearrange("p (r b) -> p r b", b=3)[:, :, 0], ALU.mult)
        for r in range(4):
            P.ts(out[:, r, :], acc[:, r, 0:64], gs[:, r:r + 1], ALU.mult, nowaw=True)
        for r in range(4):
            if r == 0:
                P.ts(imp, pb[7][:, 0:128], zc[:, 0:1], ALU.mult)
            else:
                P.op("dve", "scalar_tensor_tensor", out=imp, in0=pb[7][:, r * 128:(r + 1) * 128],
                     scalar=zc[:, r:r + 1], in1=imp, op0=ALU.mult, op1=ALU.add)
        items = []
        wch = list(range(max(0, qb - 4), qb + 1))
        for k, c in enumerate(wch):
            mask = tri if c == qb else (low if c == qb - 4 else None)

            def s1(c=c, mask=mask):
                return stage1(qTb, kwinT[:, c * 128:(c + 1) * 128], True, mask=mask)

            def s2(src, c=c, k=k):
                stage2(src, vwin[:, c, 0:65], pb[7], k == 0, k == len(wch) - 1)
            items.append((s1, s2))
        pipeline(items)
        off = 126 - 2 * qb
        P.tt(sc, imp, availW[:, off:off + 128], ALU.mult)
        P.tt(sc, sc, fbW[:, off:off + 128], ALU.add)
        P.ts(sc[:, 0:1], sc[:, 0:1], 100.0, ALU.add)
        P.op("dve", "max", out=vals[:, 0:8], in_=sc)
        P.op("dve", "match_replace", out=tmp, in_to_replace=vals[:, 0:8], in_values=sc, imm_value=-1e30)
        P.op("dve", "max", out=vals[:, 8:16], in_=tmp)
        P.ts(thr, vals[:, 15:16], 0.0, ALU.max)
        P.ts(sel, sc, thr[:, 0:1], ALU.is_ge)
        P.tr(pb[1][:, 0:128], sel, ident)
        P.cp(selT, pb[1][:, 0:128])
        P.ts(negsel, pb[1][:, 0:128].unsqueeze(1).to_broadcast([128, 4, 128]), 1.0, ALU.subtract, 30000.0, ALU.mult)
        selT_cur[0] = selT
        negsel_cur[0] = negsel
        items = []
        for c in range(qb + 1):
            def s1(c=c):
                return stage1(qTb, kslcT[:, c * 128:(c + 1) * 128], True, maskE=Eall[:, c, :], tri_too=(c == qb))

            def s2(src, c=c):
                stage2(src, vslc[:, c, 0:65], pb[6], c == 0, c == qb)
            items.append((s1, s2))
        pipeline(items)
        for (bank, br) in ((pb[6], 1), (pb[7], 2)):
            acc = bank[:, 0:260].rearrange("p (r e) -> p r e", e=65)
            P.ts(zc, acc[:, :, 64], 1e-30, ALU.max)
            P.op("dve", "reciprocal", out=zc, in_=zc)
            P.tt(gs, zc, gate.rearrange("p (r b) -> p r b", b=3)[:, :, br], ALU.mult)
            for r in range(4):
                P.op("dve", "scalar_tensor_tensor", out=out[:, r, :], in0=acc[:, r, 0:64], scalar=gs[:, r:r + 1],
                     in1=out[:, r, :], op0=ALU.mult, op1=ALU.add, _nowaw=True)
        P.dma("sp", out=A["o_out"](qb), in_=out.rearrange("p r d -> p (r d)"))
        if "after_out" in A:
            A["after_out"](qb)


GROUPS = [[0, 1, 2, 3], [4, 5, 6, 7]]
CC_BYTES = 1 << 20


def build_fused():
    nc = bass.Bass("TRN2", target_bir_lowering=False)
    P = Prog(nc)
    di = lambda n, s, dt=F32: P.dram(n, s, dt, kind="ExternalInput", track=False)
    I = {}
    for n, s in (("x_full", [S, D]), ("xs0", [TOK, D]), ("cT", [128, 8]), ("ada_w", [DEPTH, D, 6 * D]),
                 ("ada_b", [DEPTH, 1, 6 * D]), ("ln_g", [DEPTH, 2, D]), ("ln_b", [DEPTH, 2, D]),
                 ("wqk", [2, D, 512]), ("wv", [2, D, 512]), ("wg", [2, D, 512]), ("ret_w_o", [2, 2048, D]),
                 ("cos", [S, 128]), ("sin", [S, 128]), ("decT", [128, 128]), ("cols", [128, 4]),
                 ("kv_ada_w", [D, 2 * D]), ("kv_ada_b", [1, 2 * D]), ("w_kv", [D, 1536]),
                 ("wq", [2, D, 256]), ("wgate", [2, D, 12]), ("nsa_w_o", [2, D, D]),
                 ("peT", [2, 64, 32]), ("w1", [2, 2048, 256]), ("b1T", [128, 2, 2]), ("w2l", [128, 2, 2, 64]),
                 ("c2s", [128, 4, 128]), ("Eall", [128, 64, 128]), ("tri", [128, 128]), ("low", [128, 128]),
                 ("Acmp", [128, 128]), ("availW", [128, 256]), ("fbW", [128, 256]),
                 ("w_q", [DEPTH, D, 2048]), ("keysT", [DEPTH, 128, 16, 128]), ("uT", [DEPTH, 128, 128, 8, 128]),
                 ("v", [DEPTH, 16384, D]), ("ident", [128, 128]), ("iota", [128, 128])):
        I[n] = di(n, s)
    x_out = P.dram("x_out", [TOK, D], F32, kind="ExternalOutput", track=False)

    pb = [P.ps("pb%d" % i) for i in range(8)]
    ident = P.sb("ident", [128, 128])
    P.dma("sp", out=ident, in_=I["ident"])
    iota = P.sb("iota", [128, 128])
    P.dma("sp", out=iota, in_=I["iota"])
    jr = nc.sync.partition_id() % 4

    x_loc = [P.dram("x_loc%d" % l, [TOK, D]) for l in range(3)]
    x_gat = [P.dram("x_gat%d" % l, [S, D]) for l in range(3)]
    kvT_loc = P.dram("kvT_loc", [16 * 64, TOK])
    kvT_gat = P.dram("kvT_gat", [16 * 4 * 64, TOK])
    vt_loc = P.dram("vt_loc", [8 * TOK, 64])
    vt_gat = P.dram("vt_gat", [8 * 4 * TOK, 64])
    vt_loc4 = vt_loc.rearrange("(w g t) d -> w g t d", w=2, g=4)
    kvT_mine = P.dram("kvT_mine", [4 * 256, TOK])
    vt_mine = P.dram("vt_mine", [2 * S, 64])

    def x_tile_from(l):
        if l == 0:
            return lambda n: I["x_full"][n * 128:(n + 1) * 128, :]

        def f(n):
            rank, r = n // 16, (n % 16) * 128
            ch, off = r // 256, r % 256
            row = (ch * 4 + rank) * 256 + off
            return x_gat[l - 1][row:row + 128, :]
        return f

    for l in range(DEPTH):
        Wd = 512 if l < 2 else 256
        RPC = CC_BYTES // (Wd * 4)
        NCHK = S // RPC
        CPJ = TOK // RPC
        o_loc = P.dram("o_loc%d" % l, [S, Wd])
        o_gat = P.dram("o_gat%d" % l, [NCHK * 4 * RPC, Wd])
        uT_bf = P.dram("uT_bf%d" % l, [64, 128, 2, 1024], BF16)
        v_bf = P.dram("v_bf%d" % l, [64, 128, 2, D], BF16)
        def bg(n, l=l, uT_bf=uT_bf, v_bf=v_bf):
            if P.cnt["pe"] > 0:
                P._wait("pool", ("eng", "pe", P.cnt["pe"]))
            P.dma("pool", out=uT_bf[n], in_=I["uT"][l, 2 * n:2 * n + 2].rearrange("c p k n -> p c (k n)"))
            P.dma("pool", out=v_bf[n], in_=I["v"][l, 2 * n * 128:(2 * n + 2) * 128, :].rearrange("(c p) d -> p c d", p=128))
        def after_out(n, o_loc=o_loc, o_gat=o_gat, RPC=RPC):
            per = RPC // 128
            if (n + 1) % per == 0:
                ch = (n + 1) // per - 1
                P.collective("AllGather", GROUPS, o_loc[ch * RPC:(ch + 1) * RPC, :],
                             o_gat[ch * 4 * RPC:(ch + 1) * 4 * RPC, :])

        base = dict(cT=I["cT"], ada_w=I["ada_w"][l], ada_b=I["ada_b"][l], x_tile=x_tile_from(l), bg=bg,
                    after_out=after_out,
                    o_out=lambda n, o_loc=o_loc: o_loc[n * 128:(n + 1) * 128, :])
        if l < 2:
            A = dict(base, wqk=I["wqk"][l], wv=I["wv"][l], wg=I["wg"][l], cos=I["cos"], sin=I["sin"],
                     decT=I["decT"], cols=I["cols"])
            emit_retmix(P, pb, ident, A)
        else:
            A = dict(base, wq=I["wq"][l - 2], wgate=I["wgate"][l - 2],
                     kvT=lambda w, rk: kvT_mine[w * 256 + rk * 64:w * 256 + (rk + 1) * 64, :],
                     vtok=lambda w, rk: vt_mine[w * S + rk * TOK:w * S + (rk + 1) * TOK, :],
                     **{k: I[k] for k in ("peT", "w1", "b1T", "w2l", "c2s", "Eall", "tri", "low", "Acmp",
                                          "availW", "fbW")})
            emit_nsamix(P, pb, ident, A)

        o_mine = P.dram("o_mine%d" % l, [S, Wd])
        o_gat_v = o_gat.rearrange("(j q) w -> j q w", j=4)
        NSPL = 2
        for sp_ in range(NSPL):
            rs = slice(sp_ * (S // NSPL), (sp_ + 1) * (S // NSPL))
            P.dma("sp", out=o_mine[rs, :].rearrange("(o r) w -> o (r w)", o=1),
                  in_=o_gat_v[bass.ds(jr, 1), rs, :].rearrange("o r w -> o (r w)"))

        def o_pieces(tile, o_mine=o_mine, RPC=RPC, Wd=Wd):
            r = tile * 128
            sc, off = r // RPC, r % RPC
            return [(o_mine[(sc * 4 + h) * RPC + off:(sc * 4 + h) * RPC + off + 128, :], h * Wd, Wd)
                    for h in range(4)]

        rows = lambda t: slice(t * 128, (t + 1) * 128)
        A = dict(cT=I["cT"], ada_w=I["ada_w"][l], ada_b=I["ada_b"][l], ln_g=I["ln_g"][l], ln_b=I["ln_b"][l],
                 w_o=(I["ret_w_o"][l] if l < 2 else I["nsa_w_o"][l - 2]), w_q=I["w_q"][l], keysT=I["keysT"][l],
                 uT_pair=lambda q, uT_bf=uT_bf: uT_bf[q], v_pair=lambda q, v_bf=v_bf: v_bf[q], tab_q="sp",
                 o_pieces=o_pieces,
                 x_tile=(lambda t: I["xs0"][rows(t), :]) if l == 0 else (lambda t, xl=x_loc[l - 1]: xl[rows(t), :]),
                 x_out=(lambda t: [x_out[rows(t), :]]) if l == DEPTH - 1 else (lambda t, xl=x_loc[l]: [xl[rows(t), :]]))
        if l == 1:
            A.update(kv_ada_w=I["kv_ada_w"], kv_ada_b=I["kv_ada_b"], w_kv=I["w_kv"],
                     kvT_out=lambda w, g: kvT_loc[(w * 4 + g) * 64:(w * 4 + g + 1) * 64, :],
                     vtok_out=lambda w: vt_loc4[w])
        if l < DEPTH - 1:
            A["after_group"] = lambda ch, l=l: P.collective("AllGather", GROUPS, x_loc[l][ch * 256:(ch + 1) * 256, :],
                                                            x_gat[l][ch * 1024:(ch + 1) * 1024, :])
        emit_tok(P, pb, ident, iota, A, 4 * Wd, l == 1)
        if l == 1:
            for wg_ in range(16):
                P.collective("AllGather", GROUPS, kvT_loc[wg_ * 64:(wg_ + 1) * 64, :],
                             kvT_gat[wg_ * 256:(wg_ + 1) * 256, :])
            for wg_ in range(8):
                P.collective("AllGather", GROUPS, vt_loc[wg_ * TOK:(wg_ + 1) * TOK, :],
                             vt_gat[wg_ * 4 * TOK:(wg_ + 1) * 4 * TOK, :])
            kg = kvT_gat.rearrange("(w g r) t -> w g r t", w=4, g=4)
            P.dma("sp", out=kvT_mine.rearrange("(w r) t -> w (r t)", w=4),
                  in_=kg[:, bass.ds(jr, 1), :, :].rearrange("w o r t -> w (o r t)"))
            vg = vt_gat.rearrange("(w g r) d -> w g r d", w=2, g=4)
            P.dma("sp", out=vt_mine.rearrange("(w r) d -> w (r d)", w=2),
                  in_=vg[:, bass.ds(jr, 1), :, :].rearrange("w o r d -> w (o r d)"))
    P.finish()
    return nc, P


_CONST = {}


def _consts():
    if _CONST:
        return _CONST
    a = np.arange(128)
    c = _CONST
    c["ident"] = np.eye(128, dtype=np.float32)
    c["iota"] = np.tile(np.arange(128, dtype=np.float32)[None, :], (128, 1))
    import jax
    import jax.numpy as jnp
    with jax.default_device(jax.devices("cpu")[0]):
        pos = jnp.arange(S, dtype=jnp.float32)
        theta = 1.0 / (10000.0 ** jnp.linspace(0.0, 1.0, 128, dtype=jnp.float32))
        ang = pos[:, None] * theta[None, :]
        c["cos"] = np.asarray(jnp.cos(ang))
        c["sin"] = np.asarray(jnp.sin(ang))
    idx = np.arange(128, dtype=np.float64)
    for h in range(4):
        lg = np.log1p(-np.exp2(-5.0 - h))
        qdec = np.exp((idx + 1) * lg)
        kdec = np.exp((127 - idx) * lg) / 16.0
        cdec = np.exp(128 * lg)
        decT = np.where(idx[:, None] <= idx[None, :], np.exp(-(idx[:, None] + 1) * lg), 0.0)
        c["decT%d" % h] = decT.astype(np.float32)
        c["cols%d" % h] = np.stack([qdec, kdec, np.full(128, cdec), np.zeros(128)], axis=1).astype(np.float32)
    nqb = S // 128
    NCP = S // 16
    i = np.arange(NCP)[:, None] * 16
    j = np.arange(128)[None, :] * 64
    ov = np.minimum(i + 32, j + 64) - np.maximum(i, j)
    c2s = (np.clip(ov, 0, None) / 32.0).astype(np.float32)
    c2s[NCP - 1:] = 0.0
    c["c2s"] = np.ascontiguousarray(c2s.reshape(NCP // 128, 128, 128).transpose(1, 0, 2))
    jj = np.arange(128)[:, None, None]
    cc = np.arange(nqb)[None, :, None]
    kk = np.arange(128)[None, None, :]
    c["Eall"] = (jj == 2 * cc + kk // 64).astype(np.float32)
    c["tri"] = (a[:, None] <= a[None, :]).astype(np.float32)
    c["low"] = (a[:, None] > a[None, :]).astype(np.float32)
    c["Acmp"] = (16.0 * a[:, None] - a[None, :]).astype(np.float32)
    tl = a[:, None]
    jrel = np.arange(256)[None, :] - 126
    cur = (tl >= 64).astype(np.int64)
    c["availW"] = (jrel <= cur).astype(np.float32)
    c["fbW"] = np.where((jrel == cur) | (jrel == cur - 1), 100.0, np.where(jrel > cur, -1.0, 0.0)).astype(np.float32)
    return c


_PROGS = {}


def _ca(a):
    return np.ascontiguousarray(a, dtype=np.float32)


def kernel(x, c, ada_w, ada_b, ln_g, ln_b, ret_w_in, ret_w_o, kv_ada_w, kv_ada_b, nsa_w_kv,
           cmp_pe, cmp_w1, cmp_b1, cmp_w2, nsa_w_in, nsa_w_o, peer_w_q, peer_keys, peer_u, peer_v):
    f = lambda a: np.asarray(a, dtype=np.float32)
    x, c, ada_w, ada_b, ln_g, ln_b = f(x), f(c), f(ada_w), f(ada_b), f(ln_g), f(ln_b)
    ret_w_in, ret_w_o, kv_ada_w, kv_ada_b, nsa_w_kv = f(ret_w_in), f(ret_w_o), f(kv_ada_w), f(kv_ada_b), f(nsa_w_kv)
    cmp_pe, cmp_w1, cmp_b1, cmp_w2 = f(cmp_pe), f(cmp_w1), f(cmp_b1), f(cmp_w2)
    nsa_w_in, nsa_w_o, peer_w_q, peer_keys, peer_u, peer_v = (f(nsa_w_in), f(nsa_w_o), f(peer_w_q), f(peer_keys),
                                                              f(peer_u), f(peer_v))
    K = _consts()
    shared = dict(
        ada_w=ada_w, ada_b=_ca(ada_b[:, None, :]), ln_g=ln_g, ln_b=ln_b, ret_w_o=ret_w_o,
        cos=K["cos"], sin=K["sin"], kv_ada_w=kv_ada_w, kv_ada_b=_ca(kv_ada_b[None, :]), w_kv=nsa_w_kv,
        nsa_w_o=nsa_w_o, peT=_ca(cmp_pe.transpose(0, 2, 1)), w1=cmp_w1,
        b1T=_ca(cmp_b1.reshape(2, 2, 128).transpose(2, 0, 1)),
        w2l=_ca(cmp_w2.reshape(2, 2, 128, 64).transpose(2, 0, 1, 3)),
        c2s=K["c2s"], Eall=K["Eall"], tri=K["tri"], low=K["low"], Acmp=K["Acmp"], availW=K["availW"], fbW=K["fbW"],
        w_q=peer_w_q, keysT=_ca(peer_keys.reshape(DEPTH, 16, 128, 128).transpose(0, 3, 1, 2)),
        uT=_ca(peer_u.reshape(DEPTH, 128, 128, 8, 128).transpose(0, 1, 4, 3, 2)), v=peer_v,
        ident=K["ident"], iota=K["iota"])
    maps = []
    for i in range(NCORES):
        b, j = divmod(i, 4)
        d = dict(shared)
        d.update(
            x_full=_ca(x[b]), xs0=_ca(x[b, j * TOK:(j + 1) * TOK]), cT=_ca(c[b].reshape(8, 128).T),
            wqk=_ca(np.concatenate([ret_w_in[:, :, j * 256:(j + 1) * 256],
                                    ret_w_in[:, :, 1024 + j * 256:1024 + (j + 1) * 256]], axis=2)),
            wv=_ca(ret_w_in[:, :, 2048 + j * 512:2048 + (j + 1) * 512]),
            wg=_ca(ret_w_in[:, :, 4096 + j * 512:4096 + (j + 1) * 512]),
            decT=K["decT%d" % j], cols=K["cols%d" % j],
            wq=_ca(nsa_w_in[:, :, j * 256:(j + 1) * 256]),
            wgate=_ca(nsa_w_in[:, :, 1024 + j * 12:1024 + (j + 1) * 12]))
        maps.append(d)
    if "fused" not in _PROGS:
        _PROGS["fused"] = build_fused()[0]
    res = run_bass_kernel_spmd(_PROGS["fused"], maps, core_ids=list(range(NCORES))).results
    out = np.stack([np.concatenate([res[4 * b + j]["x_out"] for j in range(4)], axis=0) for b in range(B)])
    return out.astype(np.float32)
```
